# Optimizing a Trainium2 kernel written in Bass

```python
import math
import jax, jax.numpy as jnp
from jax import lax
import numpy as np

D_MODEL = 1024
BATCH = 16
SEQ = 2048
DEPTH = 2

CHUNK = 64
A_HEADS = 8
A_HEAD_DIM = 64
A_WIDTH = A_HEADS * A_HEAD_DIM
A_LEFT_CHUNKS = 8
A_BAND = (A_LEFT_CHUNKS + 1) * CHUNK
REL_CLIP = 128
B_WIDTH = D_MODEL - A_WIDTH
CONV_WIDTH = 31
C_HEADS = 4
C_KEY_DIM = D_MODEL // 2
C_VAL_DIM = D_MODEL
C_DK = C_KEY_DIM // C_HEADS
C_DV = C_VAL_DIM // C_HEADS
GATE_RANK = 16
GATE_TEMP = 16.0
D_FF = -(-8 * D_MODEL // (3 * 256)) * 256
ALPHA = (2 * DEPTH) ** 0.25
BETA = (8 * DEPTH) ** -0.25
LN_EPS = 1e-5
RMS_EPS = 1e-6
N_EVEN = (DEPTH + 1) // 2
N_ODD = DEPTH // 2
NEG_INF = -1e30

kernel_name = "hybrid_chunk_attn_conv_gla_deepnorm"


def layer_norm(x, g, b):
    xf = x.astype(jnp.float32)
    mu = jnp.mean(xf, axis=-1, keepdims=True)
    var = jnp.mean(jnp.square(xf - mu), axis=-1, keepdims=True)
    y = (xf - mu) * lax.rsqrt(var + LN_EPS)
    return (y * g.astype(jnp.float32) + b.astype(jnp.float32)).astype(x.dtype)


def chunked_relpos_attention(q, k, v, rel_bias):
    bsz, seq, h, dh = q.shape
    nc = seq // CHUNK
    qc = q.reshape(bsz, nc, CHUNK, h, dh)
    pad = ((0, 0), (A_LEFT_CHUNKS, 0), (0, 0), (0, 0), (0, 0))
    kp = jnp.pad(k.reshape(bsz, nc, CHUNK, h, dh), pad)
    vp = jnp.pad(v.reshape(bsz, nc, CHUNK, h, dh), pad)
    kb = jnp.concatenate([kp[:, o:o + nc] for o in range(A_LEFT_CHUNKS + 1)], axis=2)
    vb = jnp.concatenate([vp[:, o:o + nc] for o in range(A_LEFT_CHUNKS + 1)], axis=2)
    scores = jnp.einsum('bnqhd,bnkhd->bnhqk', qc, kb).astype(jnp.float32) * (dh ** -0.5)
    qi = jnp.arange(CHUNK)[:, None] + A_LEFT_CHUNKS * CHUNK
    kj = jnp.arange(A_BAND)[None, :]
    rel = jnp.clip(qi - kj, -REL_CLIP, REL_CLIP) + REL_CLIP
    bias = rel_bias.astype(jnp.float32)[:, rel]
    key_chunk = jnp.arange(nc)[:, None] - A_LEFT_CHUNKS + (jnp.arange(A_BAND) // CHUNK)[None, :]
    valid = (key_chunk >= 0)[None, :, None, None, :]
    scores = jnp.where(valid, scores + bias[None, None], NEG_INF)
    probs = jax.nn.softmax(scores, axis=-1).astype(v.dtype)
    out = jnp.einsum('bnhqk,bnkhd->bnqhd', probs, vb)
    return out.reshape(bsz, seq, h * dh)


def conformer_conv(u, w_dw, b_dw, g_n, b_n):
    a, gate = jnp.split(u, 2, axis=-1)
    h = a * jax.nn.sigmoid(gate)
    hp = jnp.pad(h, ((0, 0), (CONV_WIDTH - 1, 0), (0, 0)))
    y = lax.conv_general_dilated(
        hp, w_dw[:, None, :].astype(hp.dtype), window_strides=(1,), padding='VALID',
        dimension_numbers=('NWC', 'WIO', 'NWC'), feature_group_count=B_WIDTH)
    y = y + b_dw
    y = layer_norm(y, g_n, b_n)
    return jax.nn.silu(y)


def gla_chunked(q, k, v, log_a):
    bsz, seq, h, dk = q.shape
    dv = v.shape[-1]
    nc = seq // CHUNK

    def to_chunks(t):
        return t.reshape(bsz, nc, CHUNK, h, t.shape[-1]).transpose(1, 0, 3, 2, 4).astype(jnp.float32)

    qc = to_chunks(q) * (dk ** -0.5)
    kc, vc, ac = to_chunks(k), to_chunks(v), to_chunks(log_a)
    causal = jnp.tril(jnp.ones((CHUNK, CHUNK), dtype=bool))[:, :, None]

    def step(state, inp):
        qi, ki, vi, ai = inp
        b = jnp.cumsum(ai, axis=-2)
        diff = b[:, :, :, None, :] - b[:, :, None, :, :]
        decay = jnp.where(causal, jnp.exp(jnp.minimum(diff, 0.0)), 0.0)
        attn = jnp.einsum('bhid,bhjd,bhijd->bhij', qi, ki, decay)
        intra = jnp.einsum('bhij,bhje->bhie', attn, vi)
        inter = jnp.einsum('bhid,bhde->bhie', qi * jnp.exp(b), state)
        b_last = b[:, :, -1:, :]
        k_dec = ki * jnp.exp(b_last - b)
        new_state = state * jnp.exp(b_last[:, :, 0, :])[..., None] + jnp.einsum('bhjd,bhje->bhde', k_dec, vi)
        return new_state, intra + inter

    state0 = jnp.zeros((bsz, h, dk, dv), jnp.float32)
    _, out = lax.scan(step, state0, (qc, kc, vc, ac))
    return out.transpose(1, 0, 3, 2, 4).reshape(bsz, seq, h, dv)


def even_mixer(x, w_in, rel_bias, conv_w, conv_b, conv_g, conv_nb, w_out):
    bsz, seq, _ = x.shape
    u = x @ w_in
    q, k, v, cv = jnp.split(u, [A_WIDTH, 2 * A_WIDTH, 3 * A_WIDTH], axis=-1)
    shp = (bsz, seq, A_HEADS, A_HEAD_DIM)
    a_out = chunked_relpos_attention(q.reshape(shp), k.reshape(shp), v.reshape(shp), rel_bias)
    b_out = conformer_conv(cv, conv_w, conv_b, conv_g, conv_nb)
    return jnp.concatenate([a_out, b_out], axis=-1) @ w_out


def odd_mixer(x, w_in, gate_w, gate_b, head_g, w_out):
    bsz, seq, _ = x.shape
    u = x @ w_in
    q, k, v, g, z = jnp.split(
        u, [C_KEY_DIM, 2 * C_KEY_DIM, 2 * C_KEY_DIM + C_VAL_DIM, 2 * C_KEY_DIM + 2 * C_VAL_DIM], axis=-1)
    log_a = jax.nn.log_sigmoid((z @ gate_w + gate_b).astype(jnp.float32)) / GATE_TEMP
    kshp = (bsz, seq, C_HEADS, C_DK)
    o = gla_chunked(q.reshape(kshp), k.reshape(kshp), v.reshape(bsz, seq, C_HEADS, C_DV), log_a.reshape(kshp))
    o = o * lax.rsqrt(jnp.mean(jnp.square(o), axis=-1, keepdims=True) + RMS_EPS)
    o = o.reshape(bsz, seq, C_VAL_DIM) * head_g.astype(jnp.float32)
    o = (o * jax.nn.silu(g.astype(jnp.float32))).astype(x.dtype)
    return o @ w_out


def swiglu(x, w_gate, w_up, w_down):
    return (jax.nn.silu(x @ w_gate) * (x @ w_up)) @ w_down


def setup_inputs(seed: int = 0) -> dict:
    key = jax.random.key(seed)
    ks = jax.random.split(key, 24)
    f32 = jnp.float32
    nrm = lambda k, shp, s: jax.random.normal(k, shp, f32) * s
    in_even = 3 * A_WIDTH + 2 * B_WIDTH
    in_odd = 2 * C_KEY_DIM + 2 * C_VAL_DIM + GATE_RANK
    return {
        "x": nrm(ks[0], (BATCH, SEQ, D_MODEL), 1.0),
        "even_w_in": nrm(ks[1], (N_EVEN, D_MODEL, in_even), D_MODEL ** -0.5),
        "even_rel_bias": nrm(ks[2], (N_EVEN, A_HEADS, 2 * REL_CLIP + 1), 0.3),
        "even_conv_w": nrm(ks[3], (N_EVEN, CONV_WIDTH, B_WIDTH), CONV_WIDTH ** -0.5),
        "even_conv_b": nrm(ks[4], (N_EVEN, B_WIDTH), 0.01),
        "even_conv_norm_g": 1.0 + nrm(ks[5], (N_EVEN, B_WIDTH), 0.01),
        "even_conv_norm_b": nrm(ks[6], (N_EVEN, B_WIDTH), 0.01),
        "even_w_out": nrm(ks[7], (N_EVEN, A_WIDTH + B_WIDTH, D_MODEL), BETA * (A_WIDTH + B_WIDTH) ** -0.5),
        "odd_w_in": nrm(ks[8], (N_ODD, D_MODEL, in_odd), D_MODEL ** -0.5),
        "odd_gate_w": nrm(ks[9], (N_ODD, GATE_RANK, C_KEY_DIM), GATE_RANK ** -0.5),
        "odd_gate_b": nrm(ks[10], (N_ODD, C_KEY_DIM), 0.01),
        "odd_head_norm_g": 1.0 + nrm(ks[11], (N_ODD, C_VAL_DIM), 0.01),
        "odd_w_out": nrm(ks[12], (N_ODD, C_VAL_DIM, D_MODEL), BETA * C_VAL_DIM ** -0.5),
        "mix_norm_g": 1.0 + nrm(ks[13], (DEPTH, D_MODEL), 0.01),
        "mix_norm_b": nrm(ks[14], (DEPTH, D_MODEL), 0.01),
        "ffn_w_gate": nrm(ks[15], (DEPTH, D_MODEL, D_FF), D_MODEL ** -0.5),
        "ffn_w_up": nrm(ks[16], (DEPTH, D_MODEL, D_FF), D_MODEL ** -0.5),
        "ffn_w_down": nrm(ks[17], (DEPTH, D_FF, D_MODEL), BETA * D_FF ** -0.5),
        "ffn_norm_g": 1.0 + nrm(ks[18], (DEPTH, D_MODEL), 0.01),
        "ffn_norm_b": nrm(ks[19], (DEPTH, D_MODEL), 0.01),
    }


def reference(x, even_w_in, even_rel_bias, even_conv_w, even_conv_b, even_conv_norm_g,
              even_conv_norm_b, even_w_out, odd_w_in, odd_gate_w, odd_gate_b,
              odd_head_norm_g, odd_w_out, mix_norm_g, mix_norm_b, ffn_w_gate, ffn_w_up,
              ffn_w_down, ffn_norm_g, ffn_norm_b):
    for l in range(DEPTH):
        i = l // 2
        if l % 2 == 0:
            m = even_mixer(x, even_w_in[i], even_rel_bias[i], even_conv_w[i], even_conv_b[i],
                           even_conv_norm_g[i], even_conv_norm_b[i], even_w_out[i])
        else:
            m = odd_mixer(x, odd_w_in[i], odd_gate_w[i], odd_gate_b[i], odd_head_norm_g[i], odd_w_out[i])
        x = layer_norm(ALPHA * x + m, mix_norm_g[l], mix_norm_b[l])
        f = swiglu(x, ffn_w_gate[l], ffn_w_up[l], ffn_w_down[l])
        x = layer_norm(ALPHA * x + f, ffn_norm_g[l], ffn_norm_b[l])
    return x
```

```python
import contextlib
import numpy as np
import concourse.bass as bass
import concourse.mybir as mybir
from concourse.bass_utils import run_bass_kernel_spmd

F32 = mybir.dt.float32
BF16 = mybir.dt.bfloat16
AF = mybir.ActivationFunctionType
ALU = mybir.AluOpType

D = 1024
S = 2048
NSEQ = 2
NCORES = 8
DFF = 2816
HF = DFF // 2
NJ = HF // 128
ALPHA = 4 ** 0.25
LN_EPS = 1e-5
RMS_EPS = 1e-6
TB = 512
TBM = 256
WCOLS = 3 * 8 * HF
ACOLS = 38000
A_HEADS = 8
CONV_W = 31
NEGM = -30000.0
IN_EVEN = 2560
IN_ODD = 3088


class Op:
    __slots__ = ("eng", "fn", "deps", "marked", "sem", "val", "dma", "key")


class Buf:
    __slots__ = ("name", "writers", "readers")

    def __init__(self, name=""):
        self.name = name
        self.writers = {}
        self.readers = {}


class Prog:
    ENG = ("pe", "act", "dve", "pool", "sp")
    NDQ = 8

    def __init__(self, nc, es):
        self.nc = nc
        self.streams = {e: [] for e in self.ENG}
        self.sems = {e: es.enter_context(nc.semaphore("s_" + e)) for e in ("pe", "act", "dve", "pool")}
        self.dq = {q: [es.enter_context(nc.semaphore("d_%s%d" % (q, i))) for i in range(self.NDQ)]
                   for q in ("sp", "pool", "act")}
        self.dq_n = {q: 0 for q in self.dq}
        self.dq_cnt = {q: [0] * self.NDQ for q in self.dq}
        self.dq_last = {q: [None] * self.NDQ for q in self.dq}
        self.pending = {}

    def barrier(self):
        lasts = []
        for e in ("pe", "act", "dve", "pool"):
            for o in reversed(self.streams[e]):
                if not o.dma:
                    lasts.append(o)
                    break
        for q in self.dq:
            for last in self.dq_last[q]:
                if last is not None:
                    lasts.append(last)
        self.pending = {e: list(lasts) for e in self.ENG}

    def op(self, eng, fn, reads=(), writes=(), dma=False):
        o = Op()
        o.eng, o.fn, o.deps, o.marked, o.dma = eng, fn, [], False, dma
        o.sem = None
        o.val = 0
        if dma:
            i = self.dq_n[eng]
            self.dq_n[eng] += 1
            slot = i % self.NDQ
            o.key = (eng, slot)
            prev = self.dq_last[eng][slot]
            if prev is not None:
                o.deps.append(prev)
            o.sem = self.dq[eng][slot]
            self.dq_cnt[eng][slot] += 16
            o.val = self.dq_cnt[eng][slot]
            self.dq_last[eng][slot] = o
        else:
            o.key = eng
        seen = set()

        def add(d, raw):
            if d is o or id(d) in seen:
                return
            if not dma and not d.dma and d.eng == eng:
                if eng == "pe" or not raw:
                    return
            seen.add(id(d))
            o.deps.append(d)

        pend = self.pending.pop(eng, None)
        if pend:
            for d in pend:
                add(d, False)
        for b in reads:
            for d in b.writers.values():
                add(d, True)
        for b in writes:
            for d in b.writers.values():
                add(d, False)
            for d in b.readers.values():
                add(d, False)
        for b in reads:
            b.readers[o.key] = o
        for b in writes:
            if b.readers:
                b.readers = {}
                b.writers = {}
            b.writers[o.key] = o
        self.streams[eng].append(o)
        return o

    def finalize_and_emit(self):
        nc = self.nc
        for e in self.ENG:
            for o in self.streams[e]:
                for d in o.deps:
                    if not d.dma:
                        d.marked = True
        for e in ("pe", "act", "dve", "pool"):
            c = 0
            for o in self.streams[e]:
                if not o.dma and o.marked:
                    c += 1
                    o.sem = self.sems[e]
                    o.val = c
        streams = self.streams

        def emit(ename, eng):
            waited = {}
            for o in streams[ename]:
                need = {}
                for d in o.deps:
                    k = id(d.sem)
                    if waited.get(k, 0) >= d.val:
                        continue
                    if k not in need or need[k][1] < d.val:
                        need[k] = (d.sem, d.val)
                for k, (sem, val) in need.items():
                    eng.wait_ge(sem, val)
                    waited[k] = val
                ins = o.fn(eng)
                if o.dma:
                    ins.then_inc(o.sem, 16)
                elif o.marked:
                    ins.then_inc(o.sem, 1)
            if ename in self.dq:
                for slot in range(self.NDQ):
                    last = self.dq_last[ename][slot]
                    if last is not None and waited.get(id(last.sem), 0) < last.val:
                        eng.wait_ge(last.sem, last.val)

        with nc.Block() as block:
            @block.tensor
            def _(e):
                emit("pe", e)

            @block.scalar
            def _(e):
                emit("act", e)

            @block.vector
            def _(e):
                emit("dve", e)

            @block.gpsimd
            def _(e):
                emit("pool", e)

            @block.sync
            def _(e):
                emit("sp", e)


class K:
    def __init__(self, nc, es):
        self.nc = nc
        self.p = Prog(nc, es)
        self.nbuf = 0

    def sb(self, name, shape, dt):
        return self.nc.alloc_sbuf_tensor(name, list(shape), dt)

    def mm(self, out, lhsT, rhs, start, stop, reads, writes):
        return self.p.op("pe", lambda e: e.matmul(out, lhsT, rhs, start=start, stop=stop), reads, writes)

    def act(self, out, in_, func, reads, writes, bias=None, scale=None, accum_out=None):
        kw = {}
        if bias is not None:
            kw["bias"] = bias
        if scale is not None:
            kw["scale"] = scale
        if accum_out is not None:
            kw["accum_out"] = accum_out
        return self.p.op("act", lambda e: e.activation(out, in_, func, **kw), reads, writes)

    def ve(self, eng, name, args, reads, writes, **kw):
        return self.p.op(eng, lambda e: getattr(e, name)(*args, **kw), reads, writes)

    def dma(self, q, out, in_, reads, writes):
        return self.p.op(q, lambda e: e.dma_start(out=out, in_=in_), reads, writes, dma=True)


class Arena:
    def __init__(self, t, ncols):
        self.t = t
        self.n = ncols
        self.off = 0

    def reset(self):
        self.off = 0

    def alloc(self, shape, dt, parts=128):
        n = 1
        for s_ in shape:
            n *= s_
        cols = n * (2 if dt == F32 else 1)
        cols = (cols + 15) // 16 * 16
        assert self.off + cols <= self.n, ("arena overflow", self.off, cols, self.n)
        ap = self.t[0:parts, self.off:self.off + cols]
        self.off += cols
        if dt == F32:
            ap = ap.bitcast(F32)
        ap = ap[:, 0:n]
        if len(shape) == 2:
            ap = ap.rearrange("p (a b) -> p a b", a=shape[0])
        elif len(shape) == 3:
            ap = ap.rearrange("p (a b c) -> p a b c", a=shape[0], b=shape[1])
        return ap


def build(cfg=None):
    cfg = cfg or {}
    nseq = cfg.get("nseq", NSEQ)
    T = nseq * S
    NT = T // 128
    phases = cfg.get("phases", ("m0", "f0", "m1", "f1"))
    nc = bass.Bass("TRN2", target_bir_lowering=False)

    def din(name, shape):
        return nc.dram_tensor(name, list(shape), F32, kind="ExternalInput").ap()

    x_in = din("x", (T, D))
    w = {}
    w["even_w_in"] = din("even_w_in", (D, IN_EVEN))
    w["even_w_out"] = din("even_w_out", (D, D))
    w["odd_w_in"] = din("odd_w_in", (D, IN_ODD))
    w["odd_w_out"] = din("odd_w_out", (D, D))
    for l in range(2):
        w["wg%d" % l] = din("wg%d" % l, (D, DFF))
        w["wu%d" % l] = din("wu%d" % l, (D, DFF))
        w["wd%d" % l] = din("wd%d" % l, (DFF, D))
    lnv = din("lnv", (8, D))
    ident_in = din("ident", (128, 128))
    biasT_in = din("biasT", (128, A_HEADS * 5 * 128))
    convw_in = din("convw", (128, 4 * CONV_W))
    convb_in = din("convb", (128, 4))
    convn_in = din("convn", (2, 512))
    cbias_in = din("cbias", (1, A_HEADS))
    gatew_in = din("gatew", (16, 512))
    gateb_in = din("gateb", (128, 4))
    headg_in = din("headg", (1, D))
    headgT_in = din("headgT", (128, 8))
    tri_in = din("tri", (128, 64))
    out_d = nc.dram_tensor("out", [T, D], F32, kind="ExternalOutput").ap()
    xs = [nc.dram_tensor("xs%d" % i, [T, D], F32, kind="Internal").ap() for i in range(2)]
    ya = nc.dram_tensor("ya", [T, D], F32, kind="Internal").ap()
    xs_tb = [[Buf() for _ in range(NT)] for _ in range(2)]
    ya_tb = [Buf() for _ in range(NT)]
    xin_tb = [Buf() for _ in range(NT)]
    out_tb = [Buf() for _ in range(NT)]

    es = contextlib.ExitStack()
    with es:
        k = K(nc, es)
        p = k.p
        wbuf = [k.sb("wbuf%d" % i, (128, WCOLS), BF16) for i in range(2)]
        wbuf_b = [[Buf("wb%d_%d" % (i, j)) for j in range(4)] for i in range(2)]
        ident_bf = k.sb("ident_bf", (128, 128), BF16)
        ident_f = k.sb("ident_f", (128, 128), F32)
        b_ident = Buf("ident")
        NEP = 2
        stt = [k.sb("stt%d" % i, (128, 16), F32) for i in range(4)]
        b_stt = [Buf() for _ in range(4)]
        arena_t = k.sb("arena", (128, ACOLS), BF16)
        ar = Arena(arena_t, ACOLS)
        ps_all = nc.alloc_psum_tensor("ps_all", [128, 4096], F32)
        b_ps = [Buf("ps%d" % i) for i in range(8)]

        def bank(i, n=1):
            return ps_all[:, i * 512:(i + n) * 512]

        ps_rr = [0]

        def psum1():
            i = ps_rr[0] % 4
            ps_rr[0] += 1
            return bank(i), b_ps[i]

        k.dma("pool", ident_bf[:], ident_in, [], [b_ident])
        k.dma("sp", ident_f[:], ident_in, [], [b_ident])

        def wview(bi, off, kk, n):
            return wbuf[bi][:, off:off + kk * n].rearrange("p (k n) -> p k n", k=kk)

        def ffn_weight_loads(bi, l, hf):
            wg = wview(bi, 0, 8, HF)
            wu = wview(bi, 8 * HF, 8, HF)
            wd = wview(bi, 16 * HF, NJ, D)
            loads = []
            for kk in range(8):
                loads.append(lambda kk=kk: k.dma(
                    "pool", wg[:, kk, :], w["wg%d" % l][kk * 128:(kk + 1) * 128, hf * HF:(hf + 1) * HF],
                    [], [wbuf_b[bi][0]]))
                loads.append(lambda kk=kk: k.dma(
                    "pool", wu[:, kk, :], w["wu%d" % l][kk * 128:(kk + 1) * 128, hf * HF:(hf + 1) * HF],
                    [], [wbuf_b[bi][1]]))
            for j in range(NJ):
                r0 = hf * HF + j * 128
                loads.append(lambda j=j, r0=r0: k.dma("pool", wd[:, j, :], w["wd%d" % l][r0:r0 + 128, :],
                                                     [], [wbuf_b[bi][2]]))
            return (wg, wu, wd), loads

        def mixer_weight_loads(bi, name_in, nin, name_out):
            w_in = wview(bi, 0, 8, nin)
            w_out = wview(bi, 8 * nin, 8, D)
            loads = []
            for kk in range(8):
                loads.append(lambda kk=kk: k.dma("pool", w_in[:, kk, :], w[name_in][kk * 128:(kk + 1) * 128, :],
                                                 [], [wbuf_b[bi][0]]))
            for kk in range(8):
                loads.append(lambda kk=kk: k.dma("pool", w_out[:, kk, :], w[name_out][kk * 128:(kk + 1) * 128, :],
                                                 [], [wbuf_b[bi][1]]))
            return (w_in, w_out), loads

        pending_loads = []

        def drip(n):
            for _ in range(n):
                if pending_loads:
                    pending_loads.pop(0)()

        class Ctx:
            pass

        def alloc_common(tb, nep):
            c = Ctx()
            c.tb = tb
            c.ntl = tb // 128
            c.lng = ar.alloc((D,), F32)
            c.lnb = ar.alloc((D,), F32)
            c.b_ln = Buf("ln")
            c.xbf = ar.alloc((c.ntl, D), BF16)
            c.b_xbf = Buf("xbf")
            c.xT = ar.alloc((8, tb), BF16)
            c.b_xT = [Buf() for _ in range(8)]
            c.nep = nep
            c.ep = [ar.alloc((D,), F32) for _ in range(nep)]
            c.b_ep = [Buf() for _ in range(nep)]
            c.ep_n = 0
            c.ep_slot = {}
            return c

        def load_ln(c, row):
            k.dma("sp", c.lng, lnv[row:row + 1, :].partition_broadcast(128), [], [c.b_ln])
            k.dma("sp", c.lnb, lnv[row + 1:row + 2, :].partition_broadcast(128), [], [c.b_ln])

        def load_xbf(c, src, src_tb, b):
            v = src[b * c.tb:(b + 1) * c.tb, :].rearrange("(t p) d -> p t d", p=128)
            k.dma("pool", c.xbf, v, [src_tb[b * c.ntl + t] for t in range(c.ntl)], [c.b_xbf])

        def transpose_block(c, all_act=False):
            ntl = c.ntl
            for kk in range(8):
                pt, bpt = psum1()
                for t in range(ntl):
                    k.mm(pt[:, t * 128:(t + 1) * 128], c.xbf[:, t, kk * 128:(kk + 1) * 128], ident_bf[:],
                         True, True, [c.b_xbf, b_ident], [bpt])
                if all_act or kk % 2 == 0:
                    k.act(c.xT[:, kk, :], pt[:, 0:c.tb], AF.Copy, [bpt], [c.b_xT[kk]])
                else:
                    k.ve("dve", "tensor_copy", (c.xT[:, kk, :], pt[:, 0:c.tb]), [bpt], [c.b_xT[kk]])

        def ep_prefetch(c, tok, src, src_tb, ex):
            if tok >= NT or tok in c.ep_slot:
                return
            i = c.ep_n % c.nep
            c.ep_n += 1
            c.ep_slot[tok] = i
            rows = slice(tok * 128, (tok + 1) * 128)
            k.dma("sp", c.ep[i], src[rows, :], [src_tb[tok]], [c.b_ep[i]])
            if ex is not None:
                yat, b_yat = ex
                k.dma("sp", yat[i], ya[rows, :], [ya_tb[tok]], [b_yat[i]])

        def ln_epilogue(c, tok, Y, bY, src, src_tb, ex, dst, dst_tb):
            ep_prefetch(c, tok, src, src_tb, ex)
            i = c.ep_slot.pop(tok)
            ep_prefetch(c, tok + 1, src, src_tb, ex)
            e, be = c.ep[i], c.b_ep[i]
            st, bs = stt[i], b_stt[i]
            rows = slice(tok * 128, (tok + 1) * 128)
            for n in range(2):
                sl = slice(n * 512, (n + 1) * 512)
                k.ve("dve", "scalar_tensor_tensor", (e[:, sl], e[:, sl], float(ALPHA), Y[:, sl]),
                     [be, bY[n]], [be], op0=ALU.mult, op1=ALU.add)
            if ex is not None:
                yat, b_yat = ex
                k.ve("dve", "tensor_tensor", (e, e, yat[i], ALU.add), [be, b_yat[i]], [be])
            for n in range(2):
                sl = slice(n * 512, (n + 1) * 512)
                k.ve("dve", "bn_stats", (st[:, n * 6:(n + 1) * 6], e[:, sl]), [be], [bs])
            k.ve("dve", "bn_aggr", (st[:, 12:14], st[:, 0:12]), [bs], [bs])
            k.act(st[:, 11:12], st[:, 13:14], AF.Sqrt, [bs], [bs], bias=float(LN_EPS))
            k.ve("dve", "reciprocal", (st[:, 14:15], st[:, 11:12]), [bs], [bs])
            k.ve("dve", "scalar_tensor_tensor", (st[:, 15:16], st[:, 12:13], -1.0, st[:, 14:15]),
                 [bs], [bs], op0=ALU.mult, op1=ALU.mult)
            k.act(e, e, AF.Identity, [be, bs], [be], bias=st[:, 15:16], scale=st[:, 14:15])
            k.ve("dve", "tensor_tensor", (e, e, c.lng, ALU.mult), [be, c.b_ln], [be])
            k.ve("dve", "tensor_tensor", (e, e, c.lnb, ALU.add), [be, c.b_ln], [be])
            k.dma("sp", dst[rows, :], e, [be], [dst_tb[tok]])

        def ffn_phase(l, hf, bi, wts, src, src_tb, dst, dst_tb, lnrow):
            wg, wu, wd = wts
            bw = wbuf_b[bi]
            ar.reset()
            nep = 2 if hf == 0 else 4
            c = alloc_common(TB, nep)
            NB = T // TB
            hT = ar.alloc((NJ, TB), BF16)
            b_hT = [Buf() for _ in range(NJ)]
            sg = [ar.alloc((TB,), F32) for _ in range(2)]
            b_sg = [Buf(), Buf()]
            ste = ar.alloc((4, 12), F32)
            mve = ar.alloc((4, 2), F32)
            sde = ar.alloc((4, 1), F32)
            rse = ar.alloc((4, 1), F32)
            nme = ar.alloc((4, 1), F32)
            b_ste = Buf()
            if hf == 1:
                load_ln(c, lnrow)
            esrc, esrc_tb = (src, src_tb) if hf == 0 else (ya, ya_tb)
            load_xbf(c, src, src_tb, 0)
            if hf == 0:
                ep_prefetch(c, 0, esrc, esrc_tb, None)
            else:
                for t in range(4):
                    ep_prefetch(c, t, esrc, esrc_tb, None)
            yn = 0
            deferred = []
            transpose_block(c, all_act=True)
            if NB > 1:
                load_xbf(c, src, src_tb, 1)
            for b in range(NB):
                drip(4)
                for j in range(NJ):
                    pg, bpg = psum1()
                    pu, bpu = psum1()
                    for kk in range(8):
                        k.mm(pg, wg[:, kk, j * 128:(j + 1) * 128], c.xT[:, kk, :], kk == 0, kk == 7,
                             [bw[0], c.b_xT[kk]], [bpg])
                    for kk in range(8):
                        k.mm(pu, wu[:, kk, j * 128:(j + 1) * 128], c.xT[:, kk, :], kk == 0, kk == 7,
                             [bw[1], c.b_xT[kk]], [bpu])
                    si = j % 2
                    k.act(sg[si], pg, AF.Silu, [bpg], [b_sg[si]])
                    k.ve("dve", "tensor_tensor", (hT[:, j, :], sg[si], pu, ALU.mult),
                         [b_sg[si], bpu], [b_hT[j]])
                    if j == 3 and deferred:
                        deferred.pop(0)()
                es_ = []
                for t in range(4):
                    tok = b * 4 + t
                    yb = 4 + 2 * (yn % 2)
                    yn += 1
                    Y = bank(yb, 2)
                    bY = (b_ps[yb], b_ps[yb + 1])
                    for n in range(2):
                        for j in range(NJ):
                            k.mm(Y[:, n * 512:(n + 1) * 512], hT[:, j, t * 128:(t + 1) * 128],
                                 wd[:, j, n * 512:(n + 1) * 512], j == 0, j == NJ - 1, [b_hT[j], bw[2]], [bY[n]])
                    rows = slice(tok * 128, (tok + 1) * 128)
                    i = c.ep_slot.pop(tok)
                    e, be = c.ep[i], c.b_ep[i]
                    if hf == 0:
                        ep_prefetch(c, tok + 1, esrc, esrc_tb, None)
                        for n in range(2):
                            sl = slice(n * 512, (n + 1) * 512)
                            k.ve("dve", "scalar_tensor_tensor", (e[:, sl], e[:, sl], float(ALPHA), Y[:, sl]),
                                 [be, bY[n]], [be], op0=ALU.mult, op1=ALU.add)
                        k.dma("sp", ya[rows, :], e, [be], [ya_tb[tok]])
                    else:
                        for n in range(2):
                            sl = slice(n * 512, (n + 1) * 512)
                            k.ve("dve", "tensor_tensor", (e[:, sl], e[:, sl], Y[:, sl], ALU.add),
                                 [be, bY[n]], [be])
                        for n in range(2):
                            sl = slice(n * 512, (n + 1) * 512)
                            k.ve("dve", "bn_stats", (ste[:, t, n * 6:(n + 1) * 6], e[:, sl]), [be], [b_ste])
                        k.ve("dve", "bn_aggr", (mve[:, t, :], ste[:, t, :]), [b_ste], [b_ste])
                        es_.append((e, be, tok))
                if b + 1 < NB:
                    transpose_block(c, all_act=True)
                    if b + 2 < NB:
                        load_xbf(c, src, src_tb, b + 2)
                if hf == 1:
                    k.act(sde, mve[:, :, 1:2], AF.Sqrt, [b_ste], [b_ste], bias=float(LN_EPS))
                    k.ve("dve", "reciprocal", (rse, sde), [b_ste], [b_ste])
                    k.ve("dve", "scalar_tensor_tensor", (nme, mve[:, :, 0:1], -1.0, rse), [b_ste], [b_ste],
                         op0=ALU.mult, op1=ALU.mult)
                    for t in range(4):
                        e, be, tok = es_[t]
                        k.act(e, e, AF.Identity, [be, b_ste], [be], bias=nme[:, t, :], scale=rse[:, t, :])

                    def fin(es_=es_, b=b):
                        for (e, be, tok) in es_:
                            k.ve("dve", "tensor_tensor", (e, e, c.lng, ALU.mult), [be, c.b_ln], [be])
                            k.ve("dve", "tensor_tensor", (e, e, c.lnb, ALU.add), [be, c.b_ln], [be])
                            rows = slice(tok * 128, (tok + 1) * 128)
                            k.dma("sp", dst[rows, :], e, [be], [dst_tb[tok]])
                        for (e, be, tok) in es_:
                            ep_prefetch(c, tok + 4, esrc, esrc_tb, None)
                    deferred.append(fin)
            while deferred:
                deferred.pop(0)()
            drip(1000)

        def mixer0_phase(bi, wts, src, src_tb, dst, dst_tb):
            w_in, w_out = wts
            bw = wbuf_b[bi]
            ar.reset()
            c = alloc_common(TBM, NEP)
            NB = T // TBM
            BPS = S // TBM
            RING = 6
            qT = ar.alloc((4, TBM), BF16)
            b_qT = [Buf() for _ in range(4)]
            kT = ar.alloc((4, RING * 128), BF16)
            b_kT = [Buf() for _ in range(4)]
            Vaug = ar.alloc((RING, 8, 65), BF16)
            b_V = [Buf() for _ in range(RING)]
            biasT = wbuf[bi][:, 8 * IN_EVEN + 8 * D:8 * IN_EVEN + 8 * D + A_HEADS * 5 * 128].rearrange(
                "p (h j q) -> p h j q", h=A_HEADS, j=5)
            b_bias = Buf()
            probsT = [ar.alloc((5, 128), BF16) for _ in range(2)]
            b_pr = [Buf(), Buf()]
            cbias = ar.alloc((A_HEADS,), F32)
            hc = ar.alloc((4, 30 + TBM), BF16)
            b_hc = [Buf() for _ in range(4)]
            dg = ar.alloc((CONV_W, 128), BF16)
            b_dg = [Buf() for _ in range(CONV_W)]
            cacc = ar.alloc((4, TBM), F32)
            b_cacc = [Buf() for _ in range(4)]
            th = ar.alloc((512,), F32)
            b_th = Buf()
            sig = th[:, 0:TBM]
            b_sig = b_th
            cw = ar.alloc((4, CONV_W), F32)
            cb = ar.alloc((4,), F32)
            cng = ar.alloc((512,), F32)
            cnb = ar.alloc((512,), F32)
            b_cc = Buf()
            rs = [ar.alloc((8,), F32) for _ in range(2)]
            b_rs = [Buf(), Buf()]
            ao = [ar.alloc((512,), BF16) for _ in range(2)]
            b_ao = [Buf(), Buf()]
            cn = [ar.alloc((512,), F32) for _ in range(2)]
            b_cn = [Buf(), Buf()]
            cbf = [ar.alloc((512,), BF16) for _ in range(2)]
            b_cbf = [Buf(), Buf()]
            catT = [ar.alloc((8, 128), BF16) for _ in range(2)]
            b_cat = [[Buf(), Buf()], [Buf(), Buf()]]
            cb2 = ar.alloc((4,), F32)
            stc = ar.alloc((2, 6), F32)
            mvc = ar.alloc((2, 2), F32)
            sdc = ar.alloc((2, 1), F32)
            rsc = ar.alloc((2, 1), F32)
            nmc = ar.alloc((2, 1), F32)
            b_stc = Buf()
            ste = ar.alloc((2, 12), F32)
            mve = ar.alloc((2, 2), F32)
            sde = ar.alloc((2, 1), F32)
            rse = ar.alloc((2, 1), F32)
            nme = ar.alloc((2, 1), F32)
            b_ste = Buf()
            load_ln(c, 0)
            k.dma("pool", biasT.rearrange("p h j q -> p (h j q)"), biasT_in, [], [b_bias])
            k.dma("sp", cw.rearrange("p a b -> p (a b)"), convw_in, [], [b_cc])
            k.dma("sp", cb, convb_in, [], [b_cc])
            k.ve("dve", "tensor_scalar", (cb2, cb, 2.0, None), [b_cc], [b_cc], op0=ALU.mult)
            k.dma("sp", cng, convn_in[0:1, :].partition_broadcast(128), [], [b_cc])
            k.dma("sp", cnb, convn_in[1:2, :].partition_broadcast(128), [], [b_cc])
            k.dma("sp", cbias, cbias_in.partition_broadcast(128), [], [b_bias])
            for pr_, bpr_ in zip(probsT, b_pr):
                k.ve("dve", "memset", (pr_[0:64, 0, 64:128], 0.0), [], [bpr_])
            k.ve("dve", "memset", (Vaug.rearrange("p r h e -> p (r h) e")[:, :, 64:65], 1.0), [], b_V)
            load_xbf(c, src, src_tb, 0)
            def stage_A(b):
                bi_ = b % BPS
                transpose_block(c, all_act=True)
                if b + 1 < NB:
                    load_xbf(c, src, src_tb, b + 1)
                drip(3)
                xT, bxT = c.xT, c.b_xT
                slot0 = (2 * bi_) % RING
                if bi_ == 0:
                    k.ve("pool", "memset", (hc[:, :, 0:30], 0.0), [], b_hc)
                else:
                    k.ve("pool", "tensor_copy", (hc[:, :, 0:30], hc[:, :, TBM:TBM + 30]), b_hc, b_hc)
                for ch in range(4):
                    pa, bpa = psum1()
                    pg, bpg = psum1()
                    for kk in range(8):
                        k.mm(pa[:, 0:TBM], w_in[:, kk, 1536 + ch * 128:1536 + (ch + 1) * 128], xT[:, kk, :],
                             kk == 0, kk == 7, [bw[0], bxT[kk]], [bpa])
                    for kk in range(8):
                        k.mm(pg[:, 0:TBM], w_in[:, kk, 2048 + ch * 128:2048 + (ch + 1) * 128], xT[:, kk, :],
                             kk == 0, kk == 7, [bw[0], bxT[kk]], [bpg])
                    k.act(sig, pg[:, 0:TBM], AF.Tanh, [bpg], [b_sig], scale=0.5)
                    k.ve("dve", "scalar_tensor_tensor", (hc[:, ch, 30:30 + TBM], sig, 1.0, pa[:, 0:TBM]),
                         [b_sig, bpa], [b_hc[ch]], op0=ALU.add, op1=ALU.mult)
                    yield

            def conv_gen(b):
                for ch in range(4):
                    for wi in range(CONV_W):
                        k.ve("dve", "tensor_scalar", (dg[:, wi, :], ident_bf[:], cw[:, ch, wi:wi + 1], None),
                             [b_ident, b_cc], [b_dg[wi]], op0=ALU.mult)
                        if wi % 4 == 3:
                            yield
                    yield
                    pcv, bpcv = bank(6 + ch % 2), b_ps[6 + ch % 2]
                    for wi in range(CONV_W):
                        k.mm(pcv[:, 0:TBM], dg[:, wi, :], hc[:, ch, wi:wi + TBM], wi == 0, wi == CONV_W - 1,
                             [b_dg[wi], b_hc[ch]], [bpcv])
                        if wi % 8 == 7:
                            yield
                    k.act(cacc[:, ch, :], pcv[:, 0:TBM], AF.Identity, [bpcv, b_cc], [b_cacc[ch]],
                          bias=cb2[:, ch:ch + 1])
                    yield

            def stage_B1(b):
                bi_ = b % BPS
                xT, bxT = c.xT, c.b_xT
                slot0 = (2 * bi_) % RING
                for ch in range(4):
                    pq, bpq = psum1()
                    for kk in range(8):
                        k.mm(pq[:, 0:TBM], w_in[:, kk, ch * 128:(ch + 1) * 128], xT[:, kk, :], kk == 0, kk == 7,
                             [bw[0], bxT[kk]], [bpq])
                    k.act(qT[:, ch, :], pq[:, 0:TBM], AF.Copy, [bpq], [b_qT[ch]], scale=0.125)
                for ch in range(4):
                    pk, bpk = psum1()
                    for kk in range(8):
                        k.mm(pk[:, 0:TBM], w_in[:, kk, 512 + ch * 128:512 + (ch + 1) * 128], xT[:, kk, :],
                             kk == 0, kk == 7, [bw[0], bxT[kk]], [bpk])
                    k.act(kT[:, ch, slot0 * 128:slot0 * 128 + TBM], pk[:, 0:TBM], AF.Copy, [bpk], [b_kT[ch]])
                for t in range(2):
                    pv, bpv = psum1()
                    for kk in range(8):
                        k.mm(pv, xT[:, kk, t * 128:(t + 1) * 128], w_in[:, kk, 1024:1536], kk == 0, kk == 7,
                             [bw[0], bxT[kk]], [bpv])
                    k.act(Vaug[:, slot0 + t, :, 0:64], pv.rearrange("p (h e) -> p h e", h=8), AF.Copy,
                          [bpv], [b_V[slot0 + t]])

            def att_gen(b):
                bi_ = b % BPS
                for t in range(2):
                    m = 2 * bi_ + t
                    O = bank(4, 2)
                    bO = (b_ps[4], b_ps[5])
                    js = [j for j in range(5) if m - 4 + j >= 0]
                    for h in range(A_HEADS):
                        hcn, hp = h // 2, h % 2
                        prow = slice(hp * 64, (hp + 1) * 64)
                        s1, bs1 = psum1()
                        s2, bsb2 = psum1()
                        for j in js:
                            slot = (m - 4 + j) % RING
                            tgt, btg = (s1[:, j * 128:(j + 1) * 128], bs1) if j < 4 else (s2[:, 0:128], bsb2)
                            far = j < 3
                            k.mm(tgt, kT[prow, hcn, slot * 128:(slot + 1) * 128], qT[prow, hcn, t * 128:(t + 1) * 128],
                                 True, far, [b_kT[hcn], b_qT[hcn]], [btg])
                            if not far:
                                k.mm(tgt, ident_bf[:], biasT[:, h, j, :], False, True, [b_ident, b_bias], [btg])
                        pr, bpr = probsT[h % 2], b_pr[h % 2]
                        cbh = cbias[:, h:h + 1]
                        if 0 in js:
                            k.act(pr[64:128, 0, :], s1[64:128, 0:128], AF.Exp, [bs1, b_bias], [bpr], bias=cbias[64:128, h:h + 1])
                            k.act(pr[0:64, 0, 0:64], s1[0:64, 0:64], AF.Exp, [bs1, b_bias], [bpr], bias=cbias[0:64, h:h + 1])
                        jf = [j for j in js if j in (1, 2)]
                        if jf:
                            k.act(pr[:, jf[0]:3, :], s1[:, jf[0] * 128:384].rearrange("p (j q) -> p j q", q=128), AF.Exp,
                                  [bs1, b_bias], [bpr], bias=cbh)
                        if 3 in js:
                            k.act(pr[:, 3, :], s1[:, 384:512], AF.Exp, [bs1], [bpr])
                        k.act(pr[:, 4, :], s2[:, 0:128], AF.Exp, [bsb2], [bpr])
                        ob = h // 4
                        oc = (h % 4) * 65
                        for j in js:
                            slot = (m - 4 + j) % RING
                            k.mm(O[:, ob * 512 + oc:ob * 512 + oc + 65], pr[:, j, :], Vaug[:, slot, h, :],
                                 j == js[0], j == 4, [bpr, b_V[slot]], [bO[ob]])
                        if h == A_HEADS - 1:
                            for ob2 in range(2):
                                Ov = O[:, ob2 * 512:ob2 * 512 + 260].rearrange("p (h e) -> p h e", h=4)
                                k.ve("dve", "reciprocal", (rs[t][:, ob2 * 4:(ob2 + 1) * 4].unsqueeze(2), Ov[:, :, 64:65]),
                                     [bO[ob2]], [b_rs[t]])
                                k.ve("dve", "tensor_tensor",
                                     (ao[t][:, ob2 * 256:(ob2 + 1) * 256].rearrange("p (h e) -> p h e", h=4),
                                      Ov[:, :, 0:64],
                                      rs[t][:, ob2 * 4:(ob2 + 1) * 4].unsqueeze(2).to_broadcast([128, 4, 64]), ALU.mult),
                                     [bO[ob2], b_rs[t]], [b_ao[t]])
                        yield

            def stage_C(b):
                toks = [b * 2, b * 2 + 1]
                for t in range(2):
                    pc, bpc = psum1()
                    for ch in range(4):
                        k.mm(pc[:, ch * 128:(ch + 1) * 128], cacc[:, ch, t * 128:(t + 1) * 128], ident_f[:],
                             True, True, [b_cacc[ch], b_ident], [bpc])
                    k.act(cn[t], pc, AF.Copy, [bpc], [b_cn[t]])
                    k.ve("dve", "bn_stats", (stc[:, t, :], cn[t]), [b_cn[t]], [b_stc])
                    k.ve("dve", "bn_aggr", (mvc[:, t, :], stc[:, t, :]), [b_stc], [b_stc])
                yield
                for t in range(2):
                    pt, bpt = psum1()
                    for ch in range(4):
                        k.mm(pt[:, ch * 128:(ch + 1) * 128], ao[t][:, ch * 128:(ch + 1) * 128], ident_bf[:],
                             True, True, [b_ao[t], b_ident], [bpt])
                    k.act(catT[t][:, 0:4, :], pt.rearrange("p (c q) -> p c q", c=4), AF.Copy, [bpt], [b_cat[t][0]])
                k.act(sdc, mvc[:, :, 1:2], AF.Sqrt, [b_stc], [b_stc], bias=float(4.0 * LN_EPS))
                k.ve("dve", "reciprocal", (rsc, sdc), [b_stc], [b_stc])
                k.ve("dve", "scalar_tensor_tensor", (nmc, mvc[:, :, 0:1], -1.0, rsc), [b_stc], [b_stc],
                     op0=ALU.mult, op1=ALU.mult)
                yield
                for t in range(2):
                    k.act(cn[t], cn[t], AF.Identity, [b_cn[t], b_stc], [b_cn[t]], bias=nmc[:, t, :], scale=rsc[:, t, :])
                yield
                for t in range(2):
                    k.ve("dve", "tensor_tensor", (cn[t], cn[t], cng, ALU.mult), [b_cn[t], b_cc], [b_cn[t]])
                    k.ve("dve", "tensor_tensor", (cn[t], cn[t], cnb, ALU.add), [b_cn[t], b_cc], [b_cn[t]])
                    yield
                for t in range(2):
                    k.act(th, cn[t], AF.Tanh, [b_cn[t]], [b_th], scale=0.5)
                    k.ve("dve", "scalar_tensor_tensor", (cbf[t], th, 1.0, cn[t]), [b_th, b_cn[t]], [b_cbf[t]],
                         op0=ALU.add, op1=ALU.mult)
                    yield
                Ys = []
                for t in range(2):
                    pt, bpt = psum1()
                    for ch in range(4):
                        k.mm(pt[:, ch * 128:(ch + 1) * 128], cbf[t][:, ch * 128:(ch + 1) * 128], ident_bf[:],
                             True, True, [b_cbf[t], b_ident], [bpt])
                    k.act(catT[t][:, 4:8, :], pt.rearrange("p (c q) -> p c q", c=4), AF.Copy, [bpt], [b_cat[t][1]],
                          scale=0.5)
                    yield
                es_ = []
                for t in range(2):
                    ys = [psum1(), psum1()]
                    for n in range(2):
                        for kc in range(8):
                            k.mm(ys[n][0], catT[t][:, kc, :], w_out[:, kc, n * 512:(n + 1) * 512],
                                 kc == 0, kc == 7, [b_cat[t][kc // 4], bw[1]], [ys[n][1]])
                    i = c.ep_slot.pop(toks[t])
                    e, be = c.ep[i], c.b_ep[i]
                    es_.append((e, be))
                    for n in range(2):
                        sl = slice(n * 512, (n + 1) * 512)
                        k.ve("dve", "scalar_tensor_tensor", (e[:, sl], e[:, sl], float(ALPHA), ys[n][0]),
                             [be, ys[n][1]], [be], op0=ALU.mult, op1=ALU.add)
                    yield
                    for n in range(2):
                        sl = slice(n * 512, (n + 1) * 512)
                        k.ve("dve", "bn_stats", (ste[:, t, n * 6:(n + 1) * 6], e[:, sl]), [be], [b_ste])
                    k.ve("dve", "bn_aggr", (mve[:, t, :], ste[:, t, :]), [b_ste], [b_ste])
                    yield
                k.act(sde, mve[:, :, 1:2], AF.Sqrt, [b_ste], [b_ste], bias=float(LN_EPS))
                k.ve("dve", "reciprocal", (rse, sde), [b_ste], [b_ste])
                k.ve("dve", "scalar_tensor_tensor", (nme, mve[:, :, 0:1], -1.0, rse), [b_ste], [b_ste],
                     op0=ALU.mult, op1=ALU.mult)
                yield
                for t in range(2):
                    e, be = es_[t]
                    k.act(e, e, AF.Identity, [be, b_ste], [be], bias=nme[:, t, :], scale=rse[:, t, :])
                yield
                for t in range(2):
                    e, be = es_[t]
                    k.ve("dve", "tensor_tensor", (e, e, c.lng, ALU.mult), [be, c.b_ln], [be])
                    k.ve("dve", "tensor_tensor", (e, e, c.lnb, ALU.add), [be, c.b_ln], [be])
                    rows = slice(toks[t] * 128, (toks[t] + 1) * 128)
                    k.dma("sp", dst[rows, :], e, [be], [dst_tb[toks[t]]])
                    yield
                for t in range(2):
                    ep_prefetch(c, toks[t] + 2, src, src_tb, None)

            def take(gen, n):
                if gen is None:
                    return None
                for _ in range(n):
                    if next(gen, "end") == "end":
                        return None
                return gen

            ep_prefetch(c, 0, src, src_tb, None)
            ep_prefetch(c, 1, src, src_tb, None)
            for _ in stage_A(0):
                pass
            stage_B1(0)
            cg, ag = conv_gen(0), att_gen(0)
            while cg is not None or ag is not None:
                ag = take(ag, 1)
                cg = take(cg, 8)
            if NB > 1:
                for _ in stage_A(1):
                    pass
                stage_B1(1)
            for b in range(NB):
                cg = ag = None
                if b + 1 < NB:
                    cg, ag = conv_gen(b + 1), att_gen(b + 1)
                for _ in stage_C(b):
                    cg = take(cg, 3)
                    ag = take(ag, 1)
                while ag is not None:
                    ag = take(ag, 1)
                    cg = take(cg, 4)
                while cg is not None:
                    cg = take(cg, 8)
                if b + 2 < NB:
                    for _ in stage_A(b + 2):
                        pass
                    stage_B1(b + 2)
            drip(1000)

        def mixer1_phase(bi, wts, src, src_tb, dst, dst_tb):
            w_in, w_out = wts
            bw = wbuf_b[bi]
            ar.reset()
            c = alloc_common(TBM, NEP)
            NB = T // TBM
            BPS = S // TBM
            NCH = TBM // 64
            zT = ar.alloc((TBM,), F32, parts=16)
            b_zT = Buf()
            gw = ar.alloc((512,), F32, parts=16)
            gb = ar.alloc((4,), F32)
            ngb = ar.alloc((4,), F32)
            hgT = ar.alloc((8,), F32)
            tri = ar.alloc((64,), F32)
            rmask = ar.alloc((TBM,), F32)
            b_const = Buf()
            tA = [ar.alloc((TBM,), F32) for _ in range(2)]
            nb_ = [ar.alloc((TBM,), F32) for _ in range(2)]
            eb = [ar.alloc((TBM,), F32) for _ in range(2)]
            enb = [ar.alloc((TBM,), F32) for _ in range(2)]
            b_tA = [Buf(), Buf()]
            b_nb = [Buf(), Buf()]
            b_eb = [Buf(), Buf()]
            b_enb = [Buf(), Buf()]
            kf = [ar.alloc((TBM,), F32) for _ in range(2)]
            b_kf = [Buf(), Buf()]
            qtT = ar.alloc((4, TBM), BF16)
            b_qt = [Buf() for _ in range(4)]
            ktT = ar.alloc((4, TBM), BF16)
            b_kt = [Buf() for _ in range(4)]
            kdT = ar.alloc((4, TBM), BF16)
            b_kdT = [Buf() for _ in range(4)]
            ebl = ar.alloc((4, NCH), F32)
            b_ebl = [Buf() for _ in range(4)]
            kd = ar.alloc((2, 4, 128), BF16)
            b_kd = [Buf(), Buf()]
            v = ar.alloc((2, D), BF16)
            b_v = [Buf(), Buf()]
            sgg = [ar.alloc((D,), F32) for _ in range(2)]
            b_sgg = [Buf(), Buf()]
            attnT = [ar.alloc((4, 64), BF16) for _ in range(2)]
            b_at = [Buf(), Buf()]
            Sst = ar.alloc((4, 256), F32)
            Sbf = ar.alloc((4, 256), BF16)
            b_S = [Buf() for _ in range(4)]
            b_Sbf = [Buf() for _ in range(4)]
            obf = [ar.alloc((D,), BF16) for _ in range(2)]
            b_obf = [Buf(), Buf()]
            ss = ar.alloc((2, 4), F32)
            lss = ar.alloc((2, 4), F32)
            rinv = ar.alloc((2, 4), F32)
            b_ss = Buf()
            catT = [ar.alloc((8, 128), BF16) for _ in range(2)]
            b_cat = [[Buf(), Buf()], [Buf(), Buf()]]
            ste = ar.alloc((2, 12), F32)
            mve = ar.alloc((2, 2), F32)
            lve = ar.alloc((2, 1), F32)
            rse = ar.alloc((2, 1), F32)
            nme = ar.alloc((2, 1), F32)
            b_ste = Buf()
            load_ln(c, 4)
            k.dma("sp", gw, gatew_in, [], [b_const])
            k.dma("sp", gb, gateb_in, [], [b_const])
            k.dma("sp", hgT, headgT_in, [], [b_const])
            k.dma("sp", tri, tri_in, [], [b_const])
            k.ve("dve", "tensor_scalar", (ngb, gb, -1.0, None), [b_const], [b_const], op0=ALU.mult)
            k.ve("dve", "memset", (rmask, 1.0), [], [b_const])
            k.ve("dve", "memset", (rmask.rearrange("p (c j) -> p c j", j=64)[:, :, 0:1], 0.0), [], [b_const])
            load_xbf(c, src, src_tb, 0)
            DKS = 128 ** -0.5
            O_t = [bank(4, 2), bank(6, 2)]
            bO_t = [(b_ps[4], b_ps[5]), (b_ps[6], b_ps[7])]

            def stage_FE(b):
                bi_ = b % BPS
                transpose_block(c, all_act=True)
                if b + 1 < NB:
                    load_xbf(c, src, src_tb, b + 1)
                drip(3)
                xT, bxT = c.xT, c.b_xT
                yield
                pz, bpz = psum1()
                for kk in range(8):
                    k.mm(pz[0:16, 0:TBM], w_in[:, kk, 3072:3088], xT[:, kk, :], kk == 0, kk == 7,
                         [bw[0], bxT[kk]], [bpz])
                k.act(zT, pz[0:16, 0:TBM], AF.Copy, [bpz], [b_zT])
                yield
                for hh in range(4):
                    r = hh % 2
                    pg, bpg = psum1()
                    k.mm(pg[:, 0:TBM], gw[:, hh * 128:(hh + 1) * 128], zT, True, True, [b_const, b_zT], [bpg])
                    k.act(tA[r], pg[:, 0:TBM], AF.Exp, [bpg, b_const], [b_tA[r]], scale=-1.0, bias=ngb[:, hh:hh + 1])
                    pq, bpq = psum1()
                    for kk in range(8):
                        k.mm(pq[:, 0:TBM], w_in[:, kk, hh * 128:(hh + 1) * 128], xT[:, kk, :], kk == 0, kk == 7,
                             [bw[0], bxT[kk]], [bpq])
                    pk, bpk = psum1()
                    for kk in range(8):
                        k.mm(pk[:, 0:TBM], w_in[:, kk, 512 + hh * 128:512 + (hh + 1) * 128], xT[:, kk, :],
                             kk == 0, kk == 7, [bw[0], bxT[kk]], [bpk])
                    k.act(tA[r], tA[r], AF.Ln, [b_tA[r]], [b_tA[r]], bias=1.0)
                    k.ve("dve", "tensor_tensor_scan", (nb_[r], rmask, tA[r], 0.0, ALU.mult, ALU.add),
                         [b_const, b_tA[r]], [b_nb[r]])
                    k.act(eb[r], nb_[r], AF.Exp, [b_nb[r]], [b_eb[r]], scale=-1.0 / 16.0)
                    k.act(enb[r], nb_[r], AF.Exp, [b_nb[r]], [b_enb[r]], scale=1.0 / 16.0)
                    k.ve("dve", "tensor_copy",
                         (ebl[:, hh, :].unsqueeze(2), eb[r].rearrange("p (c j) -> p c j", j=64)[:, :, 63:64]),
                         [b_eb[r]], [b_ebl[hh]])
                    k.ve("dve", "scalar_tensor_tensor", (qtT[:, hh, :], pq[:, 0:TBM], float(DKS), eb[r]),
                         [bpq, b_eb[r]], [b_qt[hh]], op0=ALU.mult, op1=ALU.mult)
                    k.ve("dve", "tensor_tensor", (kf[r], pk[:, 0:TBM], enb[r], ALU.mult), [bpk, b_enb[r]], [b_kf[r]])
                    k.act(ktT[:, hh, :], kf[r], AF.Copy, [b_kf[r]], [b_kt[hh]])
                    k.ve("dve", "tensor_tensor",
                         (kdT[:, hh, :].rearrange("p (c j) -> p c j", j=64),
                          kf[r].rearrange("p (c j) -> p c j", j=64),
                          ebl[:, hh, :].unsqueeze(2).to_broadcast([128, NCH, 64]), ALU.mult),
                         [b_kf[r], b_ebl[hh]], [b_kdT[hh]])
                    yield
                for t in range(2):
                    pkd, bpkd = psum1()
                    for hh in range(4):
                        k.mm(pkd[:, hh * 128:(hh + 1) * 128], kdT[:, hh, t * 128:(t + 1) * 128], ident_bf[:],
                             True, True, [b_kdT[hh], b_ident], [bpkd])
                    k.act(kd[:, t, :, :], pkd.rearrange("p (h f) -> p h f", h=4), AF.Copy, [bpkd], [b_kd[t]])
                    for n in range(2):
                        pv, bpv = psum1()
                        for kk in range(8):
                            k.mm(pv, xT[:, kk, t * 128:(t + 1) * 128], w_in[:, kk, 1024 + n * 512:1024 + (n + 1) * 512],
                                 kk == 0, kk == 7, [bw[0], bxT[kk]], [bpv])
                        if n == 0:
                            k.act(v[:, t, 0:512], pv, AF.Copy, [bpv], [b_v[t]])
                        else:
                            k.ve("dve", "tensor_copy", (v[:, t, 512:1024], pv), [bpv], [b_v[t]])
                    yield

            def stage_REC(b):
                bi_ = b % BPS
                if bi_ == 0:
                    k.ve("dve", "memset", (Sst, 0.0), [], b_S)
                    k.ve("pool", "memset", (Sbf, 0.0), [], b_Sbf)
                for t in range(2):
                    O, bO = O_t[t], bO_t[t]
                    for cc in range(2):
                        ci = t * 2 + cc
                        rows = slice(cc * 64, (cc + 1) * 64)
                        cols = slice(ci * 64, (ci + 1) * 64)
                        pA, bpA = psum1()
                        for hh in range(4):
                            k.mm(pA[rows, hh * 64:(hh + 1) * 64], ktT[:, hh, cols], qtT[:, hh, cols], True, True,
                                 [b_kt[hh], b_qt[hh]], [bpA])
                        dSs = [psum1(), psum1()]
                        for hh in range(4):
                            oc = slice(hh * 256, (hh + 1) * 256)
                            dsb, bdsb = dSs[hh // 2]
                            k.mm(dsb[:, (hh % 2) * 256:(hh % 2 + 1) * 256], kd[rows, t, hh, :], v[rows, t, oc], True, True,
                                 [b_kd[t], b_v[t]], [bdsb])
                        at, bat = attnT[cc], b_at[cc]
                        k.ve("dve", "tensor_tensor",
                             (at[rows, :, :], pA[rows, 0:256].rearrange("p (h i) -> p h i", h=4),
                              tri[rows, :].unsqueeze(1).to_broadcast([64, 4, 64]), ALU.mult),
                             [bpA, b_const], [bat])
                        for hh in range(4):
                            oc = slice(hh * 256, (hh + 1) * 256)
                            k.mm(O[rows, oc], at[rows, hh, :], v[rows, t, oc], True, False,
                                 [bat, b_v[t]], [bO[hh // 2]])
                            k.mm(O[rows, oc], qtT[:, hh, cols], Sbf[:, hh, :], False, True,
                                 [b_qt[hh], b_Sbf[hh]], [bO[hh // 2]])
                        for hh in range(4):
                            dsb, bdsb = dSs[hh // 2]
                            k.ve("dve", "scalar_tensor_tensor",
                                 (Sst[:, hh, :], Sst[:, hh, :], ebl[:, hh, ci:ci + 1],
                                  dsb[:, (hh % 2) * 256:(hh % 2 + 1) * 256]),
                                 [b_S[hh], b_ebl[hh], bdsb], [b_S[hh]], op0=ALU.mult, op1=ALU.add)
                            k.act(Sbf[:, hh, :], Sst[:, hh, :], AF.Copy, [b_S[hh]], [b_Sbf[hh]])
                        yield

            def stage_G(b):
                xT, bxT = c.xT, c.b_xT
                for t in range(2):
                    for n in range(2):
                        pgg, bpgg = psum1()
                        for kk in range(8):
                            k.mm(pgg, xT[:, kk, t * 128:(t + 1) * 128], w_in[:, kk, 2048 + n * 512:2048 + (n + 1) * 512],
                                 kk == 0, kk == 7, [bw[0], bxT[kk]], [bpgg])
                        k.act(sgg[t][:, n * 512:(n + 1) * 512], pgg, AF.Silu, [bpgg], [b_sgg[t]])
                        yield

            def stage_TAIL(b):
                toks = [b * 2, b * 2 + 1]
                for t in range(2):
                    O, bO = O_t[t], bO_t[t]
                    for hh in range(4):
                        k.act(obf[t][:, hh * 256:(hh + 1) * 256], O[:, hh * 256:(hh + 1) * 256], AF.Square,
                              [bO[hh // 2]], [b_obf[t], b_ss], accum_out=ss[:, t, hh:hh + 1])
                yield
                k.act(lss, ss, AF.Ln, [b_ss], [b_ss], scale=1.0 / 256.0, bias=float(RMS_EPS))
                k.act(rinv, lss, AF.Exp, [b_ss], [b_ss], scale=-0.5)
                yield
                for t in range(2):
                    O, bO = O_t[t], bO_t[t]
                    for hh in range(4):
                        oc = slice(hh * 256, (hh + 1) * 256)
                        k.ve("dve", "scalar_tensor_tensor", (obf[t][:, oc], O[:, oc], rinv[:, t, hh:hh + 1], sgg[t][:, oc]),
                             [bO[hh // 2], b_ss, b_sgg[t]], [b_obf[t]], op0=ALU.mult, op1=ALU.mult)
                    yield
                for t in range(2):
                    for half in range(2):
                        pt, bpt = psum1()
                        for ch in range(4):
                            kc = half * 4 + ch
                            k.mm(pt[:, ch * 128:(ch + 1) * 128], obf[t][:, kc * 128:(kc + 1) * 128], ident_bf[:],
                                 True, True, [b_obf[t], b_ident], [bpt])
                        k.ve("dve", "tensor_tensor",
                             (catT[t][:, half * 4:(half + 1) * 4, :], pt.rearrange("p (c q) -> p c q", c=4),
                              hgT[:, half * 4:(half + 1) * 4].unsqueeze(2).to_broadcast([128, 4, 128]), ALU.mult),
                             [bpt, b_const], [b_cat[t][half]])
                    yield
                es_ = []
                for t in range(2):
                    ys = [psum1(), psum1()]
                    for n in range(2):
                        for kc in range(8):
                            k.mm(ys[n][0], catT[t][:, kc, :], w_out[:, kc, n * 512:(n + 1) * 512],
                                 kc == 0, kc == 7, [b_cat[t][kc // 4], bw[1]], [ys[n][1]])
                    i = c.ep_slot.pop(toks[t])
                    e, be = c.ep[i], c.b_ep[i]
                    es_.append((e, be))
                    for n in range(2):
                        sl = slice(n * 512, (n + 1) * 512)
                        k.ve("dve", "scalar_tensor_tensor", (e[:, sl], e[:, sl], float(ALPHA), ys[n][0]),
                             [be, ys[n][1]], [be], op0=ALU.mult, op1=ALU.add)
                    yield
                    for n in range(2):
                        sl = slice(n * 512, (n + 1) * 512)
                        k.ve("dve", "bn_stats", (ste[:, t, n * 6:(n + 1) * 6], e[:, sl]), [be], [b_ste])
                    k.ve("dve", "bn_aggr", (mve[:, t, :], ste[:, t, :]), [b_ste], [b_ste])
                    yield
                k.act(lve, mve[:, :, 1:2], AF.Ln, [b_ste], [b_ste], bias=float(LN_EPS))
                k.act(rse, lve, AF.Exp, [b_ste], [b_ste], scale=-0.5)
                k.ve("dve", "scalar_tensor_tensor", (nme, mve[:, :, 0:1], -1.0, rse), [b_ste], [b_ste],
                     op0=ALU.mult, op1=ALU.mult)
                yield
                for t in range(2):
                    e, be = es_[t]
                    k.act(e, e, AF.Identity, [be, b_ste], [be], bias=nme[:, t, :], scale=rse[:, t, :])
                yield
                for t in range(2):
                    e, be = es_[t]
                    k.ve("dve", "tensor_tensor", (e, e, c.lng, ALU.mult), [be, c.b_ln], [be])
                    k.ve("dve", "tensor_tensor", (e, e, c.lnb, ALU.add), [be, c.b_ln], [be])
                    rows_ = slice(toks[t] * 128, (toks[t] + 1) * 128)
                    k.dma("sp", dst[rows_, :], e, [be], [dst_tb[toks[t]]])
                    yield
                for t in range(2):
                    ep_prefetch(c, toks[t] + 2, src, src_tb, None)

            def take(gen, n):
                if gen is None:
                    return None
                for _ in range(n):
                    if next(gen, "end") == "end":
                        return None
                return gen

            ep_prefetch(c, 0, src, src_tb, None)
            ep_prefetch(c, 1, src, src_tb, None)
            for _ in stage_FE(0):
                pass
            for b in range(NB):
                gg = stage_G(b)
                for _ in stage_REC(b):
                    gg = take(gg, 1)
                while gg is not None:
                    gg = take(gg, 1)
                fe = stage_FE(b + 1) if b + 1 < NB else None
                for _ in stage_TAIL(b):
                    fe = take(fe, 1)
                while fe is not None:
                    fe = take(fe, 1)
            drip(1000)

        if phases == ("m0", "f0", "m1", "f1"):
            w0, l0 = mixer_weight_loads(0, "even_w_in", IN_EVEN, "even_w_out")
            w1, l1 = ffn_weight_loads(1, 0, 0)
            pending_loads.extend(l0)
            drip(1000)
            pending_loads.extend(l1)
            mixer0_phase(0, w0, x_in, xin_tb, xs[0], xs_tb[0])
            p.barrier()
            w2, l2 = ffn_weight_loads(0, 0, 1)
            pending_loads.extend(l2)
            ffn_phase(0, 0, 1, w1, xs[0], xs_tb[0], None, None, 2)
            p.barrier()
            w3, l3 = mixer_weight_loads(1, "odd_w_in", IN_ODD, "odd_w_out")
            pending_loads.extend(l3)
            ffn_phase(0, 1, 0, w2, xs[0], xs_tb[0], xs[1], xs_tb[1], 2)
            p.barrier()
            w4, l4 = ffn_weight_loads(0, 1, 0)
            pending_loads.extend(l4)
            mixer1_phase(1, w3, xs[1], xs_tb[1], xs[0], xs_tb[0])
            p.barrier()
            w5, l5 = ffn_weight_loads(1, 1, 1)
            pending_loads.extend(l5)
            ffn_phase(1, 0, 0, w4, xs[0], xs_tb[0], None, None, 6)
            p.barrier()
            ffn_phase(1, 1, 1, w5, xs[0], xs_tb[0], out_d, out_tb, 6)
        elif phases == ("f0",):
            wa, la = ffn_weight_loads(0, 0, 0)
            wb, lb = ffn_weight_loads(1, 0, 1)
            pending_loads.extend(la + lb)
            drip(1000)
            ffn_phase(0, 0, 0, wa, x_in, xin_tb, None, None, 2)
            p.barrier()
            ffn_phase(0, 1, 1, wb, x_in, xin_tb, out_d, out_tb, 2)
        elif phases == ("m1",):
            wm, lm = mixer_weight_loads(0, "odd_w_in", IN_ODD, "odd_w_out")
            pending_loads.extend(lm)
            drip(1000)
            mixer1_phase(0, wm, x_in, xin_tb, out_d, out_tb)
        elif phases == ("m0",):
            wm, lm = mixer_weight_loads(0, "even_w_in", IN_EVEN, "even_w_out")
            pending_loads.extend(lm)
            drip(1000)
            mixer0_phase(0, wm, x_in, xin_tb, out_d, out_tb)
        p.finalize_and_emit()
    return nc


def make_lnv(inp):
    return np.ascontiguousarray(np.stack([
        inp["mix_norm_g"][0], inp["mix_norm_b"][0], inp["ffn_norm_g"][0], inp["ffn_norm_b"][0],
        inp["mix_norm_g"][1], inp["mix_norm_b"][1], inp["ffn_norm_g"][1], inp["ffn_norm_b"][1]], 0).astype(np.float32))


def make_biasT(rel_bias):
    rb = np.asarray(rel_bias, dtype=np.float32)
    ki = np.arange(128)[:, None]
    qi = np.arange(128)[None, :]
    out = np.empty((128, A_HEADS, 5, 128), np.float32)
    for j in range(5):
        idx = np.clip(128 * (4 - j) + qi - ki, -128, 128) + 128
        t = rb[:, idx]
        out[:, :, j, :] = np.transpose(t, (1, 0, 2))
    out[0:64, :, 0, 64:128] = NEGM
    out[64:128, :, 4, 0:64] = NEGM
    return np.ascontiguousarray(out.reshape(128, A_HEADS * 5 * 128))


def make_in_maps(inp, nseq=NSEQ, ncores=NCORES):
    x = np.asarray(inp["x"], dtype=np.float32)
    cw = np.asarray(inp["even_conv_w"][0], np.float32)
    common = {
        "even_w_in": np.ascontiguousarray(inp["even_w_in"][0]),
        "even_w_out": np.ascontiguousarray(inp["even_w_out"][0]),
        "odd_w_in": np.ascontiguousarray(inp["odd_w_in"][0]),
        "odd_w_out": np.ascontiguousarray(inp["odd_w_out"][0]),
        "lnv": make_lnv(inp),
        "ident": np.eye(128, dtype=np.float32),
        "biasT": make_biasT(inp["even_rel_bias"][0]),
        "convw": np.ascontiguousarray(cw.T.reshape(4, 128, CONV_W).transpose(1, 0, 2).reshape(128, 4 * CONV_W)),
        "convb": np.ascontiguousarray(np.asarray(inp["even_conv_b"][0], np.float32).reshape(4, 128).T),
        "cbias": np.ascontiguousarray(np.asarray(inp["even_rel_bias"][0], np.float32)[:, 2 * 128].reshape(1, A_HEADS)),
        "convn": np.ascontiguousarray(np.stack([inp["even_conv_norm_g"][0], inp["even_conv_norm_b"][0]], 0)
                                      .astype(np.float32)),
        "gatew": np.ascontiguousarray(np.asarray(inp["odd_gate_w"][0], np.float32)),
        "gateb": np.ascontiguousarray(np.asarray(inp["odd_gate_b"][0], np.float32).reshape(4, 128).T),
        "headg": np.ascontiguousarray(np.asarray(inp["odd_head_norm_g"][0], np.float32).reshape(1, D)),
        "headgT": np.ascontiguousarray(np.asarray(inp["odd_head_norm_g"][0], np.float32).reshape(8, 128).T),
        "tri": np.ascontiguousarray((np.arange(128)[:, None] % 64 <= np.arange(64)[None, :]).astype(np.float32)),
    }
    for l in range(2):
        common["wg%d" % l] = np.ascontiguousarray(inp["ffn_w_gate"][l])
        common["wu%d" % l] = np.ascontiguousarray(inp["ffn_w_up"][l])
        common["wd%d" % l] = np.ascontiguousarray(inp["ffn_w_down"][l])
    maps = []
    for c_ in range(ncores):
        m = dict(common)
        m["x"] = np.ascontiguousarray(x[c_ * nseq:(c_ + 1) * nseq].reshape(nseq * S, D))
        maps.append(m)
    return maps


def kernel(**inputs):
    inp = {k_: np.asarray(v) for k_, v in inputs.items()}
    nc = build()
    in_maps = make_in_maps(inp)
    res = run_bass_kernel_spmd(nc, in_maps, core_ids=list(range(NCORES)))
    outs = [np.asarray(r["out"]).reshape(NSEQ, S, D) for r in res.results]
    return np.concatenate(outs, axis=0).astype(np.float32)
```

```python
import contextlib
import numpy as np
import concourse.bass as bass
import concourse.mybir as mybir
from concourse.bass_utils import run_bass_kernel_spmd

F32 = mybir.dt.float32
BF16 = mybir.dt.bfloat16
AF = mybir.ActivationFunctionType
ALU = mybir.AluOpType

D = 1024
S = 2048
NSEQ = 2
NCORES = 8
DFF = 2816
HF = DFF // 2
NJ = HF // 128
ALPHA = 4 ** 0.25
LN_EPS = 1e-5
RMS_EPS = 1e-6
TB = 512
TBM = 256
WCOLS = 3 * 8 * HF
ACOLS = 38000
A_HEADS = 8
CONV_W = 31
NEGM = -30000.0
IN_EVEN = 2560
IN_ODD = 3088


class Op:
    __slots__ = ("eng", "fn", "deps", "marked", "sem", "val", "dma", "key")


class Buf:
    __slots__ = ("name", "writers", "readers")

    def __init__(self, name=""):
        self.name = name
        self.writers = {}
        self.readers = {}


class Prog:
    ENG = ("pe", "act", "dve", "pool", "sp")
    NDQ = 8

    def __init__(self, nc, es):
        self.nc = nc
        self.streams = {e: [] for e in self.ENG}
        self.sems = {e: es.enter_context(nc.semaphore("s_" + e)) for e in ("pe", "act", "dve", "pool")}
        self.dq = {q: [es.enter_context(nc.semaphore("d_%s%d" % (q, i))) for i in range(self.NDQ)]
                   for q in ("sp", "pool", "act")}
        self.dq_n = {q: 0 for q in self.dq}
        self.dq_cnt = {q: [0] * self.NDQ for q in self.dq}
        self.dq_last = {q: [None] * self.NDQ for q in self.dq}
        self.pending = {}

    def barrier(self):
        lasts = []
        for e in ("pe", "act", "dve", "pool"):
            for o in reversed(self.streams[e]):
                if not o.dma:
                    lasts.append(o)
                    break
        for q in self.dq:
            for last in self.dq_last[q]:
                if last is not None:
                    lasts.append(last)
        self.pending = {e: list(lasts) for e in self.ENG}

    def op(self, eng, fn, reads=(), writes=(), dma=False):
        o = Op()
        o.eng, o.fn, o.deps, o.marked, o.dma = eng, fn, [], False, dma
        o.sem = None
        o.val = 0
        if dma:
            i = self.dq_n[eng]
            self.dq_n[eng] += 1
            slot = i % self.NDQ
            o.key = (eng, slot)
            prev = self.dq_last[eng][slot]
            if prev is not None:
                o.deps.append(prev)
            o.sem = self.dq[eng][slot]
            self.dq_cnt[eng][slot] += 16
            o.val = self.dq_cnt[eng][slot]
            self.dq_last[eng][slot] = o
        else:
            o.key = eng
        seen = set()

        def add(d, raw):
            if d is o or id(d) in seen:
                return
            if not dma and not d.dma and d.eng == eng:
                if eng == "pe" or not raw:
                    return
            seen.add(id(d))
            o.deps.append(d)

        pend = self.pending.pop(eng, None)
        if pend:
            for d in pend:
                add(d, False)
        for b in reads:
            for d in b.writers.values():
                add(d, True)
        for b in writes:
            for d in b.writers.values():
                add(d, False)
            for d in b.readers.values():
                add(d, False)
        for b in reads:
            b.readers[o.key] = o
        for b in writes:
            if b.readers:
                b.readers = {}
                b.writers = {}
            b.writers[o.key] = o
        self.streams[eng].append(o)
        return o

    def finalize_and_emit(self):
        nc = self.nc
        for e in self.ENG:
            for o in self.streams[e]:
                for d in o.deps:
                    if not d.dma:
                        d.marked = True
        for e in ("pe", "act", "dve", "pool"):
            c = 0
            for o in self.streams[e]:
                if not o.dma and o.marked:
                    c += 1
                    o.sem = self.sems[e]
                    o.val = c
        streams = self.streams

        def emit(ename, eng):
            waited = {}
            for o in streams[ename]:
                need = {}
                for d in o.deps:
                    k = id(d.sem)
                    if waited.get(k, 0) >= d.val:
                        continue
                    if k not in need or need[k][1] < d.val:
                        need[k] = (d.sem, d.val)
                for k, (sem, val) in need.items():
                    eng.wait_ge(sem, val)
                    waited[k] = val
                ins = o.fn(eng)
                if o.dma:
                    ins.then_inc(o.sem, 16)
                elif o.marked:
                    ins.then_inc(o.sem, 1)
            if ename in self.dq:
                for slot in range(self.NDQ):
                    last = self.dq_last[ename][slot]
                    if last is not None and waited.get(id(last.sem), 0) < last.val:
                        eng.wait_ge(last.sem, last.val)

        with nc.Block() as block:
            @block.tensor
            def _(e):
                emit("pe", e)

            @block.scalar
            def _(e):
                emit("act", e)

            @block.vector
            def _(e):
                emit("dve", e)

            @block.gpsimd
            def _(e):
                emit("pool", e)

            @block.sync
            def _(e):
                emit("sp", e)


class K:
    def __init__(self, nc, es):
        self.nc = nc
        self.p = Prog(nc, es)
        self.nbuf = 0

    def sb(self, name, shape, dt):
        return self.nc.alloc_sbuf_tensor(name, list(shape), dt)

    def mm(self, out, lhsT, rhs, start, stop, reads, writes):
        return self.p.op("pe", lambda e: e.matmul(out, lhsT, rhs, start=start, stop=stop), reads, writes)

    def act(self, out, in_, func, reads, writes, bias=None, scale=None, accum_out=None):
        kw = {}
        if bias is not None:
            kw["bias"] = bias
        if scale is not None:
            kw["scale"] = scale
        if accum_out is not None:
            kw["accum_out"] = accum_out
        return self.p.op("act", lambda e: e.activation(out, in_, func, **kw), reads, writes)

    def ve(self, eng, name, args, reads, writes, **kw):
        return self.p.op(eng, lambda e: getattr(e, name)(*args, **kw), reads, writes)

    def dma(self, q, out, in_, reads, writes):
        return self.p.op(q, lambda e: e.dma_start(out=out, in_=in_), reads, writes, dma=True)


class Arena:
    def __init__(self, t, ncols):
        self.t = t
        self.n = ncols
        self.off = 0

    def reset(self):
        self.off = 0

    def alloc(self, shape, dt, parts=128):
        n = 1
        for s_ in shape:
            n *= s_
        cols = n * (2 if dt == F32 else 1)
        cols = (cols + 15) // 16 * 16
        assert self.off + cols <= self.n, ("arena overflow", self.off, cols, self.n)
        ap = self.t[0:parts, self.off:self.off + cols]
        self.off += cols
        if dt == F32:
            ap = ap.bitcast(F32)
        ap = ap[:, 0:n]
        if len(shape) == 2:
            ap = ap.rearrange("p (a b) -> p a b", a=shape[0])
        elif len(shape) == 3:
            ap = ap.rearrange("p (a b c) -> p a b c", a=shape[0], b=shape[1])
        return ap


def build(cfg=None):
    cfg = cfg or {}
    nseq = cfg.get("nseq", NSEQ)
    T = nseq * S
    NT = T // 128
    phases = cfg.get("phases", ("m0", "f0", "m1", "f1"))
    nc = bass.Bass("TRN2", target_bir_lowering=False)

    def din(name, shape):
        return nc.dram_tensor(name, list(shape), F32, kind="ExternalInput").ap()

    x_in = din("x", (T, D))
    w = {}
    w["even_w_in"] = din("even_w_in", (D, IN_EVEN))
    w["even_w_out"] = din("even_w_out", (D, D))
    w["odd_w_in"] = din("odd_w_in", (D, IN_ODD))
    w["odd_w_out"] = din("odd_w_out", (D, D))
    for l in range(2):
        w["wg%d" % l] = din("wg%d" % l, (D, DFF))
        w["wu%d" % l] = din("wu%d" % l, (D, DFF))
        w["wd%d" % l] = din("wd%d" % l, (DFF, D))
    lnv = din("lnv", (8, D))
    ident_in = din("ident", (128, 128))
    biasT_in = din("biasT", (128, A_HEADS * 5 * 128))
    convw_in = din("convw", (128, 4 * CONV_W))
    convb_in = din("convb", (128, 4))
    convn_in = din("convn", (2, 512))
    cbias_in = din("cbias", (1, A_HEADS))
    gatew_in = din("gatew", (16, 512))
    gateb_in = din("gateb", (128, 4))
    headg_in = din("headg", (1, D))
    headgT_in = din("headgT", (128, 8))
    tri_in = din("tri", (128, 64))
    out_d = nc.dram_tensor("out", [T, D], F32, kind="ExternalOutput").ap()
    xs = [nc.dram_tensor("xs%d" % i, [T, D], F32, kind="Internal").ap() for i in range(2)]
    ya = nc.dram_tensor("ya", [T, D], F32, kind="Internal").ap()
    xs_tb = [[Buf() for _ in range(NT)] for _ in range(2)]
    ya_tb = [Buf() for _ in range(NT)]
    xin_tb = [Buf() for _ in range(NT)]
    out_tb = [Buf() for _ in range(NT)]

    es = contextlib.ExitStack()
    with es:
        k = K(nc, es)
        p = k.p
        wbuf = [k.sb("wbuf%d" % i, (128, WCOLS), BF16) for i in range(2)]
        wbuf_b = [[Buf("wb%d_%d" % (i, j)) for j in range(4)] for i in range(2)]
        ident_bf = k.sb("ident_bf", (128, 128), BF16)
        ident_f = k.sb("ident_f", (128, 128), F32)
        b_ident = Buf("ident")
        NEP = 2
        stt = [k.sb("stt%d" % i, (128, 16), F32) for i in range(4)]
        b_stt = [Buf() for _ in range(4)]
        arena_t = k.sb("arena", (128, ACOLS), BF16)
        ar = Arena(arena_t, ACOLS)
        ps_all = nc.alloc_psum_tensor("ps_all", [128, 4096], F32)
        b_ps = [Buf("ps%d" % i) for i in range(8)]

        def bank(i, n=1):
            return ps_all[:, i * 512:(i + n) * 512]

        ps_rr = [0]

        def psum1():
            i = ps_rr[0] % 4
            ps_rr[0] += 1
            return bank(i), b_ps[i]

        k.dma("pool", ident_bf[:], ident_in, [], [b_ident])
        k.dma("sp", ident_f[:], ident_in, [], [b_ident])

        def wview(bi, off, kk, n):
            return wbuf[bi][:, off:off + kk * n].rearrange("p (k n) -> p k n", k=kk)

        def ffn_weight_loads(bi, l, hf):
            wg = wview(bi, 0, 8, HF)
            wu = wview(bi, 8 * HF, 8, HF)
            wd = wview(bi, 16 * HF, NJ, D)
            loads = []
            for kk in range(8):
                loads.append(lambda kk=kk: k.dma(
                    "pool", wg[:, kk, :], w["wg%d" % l][kk * 128:(kk + 1) * 128, hf * HF:(hf + 1) * HF],
                    [], [wbuf_b[bi][0]]))
                loads.append(lambda kk=kk: k.dma(
                    "pool", wu[:, kk, :], w["wu%d" % l][kk * 128:(kk + 1) * 128, hf * HF:(hf + 1) * HF],
                    [], [wbuf_b[bi][1]]))
            for j in range(NJ):
                r0 = hf * HF + j * 128
                loads.append(lambda j=j, r0=r0: k.dma("pool", wd[:, j, :], w["wd%d" % l][r0:r0 + 128, :],
                                                     [], [wbuf_b[bi][2]]))
            return (wg, wu, wd), loads

        def mixer_weight_loads(bi, name_in, nin, name_out, first_cols=None):
            w_in = wview(bi, 0, 8, nin)
            w_out = wview(bi, 8 * nin, 8, D)
            loads = []
            if first_cols is not None:
                c0, c1 = first_cols
                for kk in range(8):
                    loads.append(lambda kk=kk: k.dma("pool", w_in[:, kk, c0:c1], w[name_in][kk * 128:(kk + 1) * 128, c0:c1],
                                                     [], [wbuf_b[bi][0]]))
                for kk in range(8):
                    loads.append(lambda kk=kk: k.dma("pool", w_in[:, kk, 0:c0], w[name_in][kk * 128:(kk + 1) * 128, 0:c0],
                                                     [], [wbuf_b[bi][0]]))
                    if c1 < nin:
                        loads.append(lambda kk=kk: k.dma("pool", w_in[:, kk, c1:nin],
                                                         w[name_in][kk * 128:(kk + 1) * 128, c1:nin],
                                                         [], [wbuf_b[bi][0]]))
            else:
                for kk in range(8):
                    loads.append(lambda kk=kk: k.dma("pool", w_in[:, kk, :], w[name_in][kk * 128:(kk + 1) * 128, :],
                                                     [], [wbuf_b[bi][0]]))
            for kk in range(8):
                loads.append(lambda kk=kk: k.dma("pool", w_out[:, kk, :], w[name_out][kk * 128:(kk + 1) * 128, :],
                                                 [], [wbuf_b[bi][1]]))
            return (w_in, w_out), loads

        pending_loads = []

        def drip(n):
            for _ in range(n):
                if pending_loads:
                    pending_loads.pop(0)()

        class Ctx:
            pass

        def alloc_common(tb, nep):
            c = Ctx()
            c.tb = tb
            c.ntl = tb // 128
            c.lng = ar.alloc((D,), F32)
            c.lnb = ar.alloc((D,), F32)
            c.b_ln = Buf("ln")
            c.xbf = ar.alloc((c.ntl, D), BF16)
            c.b_xbf = Buf("xbf")
            c.xT = ar.alloc((8, tb), BF16)
            c.b_xT = [Buf() for _ in range(8)]
            c.nep = nep
            c.ep = [ar.alloc((D,), F32) for _ in range(nep)]
            c.b_ep = [Buf() for _ in range(nep)]
            c.ep_n = 0
            c.ep_slot = {}
            return c

        def load_ln(c, row):
            k.dma("sp", c.lng, lnv[row:row + 1, :].partition_broadcast(128), [], [c.b_ln])
            k.dma("sp", c.lnb, lnv[row + 1:row + 2, :].partition_broadcast(128), [], [c.b_ln])

        def load_xbf(c, src, src_tb, b):
            v = src[b * c.tb:(b + 1) * c.tb, :].rearrange("(t p) d -> p t d", p=128)
            k.dma("pool", c.xbf, v, [src_tb[b * c.ntl + t] for t in range(c.ntl)], [c.b_xbf])

        def transpose_block(c, all_act=False):
            ntl = c.ntl
            for kk in range(8):
                pt, bpt = psum1()
                for t in range(ntl):
                    k.mm(pt[:, t * 128:(t + 1) * 128], c.xbf[:, t, kk * 128:(kk + 1) * 128], ident_bf[:],
                         True, True, [c.b_xbf, b_ident], [bpt])
                if all_act or kk % 2 == 0:
                    k.act(c.xT[:, kk, :], pt[:, 0:c.tb], AF.Copy, [bpt], [c.b_xT[kk]])
                else:
                    k.ve("dve", "tensor_copy", (c.xT[:, kk, :], pt[:, 0:c.tb]), [bpt], [c.b_xT[kk]])

        def ep_prefetch(c, tok, src, src_tb, ex):
            if tok >= NT or tok in c.ep_slot:
                return
            i = c.ep_n % c.nep
            c.ep_n += 1
            c.ep_slot[tok] = i
            rows = slice(tok * 128, (tok + 1) * 128)
            k.dma("sp", c.ep[i], src[rows, :], [src_tb[tok]], [c.b_ep[i]])
            if ex is not None:
                yat, b_yat = ex
                k.dma("sp", yat[i], ya[rows, :], [ya_tb[tok]], [b_yat[i]])

        def ln_epilogue(c, tok, Y, bY, src, src_tb, ex, dst, dst_tb):
            ep_prefetch(c, tok, src, src_tb, ex)
            i = c.ep_slot.pop(tok)
            ep_prefetch(c, tok + 1, src, src_tb, ex)
            e, be = c.ep[i], c.b_ep[i]
            st, bs = stt[i], b_stt[i]
            rows = slice(tok * 128, (tok + 1) * 128)
            for n in range(2):
                sl = slice(n * 512, (n + 1) * 512)
                k.ve("dve", "scalar_tensor_tensor", (e[:, sl], e[:, sl], float(ALPHA), Y[:, sl]),
                     [be, bY[n]], [be], op0=ALU.mult, op1=ALU.add)
            if ex is not None:
                yat, b_yat = ex
                k.ve("dve", "tensor_tensor", (e, e, yat[i], ALU.add), [be, b_yat[i]], [be])
            for n in range(2):
                sl = slice(n * 512, (n + 1) * 512)
                k.ve("dve", "bn_stats", (st[:, n * 6:(n + 1) * 6], e[:, sl]), [be], [bs])
            k.ve("dve", "bn_aggr", (st[:, 12:14], st[:, 0:12]), [bs], [bs])
            k.act(st[:, 11:12], st[:, 13:14], AF.Sqrt, [bs], [bs], bias=float(LN_EPS))
            k.ve("dve", "reciprocal", (st[:, 14:15], st[:, 11:12]), [bs], [bs])
            k.ve("dve", "scalar_tensor_tensor", (st[:, 15:16], st[:, 12:13], -1.0, st[:, 14:15]),
                 [bs], [bs], op0=ALU.mult, op1=ALU.mult)
            k.act(e, e, AF.Identity, [be, bs], [be], bias=st[:, 15:16], scale=st[:, 14:15])
            k.ve("dve", "tensor_tensor", (e, e, c.lng, ALU.mult), [be, c.b_ln], [be])
            k.ve("dve", "tensor_tensor", (e, e, c.lnb, ALU.add), [be, c.b_ln], [be])
            k.dma("sp", dst[rows, :], e, [be], [dst_tb[tok]])

        def ffn_phase(l, hf, bi, wts, src, src_tb, dst, dst_tb, lnrow):
            wg, wu, wd = wts
            bw = wbuf_b[bi]
            ar.reset()
            nep = 2 if hf == 0 else 4
            c = alloc_common(TB, nep)
            NB = T // TB
            hT = ar.alloc((NJ, TB), BF16)
            b_hT = [Buf() for _ in range(NJ)]
            sg = [ar.alloc((TB,), F32) for _ in range(2)]
            b_sg = [Buf(), Buf()]
            ste = ar.alloc((4, 12), F32)
            mve = ar.alloc((4, 2), F32)
            sde = ar.alloc((4, 1), F32)
            rse = ar.alloc((4, 1), F32)
            nme = ar.alloc((4, 1), F32)
            b_ste = Buf()
            if hf == 1:
                load_ln(c, lnrow)
            esrc, esrc_tb = (src, src_tb) if hf == 0 else (ya, ya_tb)
            load_xbf(c, src, src_tb, 0)
            if hf == 0:
                ep_prefetch(c, 0, esrc, esrc_tb, None)
            else:
                for t in range(4):
                    ep_prefetch(c, t, esrc, esrc_tb, None)
            yn = 0
            deferred = []
            transpose_block(c, all_act=True)
            if NB > 1:
                load_xbf(c, src, src_tb, 1)
            for b in range(NB):
                drip(4)
                for j in range(NJ):
                    pg, bpg = psum1()
                    pu, bpu = psum1()
                    for kk in range(8):
                        k.mm(pg, wg[:, kk, j * 128:(j + 1) * 128], c.xT[:, kk, :], kk == 0, kk == 7,
                             [bw[0], c.b_xT[kk]], [bpg])
                    for kk in range(8):
                        k.mm(pu, wu[:, kk, j * 128:(j + 1) * 128], c.xT[:, kk, :], kk == 0, kk == 7,
                             [bw[1], c.b_xT[kk]], [bpu])
                    si = j % 2
                    k.act(sg[si], pg, AF.Silu, [bpg], [b_sg[si]])
                    k.ve("dve", "tensor_tensor", (hT[:, j, :], sg[si], pu, ALU.mult),
                         [b_sg[si], bpu], [b_hT[j]])
                    if j == 3 and deferred:
                        deferred.pop(0)()
                es_ = []
                for t in range(4):
                    tok = b * 4 + t
                    yb = 4 + 2 * (yn % 2)
                    yn += 1
                    Y = bank(yb, 2)
                    bY = (b_ps[yb], b_ps[yb + 1])
                    for n in range(2):
                        for j in range(NJ):
                            k.mm(Y[:, n * 512:(n + 1) * 512], hT[:, j, t * 128:(t + 1) * 128],
                                 wd[:, j, n * 512:(n + 1) * 512], j == 0, j == NJ - 1, [b_hT[j], bw[2]], [bY[n]])
                    rows = slice(tok * 128, (tok + 1) * 128)
                    i = c.ep_slot.pop(tok)
                    e, be = c.ep[i], c.b_ep[i]
                    if hf == 0:
                        ep_prefetch(c, tok + 1, esrc, esrc_tb, None)
                        for n in range(2):
                            sl = slice(n * 512, (n + 1) * 512)
                            k.ve("dve", "scalar_tensor_tensor", (e[:, sl], e[:, sl], float(ALPHA), Y[:, sl]),
                                 [be, bY[n]], [be], op0=ALU.mult, op1=ALU.add)
                        k.dma("sp", ya[rows, :], e, [be], [ya_tb[tok]])
                    else:
                        for n in range(2):
                            sl = slice(n * 512, (n + 1) * 512)
                            k.ve("dve", "tensor_tensor", (e[:, sl], e[:, sl], Y[:, sl], ALU.add),
                                 [be, bY[n]], [be])
                        for n in range(2):
                            sl = slice(n * 512, (n + 1) * 512)
                            k.ve("dve", "bn_stats", (ste[:, t, n * 6:(n + 1) * 6], e[:, sl]), [be], [b_ste])
                        k.ve("dve", "bn_aggr", (mve[:, t, :], ste[:, t, :]), [b_ste], [b_ste])
                        es_.append((e, be, tok))
                if b + 1 < NB:
                    transpose_block(c, all_act=True)
                    if b + 2 < NB:
                        load_xbf(c, src, src_tb, b + 2)
                if hf == 1:
                    k.act(sde, mve[:, :, 1:2], AF.Sqrt, [b_ste], [b_ste], bias=float(LN_EPS))
                    k.ve("dve", "reciprocal", (rse, sde), [b_ste], [b_ste])
                    k.ve("dve", "scalar_tensor_tensor", (nme, mve[:, :, 0:1], -1.0, rse), [b_ste], [b_ste],
                         op0=ALU.mult, op1=ALU.mult)
                    for t in range(4):
                        e, be, tok = es_[t]
                        k.act(e, e, AF.Identity, [be, b_ste], [be], bias=nme[:, t, :], scale=rse[:, t, :])

                    def fin(es_=es_, b=b):
                        for (e, be, tok) in es_:
                            k.ve("dve", "tensor_tensor", (e, e, c.lng, ALU.mult), [be, c.b_ln], [be])
                            k.ve("dve", "tensor_tensor", (e, e, c.lnb, ALU.add), [be, c.b_ln], [be])
                            rows = slice(tok * 128, (tok + 1) * 128)
                            k.dma("sp", dst[rows, :], e, [be], [dst_tb[tok]])
                        for (e, be, tok) in es_:
                            ep_prefetch(c, tok + 4, esrc, esrc_tb, None)
                    deferred.append(fin)
            while deferred:
                deferred.pop(0)()
            drip(1000)

        def mixer0_phase(bi, wts, src, src_tb, dst, dst_tb, first=False):
            w_in, w_out = wts
            bw = wbuf_b[bi]
            ar.reset()
            c = alloc_common(TBM, NEP)
            NB = T // TBM
            BPS = S // TBM
            RING = 6
            qT = ar.alloc((4, TBM), BF16)
            b_qT = [Buf() for _ in range(4)]
            kT = ar.alloc((4, RING * 128), BF16)
            b_kT = [Buf() for _ in range(4)]
            Vaug = ar.alloc((RING, 8, 65), BF16)
            b_V = [Buf() for _ in range(RING)]
            biasT = wbuf[bi][:, 8 * IN_EVEN + 8 * D:8 * IN_EVEN + 8 * D + A_HEADS * 5 * 128].rearrange(
                "p (h j q) -> p h j q", h=A_HEADS, j=5)
            b_bias = Buf()
            probsT = [ar.alloc((5, 128), BF16) for _ in range(2)]
            b_pr = [Buf(), Buf()]
            cbias = ar.alloc((A_HEADS,), F32)
            hc = ar.alloc((4, 30 + TBM), BF16)
            b_hc = [Buf() for _ in range(4)]
            dg = ar.alloc((CONV_W, 128), BF16)
            b_dg = [Buf() for _ in range(CONV_W)]
            cacc = ar.alloc((4, TBM), F32)
            b_cacc = [Buf() for _ in range(4)]
            th = ar.alloc((512,), F32)
            b_th = Buf()
            sig = th[:, 0:TBM]
            b_sig = b_th
            cw = ar.alloc((4, CONV_W), F32)
            cb = ar.alloc((4,), F32)
            cng = ar.alloc((512,), F32)
            cnb = ar.alloc((512,), F32)
            b_cc = Buf()
            rs = [ar.alloc((8,), F32) for _ in range(2)]
            b_rs = [Buf(), Buf()]
            ao = [ar.alloc((512,), BF16) for _ in range(2)]
            b_ao = [Buf(), Buf()]
            cn = [ar.alloc((512,), F32) for _ in range(2)]
            b_cn = [Buf(), Buf()]
            cbf = [ar.alloc((512,), BF16) for _ in range(2)]
            b_cbf = [Buf(), Buf()]
            catT = [ar.alloc((8, 128), BF16) for _ in range(2)]
            b_cat = [[Buf(), Buf()], [Buf(), Buf()]]
            cb2 = ar.alloc((4,), F32)
            stc = ar.alloc((2, 6), F32)
            mvc = ar.alloc((2, 2), F32)
            sdc = ar.alloc((2, 1), F32)
            rsc = ar.alloc((2, 1), F32)
            nmc = ar.alloc((2, 1), F32)
            b_stc = Buf()
            ste = ar.alloc((2, 12), F32)
            mve = ar.alloc((2, 2), F32)
            sde = ar.alloc((2, 1), F32)
            rse = ar.alloc((2, 1), F32)
            nme = ar.alloc((2, 1), F32)
            b_ste = Buf()
            if first:
                load_xbf(c, src, src_tb, 0)
                drip(24)
            load_ln(c, 0)
            k.dma("pool", biasT.rearrange("p h j q -> p (h j q)"), biasT_in, [], [b_bias])
            k.dma("sp", cw.rearrange("p a b -> p (a b)"), convw_in, [], [b_cc])
            k.dma("sp", cb, convb_in, [], [b_cc])
            k.ve("dve", "tensor_scalar", (cb2, cb, 2.0, None), [b_cc], [b_cc], op0=ALU.mult)
            k.dma("sp", cng, convn_in[0:1, :].partition_broadcast(128), [], [b_cc])
            k.dma("sp", cnb, convn_in[1:2, :].partition_broadcast(128), [], [b_cc])
            k.dma("sp", cbias, cbias_in.partition_broadcast(128), [], [b_bias])
            for pr_, bpr_ in zip(probsT, b_pr):
                k.ve("dve", "memset", (pr_[0:64, 0, 64:128], 0.0), [], [bpr_])
            k.ve("dve", "memset", (Vaug.rearrange("p r h e -> p (r h) e")[:, :, 64:65], 1.0), [], b_V)
            if not first:
                load_xbf(c, src, src_tb, 0)
            def stage_A(b):
                bi_ = b % BPS
                transpose_block(c, all_act=True)
                if b + 1 < NB:
                    load_xbf(c, src, src_tb, b + 1)
                drip(3)
                xT, bxT = c.xT, c.b_xT
                slot0 = (2 * bi_) % RING
                if bi_ == 0:
                    k.ve("dve", "memset", (hc[:, :, 0:30], 0.0), [], b_hc)
                else:
                    k.ve("dve", "tensor_copy", (hc[:, :, 0:30], hc[:, :, TBM:TBM + 30]), b_hc, b_hc)
                for ch in range(4):
                    pa, bpa = psum1()
                    pg, bpg = psum1()
                    for kk in range(8):
                        k.mm(pa[:, 0:TBM], w_in[:, kk, 1536 + ch * 128:1536 + (ch + 1) * 128], xT[:, kk, :],
                             kk == 0, kk == 7, [bw[0], bxT[kk]], [bpa])
                    for kk in range(8):
                        k.mm(pg[:, 0:TBM], w_in[:, kk, 2048 + ch * 128:2048 + (ch + 1) * 128], xT[:, kk, :],
                             kk == 0, kk == 7, [bw[0], bxT[kk]], [bpg])
                    k.act(sig, pg[:, 0:TBM], AF.Tanh, [bpg], [b_sig], scale=0.5)
                    k.ve("dve", "scalar_tensor_tensor", (hc[:, ch, 30:30 + TBM], sig, 1.0, pa[:, 0:TBM]),
                         [b_sig, bpa], [b_hc[ch]], op0=ALU.add, op1=ALU.mult)
                    yield

            def conv_gen(b):
                for ch in range(4):
                    for wi in range(CONV_W):
                        k.ve("dve", "tensor_scalar", (dg[:, wi, :], ident_bf[:], cw[:, ch, wi:wi + 1], None),
                             [b_ident, b_cc], [b_dg[wi]], op0=ALU.mult)
                        if wi % 4 == 3:
                            yield
                    yield
                    pcv, bpcv = bank(6 + ch % 2), b_ps[6 + ch % 2]
                    for wi in range(CONV_W):
                        k.mm(pcv[:, 0:TBM], dg[:, wi, :], hc[:, ch, wi:wi + TBM], wi == 0, wi == CONV_W - 1,
                             [b_dg[wi], b_hc[ch]], [bpcv])
                        if wi % 8 == 7:
                            yield
                    k.act(cacc[:, ch, :], pcv[:, 0:TBM], AF.Identity, [bpcv, b_cc], [b_cacc[ch]],
                          bias=cb2[:, ch:ch + 1])
                    yield

            def stage_B1(b):
                bi_ = b % BPS
                xT, bxT = c.xT, c.b_xT
                slot0 = (2 * bi_) % RING
                for ch in range(4):
                    pq, bpq = psum1()
                    for kk in range(8):
                        k.mm(pq[:, 0:TBM], w_in[:, kk, ch * 128:(ch + 1) * 128], xT[:, kk, :], kk == 0, kk == 7,
                             [bw[0], bxT[kk]], [bpq])
                    k.act(qT[:, ch, :], pq[:, 0:TBM], AF.Copy, [bpq], [b_qT[ch]], scale=0.125)
                for ch in range(4):
                    pk, bpk = psum1()
                    for kk in range(8):
                        k.mm(pk[:, 0:TBM], w_in[:, kk, 512 + ch * 128:512 + (ch + 1) * 128], xT[:, kk, :],
                             kk == 0, kk == 7, [bw[0], bxT[kk]], [bpk])
                    k.act(kT[:, ch, slot0 * 128:slot0 * 128 + TBM], pk[:, 0:TBM], AF.Copy, [bpk], [b_kT[ch]])
                for t in range(2):
                    pv, bpv = psum1()
                    for kk in range(8):
                        k.mm(pv, xT[:, kk, t * 128:(t + 1) * 128], w_in[:, kk, 1024:1536], kk == 0, kk == 7,
                             [bw[0], bxT[kk]], [bpv])
                    k.act(Vaug[:, slot0 + t, :, 0:64], pv.rearrange("p (h e) -> p h e", h=8), AF.Copy,
                          [bpv], [b_V[slot0 + t]])

            def att_gen(b):
                bi_ = b % BPS
                for t in range(2):
                    m = 2 * bi_ + t
                    O = bank(4, 2)
                    bO = (b_ps[4], b_ps[5])
                    js = [j for j in range(5) if m - 4 + j >= 0]
                    for h in range(A_HEADS):
                        hcn, hp = h // 2, h % 2
                        prow = slice(hp * 64, (hp + 1) * 64)
                        s1, bs1 = psum1()
                        s2, bsb2 = psum1()
                        for j in js:
                            slot = (m - 4 + j) % RING
                            tgt, btg = (s1[:, j * 128:(j + 1) * 128], bs1) if j < 4 else (s2[:, 0:128], bsb2)
                            far = j < 3
                            k.mm(tgt, kT[prow, hcn, slot * 128:(slot + 1) * 128], qT[prow, hcn, t * 128:(t + 1) * 128],
                                 True, far, [b_kT[hcn], b_qT[hcn]], [btg])
                            if not far:
                                k.mm(tgt, ident_bf[:], biasT[:, h, j, :], False, True, [b_ident, b_bias], [btg])
                        pr, bpr = probsT[h % 2], b_pr[h % 2]
                        cbh = cbias[:, h:h + 1]
                        if 0 in js:
                            k.act(pr[64:128, 0, :], s1[64:128, 0:128], AF.Exp, [bs1, b_bias], [bpr], bias=cbias[64:128, h:h + 1])
                            k.act(pr[0:64, 0, 0:64], s1[0:64, 0:64], AF.Exp, [bs1, b_bias], [bpr], bias=cbias[0:64, h:h + 1])
                        jf = [j for j in js if j in (1, 2)]
                        if jf:
                            k.act(pr[:, jf[0]:3, :], s1[:, jf[0] * 128:384].rearrange("p (j q) -> p j q", q=128), AF.Exp,
                                  [bs1, b_bias], [bpr], bias=cbh)
                        if 3 in js:
                            k.act(pr[:, 3, :], s1[:, 384:512], AF.Exp, [bs1], [bpr])
                        k.act(pr[:, 4, :], s2[:, 0:128], AF.Exp, [bsb2], [bpr])
                        ob = h // 4
                        oc = (h % 4) * 65
                        for j in js:
                            slot = (m - 4 + j) % RING
                            k.mm(O[:, ob * 512 + oc:ob * 512 + oc + 65], pr[:, j, :], Vaug[:, slot, h, :],
                                 j == js[0], j == 4, [bpr, b_V[slot]], [bO[ob]])
                        if h == A_HEADS - 1:
                            for ob2 in range(2):
                                Ov = O[:, ob2 * 512:ob2 * 512 + 260].rearrange("p (h e) -> p h e", h=4)
                                k.ve("dve", "reciprocal", (rs[t][:, ob2 * 4:(ob2 + 1) * 4].unsqueeze(2), Ov[:, :, 64:65]),
                                     [bO[ob2]], [b_rs[t]])
                                k.ve("dve", "tensor_tensor",
                                     (ao[t][:, ob2 * 256:(ob2 + 1) * 256].rearrange("p (h e) -> p h e", h=4),
                                      Ov[:, :, 0:64],
                                      rs[t][:, ob2 * 4:(ob2 + 1) * 4].unsqueeze(2).to_broadcast([128, 4, 64]), ALU.mult),
                                     [bO[ob2], b_rs[t]], [b_ao[t]])
                        yield

            def stage_C(b):
                toks = [b * 2, b * 2 + 1]
                for t in range(2):
                    pc, bpc = psum1()
                    for ch in range(4):
                        k.mm(pc[:, ch * 128:(ch + 1) * 128], cacc[:, ch, t * 128:(t + 1) * 128], ident_f[:],
                             True, True, [b_cacc[ch], b_ident], [bpc])
                    k.act(cn[t], pc, AF.Copy, [bpc], [b_cn[t]])
                    k.ve("dve", "bn_stats", (stc[:, t, :], cn[t]), [b_cn[t]], [b_stc])
                    k.ve("dve", "bn_aggr", (mvc[:, t, :], stc[:, t, :]), [b_stc], [b_stc])
                yield
                for t in range(2):
                    pt, bpt = psum1()
                    for ch in range(4):
                        k.mm(pt[:, ch * 128:(ch + 1) * 128], ao[t][:, ch * 128:(ch + 1) * 128], ident_bf[:],
                             True, True, [b_ao[t], b_ident], [bpt])
                    k.act(catT[t][:, 0:4, :], pt.rearrange("p (c q) -> p c q", c=4), AF.Copy, [bpt], [b_cat[t][0]])
                k.act(sdc, mvc[:, :, 1:2], AF.Sqrt, [b_stc], [b_stc], bias=float(4.0 * LN_EPS))
                k.ve("dve", "reciprocal", (rsc, sdc), [b_stc], [b_stc])
                k.ve("dve", "scalar_tensor_tensor", (nmc, mvc[:, :, 0:1], -1.0, rsc), [b_stc], [b_stc],
                     op0=ALU.mult, op1=ALU.mult)
                yield
                for t in range(2):
                    k.act(cn[t], cn[t], AF.Identity, [b_cn[t], b_stc], [b_cn[t]], bias=nmc[:, t, :], scale=rsc[:, t, :])
                yield
                for t in range(2):
                    k.ve("dve", "tensor_tensor", (cn[t], cn[t], cng, ALU.mult), [b_cn[t], b_cc], [b_cn[t]])
                    k.ve("dve", "tensor_tensor", (cn[t], cn[t], cnb, ALU.add), [b_cn[t], b_cc], [b_cn[t]])
                    yield
                for t in range(2):
                    k.act(th, cn[t], AF.Tanh, [b_cn[t]], [b_th], scale=0.5)
                    k.ve("dve", "scalar_tensor_tensor", (cbf[t], th, 1.0, cn[t]), [b_th, b_cn[t]], [b_cbf[t]],
                         op0=ALU.add, op1=ALU.mult)
                    yield
                Ys = []
                for t in range(2):
                    pt, bpt = psum1()
                    for ch in range(4):
                        k.mm(pt[:, ch * 128:(ch + 1) * 128], cbf[t][:, ch * 128:(ch + 1) * 128], ident_bf[:],
                             True, True, [b_cbf[t], b_ident], [bpt])
                    k.act(catT[t][:, 4:8, :], pt.rearrange("p (c q) -> p c q", c=4), AF.Copy, [bpt], [b_cat[t][1]],
                          scale=0.5)
                    yield
                es_ = []
                for t in range(2):
                    ys = [psum1(), psum1()]
                    for n in range(2):
                        for kc in range(8):
                            k.mm(ys[n][0], catT[t][:, kc, :], w_out[:, kc, n * 512:(n + 1) * 512],
                                 kc == 0, kc == 7, [b_cat[t][kc // 4], bw[1]], [ys[n][1]])
                    i = c.ep_slot.pop(toks[t])
                    e, be = c.ep[i], c.b_ep[i]
                    es_.append((e, be))
                    for n in range(2):
                        sl = slice(n * 512, (n + 1) * 512)
                        k.ve("dve", "scalar_tensor_tensor", (e[:, sl], e[:, sl], float(ALPHA), ys[n][0]),
                             [be, ys[n][1]], [be], op0=ALU.mult, op1=ALU.add)
                    yield
                    for n in range(2):
                        sl = slice(n * 512, (n + 1) * 512)
                        k.ve("dve", "bn_stats", (ste[:, t, n * 6:(n + 1) * 6], e[:, sl]), [be], [b_ste])
                    k.ve("dve", "bn_aggr", (mve[:, t, :], ste[:, t, :]), [b_ste], [b_ste])
                    yield
                k.act(sde, mve[:, :, 1:2], AF.Sqrt, [b_ste], [b_ste], bias=float(LN_EPS))
                k.ve("dve", "reciprocal", (rse, sde), [b_ste], [b_ste])
                k.ve("dve", "scalar_tensor_tensor", (nme, mve[:, :, 0:1], -1.0, rse), [b_ste], [b_ste],
                     op0=ALU.mult, op1=ALU.mult)
                yield
                for t in range(2):
                    e, be = es_[t]
                    k.act(e, e, AF.Identity, [be, b_ste], [be], bias=nme[:, t, :], scale=rse[:, t, :])
                yield
                for t in range(2):
                    e, be = es_[t]
                    k.ve("dve", "tensor_tensor", (e, e, c.lng, ALU.mult), [be, c.b_ln], [be])
                    k.ve("dve", "tensor_tensor", (e, e, c.lnb, ALU.add), [be, c.b_ln], [be])
                    rows = slice(toks[t] * 128, (toks[t] + 1) * 128)
                    k.dma("sp", dst[rows, :], e, [be], [dst_tb[toks[t]]])
                    yield
                for t in range(2):
                    ep_prefetch(c, toks[t] + 2, src, src_tb, None)

            def take(gen, n):
                if gen is None:
                    return None
                for _ in range(n):
                    if next(gen, "end") == "end":
                        return None
                return gen

            ep_prefetch(c, 0, src, src_tb, None)
            ep_prefetch(c, 1, src, src_tb, None)
            for _ in stage_A(0):
                pass
            stage_B1(0)
            cg, ag = conv_gen(0), att_gen(0)
            while cg is not None or ag is not None:
                ag = take(ag, 1)
                cg = take(cg, 8)
            if NB > 1:
                for _ in stage_A(1):
                    pass
                stage_B1(1)
            for b in range(NB):
                cg = ag = None
                if b + 1 < NB:
                    cg, ag = conv_gen(b + 1), att_gen(b + 1)
                for _ in stage_C(b):
                    cg = take(cg, 3)
                    ag = take(ag, 1)
                while ag is not None:
                    ag = take(ag, 1)
                    cg = take(cg, 4)
                while cg is not None:
                    cg = take(cg, 8)
                if b + 2 < NB:
                    for _ in stage_A(b + 2):
                        pass
                    stage_B1(b + 2)
            drip(1000)

        def mixer1_phase(bi, wts, src, src_tb, dst, dst_tb):
            w_in, w_out = wts
            bw = wbuf_b[bi]
            ar.reset()
            c = alloc_common(TBM, NEP)
            NB = T // TBM
            BPS = S // TBM
            NCH = TBM // 64
            zT = ar.alloc((TBM,), F32, parts=16)
            b_zT = Buf()
            gw = ar.alloc((512,), F32, parts=16)
            gb = ar.alloc((4,), F32)
            ngb = ar.alloc((4,), F32)
            hgT = ar.alloc((8,), F32)
            tri = ar.alloc((64,), F32)
            rmask = ar.alloc((TBM,), F32)
            b_const = Buf()
            tA = [ar.alloc((TBM,), F32) for _ in range(2)]
            nb_ = [ar.alloc((TBM,), F32) for _ in range(2)]
            eb = [ar.alloc((TBM,), F32) for _ in range(2)]
            enb = [ar.alloc((TBM,), F32) for _ in range(2)]
            b_tA = [Buf(), Buf()]
            b_nb = [Buf(), Buf()]
            b_eb = [Buf(), Buf()]
            b_enb = [Buf(), Buf()]
            kf = [ar.alloc((TBM,), F32) for _ in range(2)]
            b_kf = [Buf(), Buf()]
            qtT = ar.alloc((4, TBM), BF16)
            b_qt = [Buf() for _ in range(4)]
            ktT = ar.alloc((4, TBM), BF16)
            b_kt = [Buf() for _ in range(4)]
            kdT = ar.alloc((4, TBM), BF16)
            b_kdT = [Buf() for _ in range(4)]
            ebl = ar.alloc((4, NCH), F32)
            b_ebl = [Buf() for _ in range(4)]
            kd = ar.alloc((2, 4, 128), BF16)
            b_kd = [Buf(), Buf()]
            v = ar.alloc((2, D), BF16)
            b_v = [Buf(), Buf()]
            sgg = [ar.alloc((D,), F32) for _ in range(2)]
            b_sgg = [Buf(), Buf()]
            attnT = [ar.alloc((4, 64), BF16) for _ in range(2)]
            b_at = [Buf(), Buf()]
            Sst = ar.alloc((4, 256), F32)
            Sbf = ar.alloc((4, 256), BF16)
            b_S = [Buf() for _ in range(4)]
            b_Sbf = [Buf() for _ in range(4)]
            obf = [ar.alloc((D,), BF16) for _ in range(2)]
            b_obf = [Buf(), Buf()]
            ss = ar.alloc((2, 4), F32)
            lss = ar.alloc((2, 4), F32)
            rinv = ar.alloc((2, 4), F32)
            b_ss = Buf()
            catT = [ar.alloc((8, 128), BF16) for _ in range(2)]
            b_cat = [[Buf(), Buf()], [Buf(), Buf()]]
            ste = ar.alloc((2, 12), F32)
            mve = ar.alloc((2, 2), F32)
            lve = ar.alloc((2, 1), F32)
            rse = ar.alloc((2, 1), F32)
            nme = ar.alloc((2, 1), F32)
            b_ste = Buf()
            load_ln(c, 4)
            k.dma("sp", gw, gatew_in, [], [b_const])
            k.dma("sp", gb, gateb_in, [], [b_const])
            k.dma("sp", hgT, headgT_in, [], [b_const])
            k.dma("sp", tri, tri_in, [], [b_const])
            k.ve("dve", "tensor_scalar", (ngb, gb, -1.0, None), [b_const], [b_const], op0=ALU.mult)
            k.ve("dve", "memset", (rmask, 1.0), [], [b_const])
            k.ve("dve", "memset", (rmask.rearrange("p (c j) -> p c j", j=64)[:, :, 0:1], 0.0), [], [b_const])
            load_xbf(c, src, src_tb, 0)
            DKS = 128 ** -0.5
            O_t = [bank(4, 2), bank(6, 2)]
            bO_t = [(b_ps[4], b_ps[5]), (b_ps[6], b_ps[7])]

            def stage_FE(b):
                bi_ = b % BPS
                transpose_block(c, all_act=True)
                if b + 1 < NB:
                    load_xbf(c, src, src_tb, b + 1)
                drip(3)
                xT, bxT = c.xT, c.b_xT
                yield
                pz, bpz = psum1()
                for kk in range(8):
                    k.mm(pz[0:16, 0:TBM], w_in[:, kk, 3072:3088], xT[:, kk, :], kk == 0, kk == 7,
                         [bw[0], bxT[kk]], [bpz])
                k.act(zT, pz[0:16, 0:TBM], AF.Copy, [bpz], [b_zT])
                yield
                for hh in range(4):
                    r = hh % 2
                    pg, bpg = psum1()
                    k.mm(pg[:, 0:TBM], gw[:, hh * 128:(hh + 1) * 128], zT, True, True, [b_const, b_zT], [bpg])
                    k.act(tA[r], pg[:, 0:TBM], AF.Exp, [bpg, b_const], [b_tA[r]], scale=-1.0, bias=ngb[:, hh:hh + 1])
                    pq, bpq = psum1()
                    for kk in range(8):
                        k.mm(pq[:, 0:TBM], w_in[:, kk, hh * 128:(hh + 1) * 128], xT[:, kk, :], kk == 0, kk == 7,
                             [bw[0], bxT[kk]], [bpq])
                    pk, bpk = psum1()
                    for kk in range(8):
                        k.mm(pk[:, 0:TBM], w_in[:, kk, 512 + hh * 128:512 + (hh + 1) * 128], xT[:, kk, :],
                             kk == 0, kk == 7, [bw[0], bxT[kk]], [bpk])
                    k.act(tA[r], tA[r], AF.Ln, [b_tA[r]], [b_tA[r]], bias=1.0)
                    k.ve("dve", "tensor_tensor_scan", (nb_[r], rmask, tA[r], 0.0, ALU.mult, ALU.add),
                         [b_const, b_tA[r]], [b_nb[r]])
                    k.act(eb[r], nb_[r], AF.Exp, [b_nb[r]], [b_eb[r]], scale=-1.0 / 16.0)
                    k.act(enb[r], nb_[r], AF.Exp, [b_nb[r]], [b_enb[r]], scale=1.0 / 16.0)
                    k.ve("dve", "tensor_copy",
                         (ebl[:, hh, :].unsqueeze(2), eb[r].rearrange("p (c j) -> p c j", j=64)[:, :, 63:64]),
                         [b_eb[r]], [b_ebl[hh]])
                    k.ve("dve", "scalar_tensor_tensor", (qtT[:, hh, :], pq[:, 0:TBM], float(DKS), eb[r]),
                         [bpq, b_eb[r]], [b_qt[hh]], op0=ALU.mult, op1=ALU.mult)
                    k.ve("dve", "tensor_tensor", (kf[r], pk[:, 0:TBM], enb[r], ALU.mult), [bpk, b_enb[r]], [b_kf[r]])
                    k.act(ktT[:, hh, :], kf[r], AF.Copy, [b_kf[r]], [b_kt[hh]])
                    k.ve("dve", "tensor_tensor",
                         (kdT[:, hh, :].rearrange("p (c j) -> p c j", j=64),
                          kf[r].rearrange("p (c j) -> p c j", j=64),
                          ebl[:, hh, :].unsqueeze(2).to_broadcast([128, NCH, 64]), ALU.mult),
                         [b_kf[r], b_ebl[hh]], [b_kdT[hh]])
                    yield
                for t in range(2):
                    pkd, bpkd = psum1()
                    for hh in range(4):
                        k.mm(pkd[:, hh * 128:(hh + 1) * 128], kdT[:, hh, t * 128:(t + 1) * 128], ident_bf[:],
                             True, True, [b_kdT[hh], b_ident], [bpkd])
                    k.act(kd[:, t, :, :], pkd.rearrange("p (h f) -> p h f", h=4), AF.Copy, [bpkd], [b_kd[t]])
                    for n in range(2):
                        pv, bpv = psum1()
                        for kk in range(8):
                            k.mm(pv, xT[:, kk, t * 128:(t + 1) * 128], w_in[:, kk, 1024 + n * 512:1024 + (n + 1) * 512],
                                 kk == 0, kk == 7, [bw[0], bxT[kk]], [bpv])
                        if n == 0:
                            k.act(v[:, t, 0:512], pv, AF.Copy, [bpv], [b_v[t]])
                        else:
                            k.ve("dve", "tensor_copy", (v[:, t, 512:1024], pv), [bpv], [b_v[t]])
                    yield

            def stage_REC(b):
                bi_ = b % BPS
                if bi_ == 0:
                    k.ve("dve", "memset", (Sst, 0.0), [], b_S)
                    k.ve("dve", "memset", (Sbf, 0.0), [], b_Sbf)
                for t in range(2):
                    O, bO = O_t[t], bO_t[t]
                    for cc in range(2):
                        ci = t * 2 + cc
                        rows = slice(cc * 64, (cc + 1) * 64)
                        cols = slice(ci * 64, (ci + 1) * 64)
                        pA, bpA = psum1()
                        for hh in range(4):
                            k.mm(pA[rows, hh * 64:(hh + 1) * 64], ktT[:, hh, cols], qtT[:, hh, cols], True, True,
                                 [b_kt[hh], b_qt[hh]], [bpA])
                        dSs = [psum1(), psum1()]
                        for hh in range(4):
                            oc = slice(hh * 256, (hh + 1) * 256)
                            dsb, bdsb = dSs[hh // 2]
                            k.mm(dsb[:, (hh % 2) * 256:(hh % 2 + 1) * 256], kd[rows, t, hh, :], v[rows, t, oc], True, True,
                                 [b_kd[t], b_v[t]], [bdsb])
                        at, bat = attnT[cc], b_at[cc]
                        k.ve("dve", "tensor_tensor",
                             (at[rows, :, :], pA[rows, 0:256].rearrange("p (h i) -> p h i", h=4),
                              tri[rows, :].unsqueeze(1).to_broadcast([64, 4, 64]), ALU.mult),
                             [bpA, b_const], [bat])
                        for hh in range(4):
                            oc = slice(hh * 256, (hh + 1) * 256)
                            k.mm(O[rows, oc], at[rows, hh, :], v[rows, t, oc], True, False,
                                 [bat, b_v[t]], [bO[hh // 2]])
                            k.mm(O[rows, oc], qtT[:, hh, cols], Sbf[:, hh, :], False, True,
                                 [b_qt[hh], b_Sbf[hh]], [bO[hh // 2]])
                        for hh in range(4):
                            dsb, bdsb = dSs[hh // 2]
                            k.ve("dve", "scalar_tensor_tensor",
                                 (Sst[:, hh, :], Sst[:, hh, :], ebl[:, hh, ci:ci + 1],
                                  dsb[:, (hh % 2) * 256:(hh % 2 + 1) * 256]),
                                 [b_S[hh], b_ebl[hh], bdsb], [b_S[hh]], op0=ALU.mult, op1=ALU.add)
                            k.act(Sbf[:, hh, :], Sst[:, hh, :], AF.Copy, [b_S[hh]], [b_Sbf[hh]])
                        yield

            def stage_G(b):
                xT, bxT = c.xT, c.b_xT
                for t in range(2):
                    for n in range(2):
                        pgg, bpgg = psum1()
                        for kk in range(8):
                            k.mm(pgg, xT[:, kk, t * 128:(t + 1) * 128], w_in[:, kk, 2048 + n * 512:2048 + (n + 1) * 512],
                                 kk == 0, kk == 7, [bw[0], bxT[kk]], [bpgg])
                        k.act(sgg[t][:, n * 512:(n + 1) * 512], pgg, AF.Silu, [bpgg], [b_sgg[t]])
                        yield

            def stage_TAIL(b):
                toks = [b * 2, b * 2 + 1]
                for t in range(2):
                    O, bO = O_t[t], bO_t[t]
                    for hh in range(4):
                        k.act(obf[t][:, hh * 256:(hh + 1) * 256], O[:, hh * 256:(hh + 1) * 256], AF.Square,
                              [bO[hh // 2]], [b_obf[t], b_ss], accum_out=ss[:, t, hh:hh + 1])
                yield
                k.act(lss, ss, AF.Ln, [b_ss], [b_ss], scale=1.0 / 256.0, bias=float(RMS_EPS))
                k.act(rinv, lss, AF.Exp, [b_ss], [b_ss], scale=-0.5)
                yield
                for t in range(2):
                    O, bO = O_t[t], bO_t[t]
                    for hh in range(4):
                        oc = slice(hh * 256, (hh + 1) * 256)
                        k.ve("dve", "scalar_tensor_tensor", (obf[t][:, oc], O[:, oc], rinv[:, t, hh:hh + 1], sgg[t][:, oc]),
                             [bO[hh // 2], b_ss, b_sgg[t]], [b_obf[t]], op0=ALU.mult, op1=ALU.mult)
                    yield
                for t in range(2):
                    for half in range(2):
                        pt, bpt = psum1()
                        for ch in range(4):
                            kc = half * 4 + ch
                            k.mm(pt[:, ch * 128:(ch + 1) * 128], obf[t][:, kc * 128:(kc + 1) * 128], ident_bf[:],
                                 True, True, [b_obf[t], b_ident], [bpt])
                        k.ve("dve", "tensor_tensor",
                             (catT[t][:, half * 4:(half + 1) * 4, :], pt.rearrange("p (c q) -> p c q", c=4),
                              hgT[:, half * 4:(half + 1) * 4].unsqueeze(2).to_broadcast([128, 4, 128]), ALU.mult),
                             [bpt, b_const], [b_cat[t][half]])
                    yield
                es_ = []
                for t in range(2):
                    ys = [psum1(), psum1()]
                    for n in range(2):
                        for kc in range(8):
                            k.mm(ys[n][0], catT[t][:, kc, :], w_out[:, kc, n * 512:(n + 1) * 512],
                                 kc == 0, kc == 7, [b_cat[t][kc // 4], bw[1]], [ys[n][1]])
                    i = c.ep_slot.pop(toks[t])
                    e, be = c.ep[i], c.b_ep[i]
                    es_.append((e, be))
                    for n in range(2):
                        sl = slice(n * 512, (n + 1) * 512)
                        k.ve("dve", "scalar_tensor_tensor", (e[:, sl], e[:, sl], float(ALPHA), ys[n][0]),
                             [be, ys[n][1]], [be], op0=ALU.mult, op1=ALU.add)
                    yield
                    for n in range(2):
                        sl = slice(n * 512, (n + 1) * 512)
                        k.ve("dve", "bn_stats", (ste[:, t, n * 6:(n + 1) * 6], e[:, sl]), [be], [b_ste])
                    k.ve("dve", "bn_aggr", (mve[:, t, :], ste[:, t, :]), [b_ste], [b_ste])
                    yield
                k.act(lve, mve[:, :, 1:2], AF.Ln, [b_ste], [b_ste], bias=float(LN_EPS))
                k.act(rse, lve, AF.Exp, [b_ste], [b_ste], scale=-0.5)
                k.ve("dve", "scalar_tensor_tensor", (nme, mve[:, :, 0:1], -1.0, rse), [b_ste], [b_ste],
                     op0=ALU.mult, op1=ALU.mult)
                yield
                for t in range(2):
                    e, be = es_[t]
                    k.act(e, e, AF.Identity, [be, b_ste], [be], bias=nme[:, t, :], scale=rse[:, t, :])
                yield
                for t in range(2):
                    e, be = es_[t]
                    k.ve("dve", "tensor_tensor", (e, e, c.lng, ALU.mult), [be, c.b_ln], [be])
                    k.ve("dve", "tensor_tensor", (e, e, c.lnb, ALU.add), [be, c.b_ln], [be])
                    rows_ = slice(toks[t] * 128, (toks[t] + 1) * 128)
                    k.dma("sp", dst[rows_, :], e, [be], [dst_tb[toks[t]]])
                    yield
                for t in range(2):
                    ep_prefetch(c, toks[t] + 2, src, src_tb, None)

            def take(gen, n):
                if gen is None:
                    return None
                for _ in range(n):
                    if next(gen, "end") == "end":
                        return None
                return gen

            ep_prefetch(c, 0, src, src_tb, None)
            ep_prefetch(c, 1, src, src_tb, None)
            for _ in stage_FE(0):
                pass
            for b in range(NB):
                gg = stage_G(b)
                for _ in stage_REC(b):
                    gg = take(gg, 1)
                while gg is not None:
                    gg = take(gg, 1)
                fe = stage_FE(b + 1) if b + 1 < NB else None
                for _ in stage_TAIL(b):
                    fe = take(fe, 1)
                while fe is not None:
                    fe = take(fe, 1)
            drip(1000)

        if phases == ("m0", "f0", "m1", "f1"):
            w0, l0 = mixer_weight_loads(0, "even_w_in", IN_EVEN, "even_w_out", first_cols=(1536, 2560))
            w1, l1 = ffn_weight_loads(1, 0, 0)
            pending_loads.extend(l0)
            pending_loads.extend(l1)
            mixer0_phase(0, w0, x_in, xin_tb, xs[0], xs_tb[0], first=True)
            p.barrier()
            w2, l2 = ffn_weight_loads(0, 0, 1)
            pending_loads.extend(l2)
            ffn_phase(0, 0, 1, w1, xs[0], xs_tb[0], None, None, 2)
            p.barrier()
            w3, l3 = mixer_weight_loads(1, "odd_w_in", IN_ODD, "odd_w_out")
            pending_loads.extend(l3)
            ffn_phase(0, 1, 0, w2, xs[0], xs_tb[0], xs[1], xs_tb[1], 2)
            p.barrier()
            w4, l4 = ffn_weight_loads(0, 1, 0)
            pending_loads.extend(l4)
            mixer1_phase(1, w3, xs[1], xs_tb[1], xs[0], xs_tb[0])
            p.barrier()
            w5, l5 = ffn_weight_loads(1, 1, 1)
            pending_loads.extend(l5)
            ffn_phase(1, 0, 0, w4, xs[0], xs_tb[0], None, None, 6)
            p.barrier()
            ffn_phase(1, 1, 1, w5, xs[0], xs_tb[0], out_d, out_tb, 6)
        elif phases == ("f0",):
            wa, la = ffn_weight_loads(0, 0, 0)
            wb, lb = ffn_weight_loads(1, 0, 1)
            pending_loads.extend(la + lb)
            drip(1000)
            ffn_phase(0, 0, 0, wa, x_in, xin_tb, None, None, 2)
            p.barrier()
            ffn_phase(0, 1, 1, wb, x_in, xin_tb, out_d, out_tb, 2)
        elif phases == ("m1",):
            wm, lm = mixer_weight_loads(0, "odd_w_in", IN_ODD, "odd_w_out")
            pending_loads.extend(lm)
            drip(1000)
            mixer1_phase(0, wm, x_in, xin_tb, out_d, out_tb)
        elif phases == ("m0",):
            wm, lm = mixer_weight_loads(0, "even_w_in", IN_EVEN, "even_w_out")
            pending_loads.extend(lm)
            drip(1000)
            mixer0_phase(0, wm, x_in, xin_tb, out_d, out_tb)
        p.finalize_and_emit()
    return nc


def make_lnv(inp):
    return np.ascontiguousarray(np.stack([
        inp["mix_norm_g"][0], inp["mix_norm_b"][0], inp["ffn_norm_g"][0], inp["ffn_norm_b"][0],
        inp["mix_norm_g"][1], inp["mix_norm_b"][1], inp["ffn_norm_g"][1], inp["ffn_norm_b"][1]], 0).astype(np.float32))


def make_biasT(rel_bias):
    rb = np.asarray(rel_bias, dtype=np.float32)
    ki = np.arange(128)[:, None]
    qi = np.arange(128)[None, :]
    out = np.empty((128, A_HEADS, 5, 128), np.float32)
    for j in range(5):
        idx = np.clip(128 * (4 - j) + qi - ki, -128, 128) + 128
        t = rb[:, idx]
        out[:, :, j, :] = np.transpose(t, (1, 0, 2))
    out[0:64, :, 0, 64:128] = NEGM
    out[64:128, :, 4, 0:64] = NEGM
    return np.ascontiguousarray(out.reshape(128, A_HEADS * 5 * 128))


def make_in_maps(inp, nseq=NSEQ, ncores=NCORES):
    x = np.asarray(inp["x"], dtype=np.float32)
    cw = np.asarray(inp["even_conv_w"][0], np.float32)
    common = {
        "even_w_in": np.ascontiguousarray(inp["even_w_in"][0]),
        "even_w_out": np.ascontiguousarray(inp["even_w_out"][0]),
        "odd_w_in": np.ascontiguousarray(inp["odd_w_in"][0]),
        "odd_w_out": np.ascontiguousarray(inp["odd_w_out"][0]),
        "lnv": make_lnv(inp),
        "ident": np.eye(128, dtype=np.float32),
        "biasT": make_biasT(inp["even_rel_bias"][0]),
        "convw": np.ascontiguousarray(cw.T.reshape(4, 128, CONV_W).transpose(1, 0, 2).reshape(128, 4 * CONV_W)),
        "convb": np.ascontiguousarray(np.asarray(inp["even_conv_b"][0], np.float32).reshape(4, 128).T),
        "cbias": np.ascontiguousarray(np.asarray(inp["even_rel_bias"][0], np.float32)[:, 2 * 128].reshape(1, A_HEADS)),
        "convn": np.ascontiguousarray(np.stack([inp["even_conv_norm_g"][0], inp["even_conv_norm_b"][0]], 0)
                                      .astype(np.float32)),
        "gatew": np.ascontiguousarray(np.asarray(inp["odd_gate_w"][0], np.float32)),
        "gateb": np.ascontiguousarray(np.asarray(inp["odd_gate_b"][0], np.float32).reshape(4, 128).T),
        "headg": np.ascontiguousarray(np.asarray(inp["odd_head_norm_g"][0], np.float32).reshape(1, D)),
        "headgT": np.ascontiguousarray(np.asarray(inp["odd_head_norm_g"][0], np.float32).reshape(8, 128).T),
        "tri": np.ascontiguousarray((np.arange(128)[:, None] % 64 <= np.arange(64)[None, :]).astype(np.float32)),
    }
    for l in range(2):
        common["wg%d" % l] = np.ascontiguousarray(inp["ffn_w_gate"][l])
        common["wu%d" % l] = np.ascontiguousarray(inp["ffn_w_up"][l])
        common["wd%d" % l] = np.ascontiguousarray(inp["ffn_w_down"][l])
    maps = []
    for c_ in range(ncores):
        m = dict(common)
        m["x"] = np.ascontiguousarray(x[c_ * nseq:(c_ + 1) * nseq].reshape(nseq * S, D))
        maps.append(m)
    return maps


def kernel(**inputs):
    inp = {k_: np.asarray(v) for k_, v in inputs.items()}
    nc = build()
    in_maps = make_in_maps(inp)
    res = run_bass_kernel_spmd(nc, in_maps, core_ids=list(range(NCORES)))
    outs = [np.asarray(r["out"]).reshape(NSEQ, S, D) for r in res.results]
    return np.concatenate(outs, axis=0).astype(np.float32)
```

```python
import contextlib
import numpy as np
import concourse.bass as bass
import concourse.mybir as mybir
from concourse.bass_utils import run_bass_kernel_spmd

F32 = mybir.dt.float32
BF16 = mybir.dt.bfloat16
AF = mybir.ActivationFunctionType
ALU = mybir.AluOpType

D = 1024
S = 2048
NSEQ = 2
NCORES = 8
DFF = 2816
HF = DFF // 2
NJ = HF // 128
ALPHA = 4 ** 0.25
LN_EPS = 1e-5
RMS_EPS = 1e-6
TB = 512
TBM = 256
WCOLS = 3 * 8 * HF
ACOLS = 38000
A_HEADS = 8
CONV_W = 31
NEGM = -30000.0
IN_EVEN = 2560
IN_ODD = 3088


class Op:
    __slots__ = ("eng", "fn", "deps", "marked", "sem", "val", "dma", "key")


class Buf:
    __slots__ = ("name", "writers", "readers")

    def __init__(self, name=""):
        self.name = name
        self.writers = {}
        self.readers = {}


class Prog:
    ENG = ("pe", "act", "dve", "pool", "sp")
    NDQ = 8

    def __init__(self, nc, es):
        self.nc = nc
        self.streams = {e: [] for e in self.ENG}
        self.sems = {e: es.enter_context(nc.semaphore("s_" + e)) for e in ("pe", "act", "dve", "pool")}
        self.dq = {q: [es.enter_context(nc.semaphore("d_%s%d" % (q, i))) for i in range(self.NDQ)]
                   for q in ("sp", "pool", "act")}
        self.dq_n = {q: 0 for q in self.dq}
        self.dq_cnt = {q: [0] * self.NDQ for q in self.dq}
        self.dq_last = {q: [None] * self.NDQ for q in self.dq}
        self.pending = {}

    def barrier(self):
        lasts = []
        for e in ("pe", "act", "dve", "pool"):
            for o in reversed(self.streams[e]):
                if not o.dma:
                    lasts.append(o)
                    break
        for q in self.dq:
            for last in self.dq_last[q]:
                if last is not None:
                    lasts.append(last)
        self.pending = {e: list(lasts) for e in self.ENG}

    def op(self, eng, fn, reads=(), writes=(), dma=False):
        o = Op()
        o.eng, o.fn, o.deps, o.marked, o.dma = eng, fn, [], False, dma
        o.sem = None
        o.val = 0
        if dma:
            i = self.dq_n[eng]
            self.dq_n[eng] += 1
            slot = i % self.NDQ
            o.key = (eng, slot)
            prev = self.dq_last[eng][slot]
            if prev is not None:
                o.deps.append(prev)
            o.sem = self.dq[eng][slot]
            self.dq_cnt[eng][slot] += 16
            o.val = self.dq_cnt[eng][slot]
            self.dq_last[eng][slot] = o
        else:
            o.key = eng
        seen = set()

        def add(d, raw):
            if d is o or id(d) in seen:
                return
            if not dma and not d.dma and d.eng == eng:
                if eng == "pe" or not raw:
                    return
            seen.add(id(d))
            o.deps.append(d)

        pend = self.pending.pop(eng, None)
        if pend:
            for d in pend:
                add(d, False)
        for b in reads:
            for d in b.writers.values():
                add(d, True)
        for b in writes:
            for d in b.writers.values():
                add(d, False)
            for d in b.readers.values():
                add(d, False)
        for b in reads:
            b.readers[o.key] = o
        for b in writes:
            if b.readers:
                b.readers = {}
                b.writers = {}
            b.writers[o.key] = o
        self.streams[eng].append(o)
        return o

    def finalize_and_emit(self):
        nc = self.nc
        for e in self.ENG:
            for o in self.streams[e]:
                for d in o.deps:
                    if not d.dma:
                        d.marked = True
        for e in ("pe", "act", "dve", "pool"):
            c = 0
            for o in self.streams[e]:
                if not o.dma and o.marked:
                    c += 1
                    o.sem = self.sems[e]
                    o.val = c
        streams = self.streams

        def emit(ename, eng):
            waited = {}
            for o in streams[ename]:
                need = {}
                for d in o.deps:
                    k = id(d.sem)
                    if waited.get(k, 0) >= d.val:
                        continue
                    if k not in need or need[k][1] < d.val:
                        need[k] = (d.sem, d.val)
                for k, (sem, val) in need.items():
                    eng.wait_ge(sem, val)
                    waited[k] = val
                ins = o.fn(eng)
                if o.dma:
                    ins.then_inc(o.sem, 16)
                elif o.marked:
                    ins.then_inc(o.sem, 1)
            if ename in self.dq:
                for slot in range(self.NDQ):
                    last = self.dq_last[ename][slot]
                    if last is not None and waited.get(id(last.sem), 0) < last.val:
                        eng.wait_ge(last.sem, last.val)

        with nc.Block() as block:
            @block.tensor
            def _(e):
                emit("pe", e)

            @block.scalar
            def _(e):
                emit("act", e)

            @block.vector
            def _(e):
                emit("dve", e)

            @block.gpsimd
            def _(e):
                emit("pool", e)

            @block.sync
            def _(e):
                emit("sp", e)


class K:
    def __init__(self, nc, es):
        self.nc = nc
        self.p = Prog(nc, es)
        self.nbuf = 0

    def sb(self, name, shape, dt):
        return self.nc.alloc_sbuf_tensor(name, list(shape), dt)

    def mm(self, out, lhsT, rhs, start, stop, reads, writes):
        return self.p.op("pe", lambda e: e.matmul(out, lhsT, rhs, start=start, stop=stop), reads, writes)

    def act(self, out, in_, func, reads, writes, bias=None, scale=None, accum_out=None):
        kw = {}
        if bias is not None:
            kw["bias"] = bias
        if scale is not None:
            kw["scale"] = scale
        if accum_out is not None:
            kw["accum_out"] = accum_out
        return self.p.op("act", lambda e: e.activation(out, in_, func, **kw), reads, writes)

    def ve(self, eng, name, args, reads, writes, **kw):
        return self.p.op(eng, lambda e: getattr(e, name)(*args, **kw), reads, writes)

    def dma(self, q, out, in_, reads, writes):
        return self.p.op(q, lambda e: e.dma_start(out=out, in_=in_), reads, writes, dma=True)


class Arena:
    def __init__(self, t, ncols):
        self.t = t
        self.n = ncols
        self.off = 0

    def reset(self):
        self.off = 0

    def alloc(self, shape, dt, parts=128):
        n = 1
        for s_ in shape:
            n *= s_
        cols = n * (2 if dt == F32 else 1)
        cols = (cols + 15) // 16 * 16
        assert self.off + cols <= self.n, ("arena overflow", self.off, cols, self.n)
        ap = self.t[0:parts, self.off:self.off + cols]
        self.off += cols
        if dt == F32:
            ap = ap.bitcast(F32)
        ap = ap[:, 0:n]
        if len(shape) == 2:
            ap = ap.rearrange("p (a b) -> p a b", a=shape[0])
        elif len(shape) == 3:
            ap = ap.rearrange("p (a b c) -> p a b c", a=shape[0], b=shape[1])
        return ap


def build(cfg=None):
    cfg = cfg or {}
    nseq = cfg.get("nseq", NSEQ)
    T = nseq * S
    NT = T // 128
    phases = cfg.get("phases", ("m0", "f0", "m1", "f1"))
    nc = bass.Bass("TRN2", target_bir_lowering=False)

    def din(name, shape):
        return nc.dram_tensor(name, list(shape), F32, kind="ExternalInput").ap()

    x_in = din("x", (T, D))
    w = {}
    w["even_w_in"] = din("even_w_in", (D, IN_EVEN))
    w["even_w_out"] = din("even_w_out", (D, D))
    w["odd_w_in"] = din("odd_w_in", (D, IN_ODD))
    w["odd_w_out"] = din("odd_w_out", (D, D))
    for l in range(2):
        w["wg%d" % l] = din("wg%d" % l, (D, DFF))
        w["wu%d" % l] = din("wu%d" % l, (D, DFF))
        w["wd%d" % l] = din("wd%d" % l, (DFF, D))
    lnv = din("lnv", (8, D))
    ident_in = din("ident", (128, 128))
    biasT_in = din("biasT", (128, A_HEADS * 5 * 128))
    convw_in = din("convw", (128, 4 * CONV_W))
    convb_in = din("convb", (128, 4))
    convn_in = din("convn", (2, 512))
    cbias_in = din("cbias", (1, A_HEADS))
    gatew_in = din("gatew", (16, 512))
    gateb_in = din("gateb", (128, 4))
    headg_in = din("headg", (1, D))
    headgT_in = din("headgT", (128, 8))
    tri_in = din("tri", (128, 64))
    out_d = nc.dram_tensor("out", [T, D], F32, kind="ExternalOutput").ap()
    xs = [nc.dram_tensor("xs%d" % i, [T, D], F32, kind="Internal").ap() for i in range(2)]
    ya = nc.dram_tensor("ya", [T, D], F32, kind="Internal").ap()
    xs_tb = [[Buf() for _ in range(NT)] for _ in range(2)]
    ya_tb = [Buf() for _ in range(NT)]
    xin_tb = [Buf() for _ in range(NT)]
    out_tb = [Buf() for _ in range(NT)]

    es = contextlib.ExitStack()
    with es:
        k = K(nc, es)
        p = k.p
        wbuf = [k.sb("wbuf%d" % i, (128, WCOLS), BF16) for i in range(2)]
        wbuf_b = [[Buf("wb%d_%d" % (i, j)) for j in range(4)] for i in range(2)]
        ident_bf = k.sb("ident_bf", (128, 128), BF16)
        ident_f = k.sb("ident_f", (128, 128), F32)
        b_ident = Buf("ident")
        NEP = 2
        stt = [k.sb("stt%d" % i, (128, 16), F32) for i in range(4)]
        b_stt = [Buf() for _ in range(4)]
        arena_t = k.sb("arena", (128, ACOLS), BF16)
        ar = Arena(arena_t, ACOLS)
        ps_all = nc.alloc_psum_tensor("ps_all", [128, 4096], F32)
        b_ps = [Buf("ps%d" % i) for i in range(8)]

        def bank(i, n=1):
            return ps_all[:, i * 512:(i + n) * 512]

        ps_rr = [0]

        def psum1():
            i = ps_rr[0] % 4
            ps_rr[0] += 1
            return bank(i), b_ps[i]

        k.dma("pool", ident_bf[:], ident_in, [], [b_ident])
        k.dma("sp", ident_f[:], ident_in, [], [b_ident])

        def wview(bi, off, kk, n):
            return wbuf[bi][:, off:off + kk * n].rearrange("p (k n) -> p k n", k=kk)

        def ffn_weight_loads(bi, l, hf):
            wg = wview(bi, 0, 8, HF)
            wu = wview(bi, 8 * HF, 8, HF)
            wd = wview(bi, 16 * HF, NJ, D)
            loads = []
            for kk in range(8):
                loads.append(lambda kk=kk: k.dma(
                    "pool", wg[:, kk, :], w["wg%d" % l][kk * 128:(kk + 1) * 128, hf * HF:(hf + 1) * HF],
                    [], [wbuf_b[bi][0]]))
                loads.append(lambda kk=kk: k.dma(
                    "pool", wu[:, kk, :], w["wu%d" % l][kk * 128:(kk + 1) * 128, hf * HF:(hf + 1) * HF],
                    [], [wbuf_b[bi][1]]))
            for j in range(NJ):
                r0 = hf * HF + j * 128
                loads.append(lambda j=j, r0=r0: k.dma("pool", wd[:, j, :], w["wd%d" % l][r0:r0 + 128, :],
                                                     [], [wbuf_b[bi][2]]))
            return (wg, wu, wd), loads

        def mixer_weight_loads(bi, name_in, nin, name_out, first_cols=None):
            w_in = wview(bi, 0, 8, nin)
            w_out = wview(bi, 8 * nin, 8, D)
            loads = []
            if first_cols is not None:
                c0, c1 = first_cols
                for kk in range(8):
                    loads.append(lambda kk=kk: k.dma("pool", w_in[:, kk, c0:c1], w[name_in][kk * 128:(kk + 1) * 128, c0:c1],
                                                     [], [wbuf_b[bi][0]]))
                for kk in range(8):
                    loads.append(lambda kk=kk: k.dma("pool", w_in[:, kk, 0:c0], w[name_in][kk * 128:(kk + 1) * 128, 0:c0],
                                                     [], [wbuf_b[bi][0]]))
                    if c1 < nin:
                        loads.append(lambda kk=kk: k.dma("pool", w_in[:, kk, c1:nin],
                                                         w[name_in][kk * 128:(kk + 1) * 128, c1:nin],
                                                         [], [wbuf_b[bi][0]]))
            else:
                for kk in range(8):
                    loads.append(lambda kk=kk: k.dma("pool", w_in[:, kk, :], w[name_in][kk * 128:(kk + 1) * 128, :],
                                                     [], [wbuf_b[bi][0]]))
            for kk in range(8):
                loads.append(lambda kk=kk: k.dma("pool", w_out[:, kk, :], w[name_out][kk * 128:(kk + 1) * 128, :],
                                                 [], [wbuf_b[bi][1]]))
            return (w_in, w_out), loads

        pending_loads = []

        def drip(n):
            for _ in range(n):
                if pending_loads:
                    pending_loads.pop(0)()

        class Ctx:
            pass

        def alloc_common(tb, nep):
            c = Ctx()
            c.tb = tb
            c.ntl = tb // 128
            c.lng = ar.alloc((D,), F32)
            c.lnb = ar.alloc((D,), F32)
            c.b_ln = Buf("ln")
            c.xbf = ar.alloc((c.ntl, D), BF16)
            c.b_xbf = Buf("xbf")
            c.xT = ar.alloc((8, tb), BF16)
            c.b_xT = [Buf() for _ in range(8)]
            c.nep = nep
            c.ep = [ar.alloc((D,), F32) for _ in range(nep)]
            c.b_ep = [Buf() for _ in range(nep)]
            c.ep_n = 0
            c.ep_slot = {}
            return c

        def load_ln(c, row):
            k.dma("sp", c.lng, lnv[row:row + 1, :].partition_broadcast(128), [], [c.b_ln])
            k.dma("sp", c.lnb, lnv[row + 1:row + 2, :].partition_broadcast(128), [], [c.b_ln])

        def load_xbf(c, src, src_tb, b):
            v = src[b * c.tb:(b + 1) * c.tb, :].rearrange("(t p) d -> p t d", p=128)
            k.dma("pool", c.xbf, v, [src_tb[b * c.ntl + t] for t in range(c.ntl)], [c.b_xbf])

        def transpose_block(c, all_act=False):
            ntl = c.ntl
            for kk in range(8):
                pt, bpt = psum1()
                for t in range(ntl):
                    k.mm(pt[:, t * 128:(t + 1) * 128], c.xbf[:, t, kk * 128:(kk + 1) * 128], ident_bf[:],
                         True, True, [c.b_xbf, b_ident], [bpt])
                if all_act or kk % 2 == 0:
                    k.act(c.xT[:, kk, :], pt[:, 0:c.tb], AF.Copy, [bpt], [c.b_xT[kk]])
                else:
                    k.ve("dve", "tensor_copy", (c.xT[:, kk, :], pt[:, 0:c.tb]), [bpt], [c.b_xT[kk]])

        def ep_prefetch(c, tok, src, src_tb, ex):
            if tok >= NT or tok in c.ep_slot:
                return
            i = c.ep_n % c.nep
            c.ep_n += 1
            c.ep_slot[tok] = i
            rows = slice(tok * 128, (tok + 1) * 128)
            k.dma("sp", c.ep[i], src[rows, :], [src_tb[tok]], [c.b_ep[i]])
            if ex is not None:
                yat, b_yat = ex
                k.dma("sp", yat[i], ya[rows, :], [ya_tb[tok]], [b_yat[i]])

        def ln_epilogue(c, tok, Y, bY, src, src_tb, ex, dst, dst_tb):
            ep_prefetch(c, tok, src, src_tb, ex)
            i = c.ep_slot.pop(tok)
            ep_prefetch(c, tok + 1, src, src_tb, ex)
            e, be = c.ep[i], c.b_ep[i]
            st, bs = stt[i], b_stt[i]
            rows = slice(tok * 128, (tok + 1) * 128)
            for n in range(2):
                sl = slice(n * 512, (n + 1) * 512)
                k.ve("dve", "scalar_tensor_tensor", (e[:, sl], e[:, sl], float(ALPHA), Y[:, sl]),
                     [be, bY[n]], [be], op0=ALU.mult, op1=ALU.add)
            if ex is not None:
                yat, b_yat = ex
                k.ve("dve", "tensor_tensor", (e, e, yat[i], ALU.add), [be, b_yat[i]], [be])
            for n in range(2):
                sl = slice(n * 512, (n + 1) * 512)
                k.ve("dve", "bn_stats", (st[:, n * 6:(n + 1) * 6], e[:, sl]), [be], [bs])
            k.ve("dve", "bn_aggr", (st[:, 12:14], st[:, 0:12]), [bs], [bs])
            k.act(st[:, 11:12], st[:, 13:14], AF.Sqrt, [bs], [bs], bias=float(LN_EPS))
            k.ve("dve", "reciprocal", (st[:, 14:15], st[:, 11:12]), [bs], [bs])
            k.ve("dve", "scalar_tensor_tensor", (st[:, 15:16], st[:, 12:13], -1.0, st[:, 14:15]),
                 [bs], [bs], op0=ALU.mult, op1=ALU.mult)
            k.act(e, e, AF.Identity, [be, bs], [be], bias=st[:, 15:16], scale=st[:, 14:15])
            k.ve("dve", "tensor_tensor", (e, e, c.lng, ALU.mult), [be, c.b_ln], [be])
            k.ve("dve", "tensor_tensor", (e, e, c.lnb, ALU.add), [be, c.b_ln], [be])
            k.dma("sp", dst[rows, :], e, [be], [dst_tb[tok]])

        def ffn_phase(l, hf, bi, wts, src, src_tb, dst, dst_tb, lnrow):
            wg, wu, wd = wts
            bw = wbuf_b[bi]
            ar.reset()
            nep = 2 if hf == 0 else 4
            c = alloc_common(TB, nep)
            NB = T // TB
            hT = ar.alloc((NJ, TB), BF16)
            b_hT = [Buf() for _ in range(NJ)]
            sg = [ar.alloc((TB,), F32) for _ in range(2)]
            b_sg = [Buf(), Buf()]
            ste = ar.alloc((4, 12), F32)
            mve = ar.alloc((4, 2), F32)
            sde = ar.alloc((4, 1), F32)
            rse = ar.alloc((4, 1), F32)
            nme = ar.alloc((4, 1), F32)
            b_ste = Buf()
            if hf == 1:
                load_ln(c, lnrow)
            esrc, esrc_tb = (src, src_tb) if hf == 0 else (ya, ya_tb)
            load_xbf(c, src, src_tb, 0)
            if hf == 0:
                ep_prefetch(c, 0, esrc, esrc_tb, None)
            else:
                for t in range(4):
                    ep_prefetch(c, t, esrc, esrc_tb, None)
            yn = 0
            deferred = []
            transpose_block(c, all_act=True)
            if NB > 1:
                load_xbf(c, src, src_tb, 1)
            for b in range(NB):
                drip(4)
                for j in range(NJ):
                    pg, bpg = psum1()
                    pu, bpu = psum1()
                    for kk in range(8):
                        k.mm(pg, wg[:, kk, j * 128:(j + 1) * 128], c.xT[:, kk, :], kk == 0, kk == 7,
                             [bw[0], c.b_xT[kk]], [bpg])
                    for kk in range(8):
                        k.mm(pu, wu[:, kk, j * 128:(j + 1) * 128], c.xT[:, kk, :], kk == 0, kk == 7,
                             [bw[1], c.b_xT[kk]], [bpu])
                    si = j % 2
                    k.act(sg[si], pg, AF.Silu, [bpg], [b_sg[si]])
                    k.ve("dve", "tensor_tensor", (hT[:, j, :], sg[si], pu, ALU.mult),
                         [b_sg[si], bpu], [b_hT[j]])
                    if j == 3 and deferred:
                        deferred.pop(0)()
                es_ = []
                for t in range(4):
                    tok = b * 4 + t
                    yb = 4 + 2 * (yn % 2)
                    yn += 1
                    Y = bank(yb, 2)
                    bY = (b_ps[yb], b_ps[yb + 1])
                    for n in range(2):
                        for j in range(NJ):
                            k.mm(Y[:, n * 512:(n + 1) * 512], hT[:, j, t * 128:(t + 1) * 128],
                                 wd[:, j, n * 512:(n + 1) * 512], j == 0, j == NJ - 1, [b_hT[j], bw[2]], [bY[n]])
                    rows = slice(tok * 128, (tok + 1) * 128)
                    i = c.ep_slot.pop(tok)
                    e, be = c.ep[i], c.b_ep[i]
                    if hf == 0:
                        ep_prefetch(c, tok + 1, esrc, esrc_tb, None)
                        for n in range(2):
                            sl = slice(n * 512, (n + 1) * 512)
                            k.ve("dve", "scalar_tensor_tensor", (e[:, sl], e[:, sl], float(ALPHA), Y[:, sl]),
                                 [be, bY[n]], [be], op0=ALU.mult, op1=ALU.add)
                        k.dma("sp", ya[rows, :], e, [be], [ya_tb[tok]])
                    else:
                        for n in range(2):
                            sl = slice(n * 512, (n + 1) * 512)
                            k.ve("dve", "tensor_tensor", (e[:, sl], e[:, sl], Y[:, sl], ALU.add),
                                 [be, bY[n]], [be])
                        for n in range(2):
                            sl = slice(n * 512, (n + 1) * 512)
                            k.ve("dve", "bn_stats", (ste[:, t, n * 6:(n + 1) * 6], e[:, sl]), [be], [b_ste])
                        k.ve("dve", "bn_aggr", (mve[:, t, :], ste[:, t, :]), [b_ste], [b_ste])
                        es_.append((e, be, tok))
                if b + 1 < NB:
                    transpose_block(c, all_act=True)
                    if b + 2 < NB:
                        load_xbf(c, src, src_tb, b + 2)
                if hf == 1:
                    k.act(sde, mve[:, :, 1:2], AF.Sqrt, [b_ste], [b_ste], bias=float(LN_EPS))
                    k.ve("dve", "reciprocal", (rse, sde), [b_ste], [b_ste])
                    k.ve("dve", "scalar_tensor_tensor", (nme, mve[:, :, 0:1], -1.0, rse), [b_ste], [b_ste],
                         op0=ALU.mult, op1=ALU.mult)
                    for t in range(4):
                        e, be, tok = es_[t]
                        k.act(e, e, AF.Identity, [be, b_ste], [be], bias=nme[:, t, :], scale=rse[:, t, :])

                    def fin(es_=es_, b=b):
                        for (e, be, tok) in es_:
                            k.ve("dve", "tensor_tensor", (e, e, c.lng, ALU.mult), [be, c.b_ln], [be])
                            k.ve("dve", "tensor_tensor", (e, e, c.lnb, ALU.add), [be, c.b_ln], [be])
                            rows = slice(tok * 128, (tok + 1) * 128)
                            k.dma("sp", dst[rows, :], e, [be], [dst_tb[tok]])
                        for (e, be, tok) in es_:
                            ep_prefetch(c, tok + 4, esrc, esrc_tb, None)
                    deferred.append(fin)
            while deferred:
                deferred.pop(0)()
            drip(1000)

        def mixer0_phase(bi, wts, src, src_tb, dst, dst_tb, first=False):
            w_in, w_out = wts
            bw = wbuf_b[bi]
            ar.reset()
            c = alloc_common(TBM, NEP)
            NB = T // TBM
            BPS = S // TBM
            RING = 6
            qT = ar.alloc((4, TBM), BF16)
            b_qT = [Buf() for _ in range(4)]
            kT = ar.alloc((4, RING * 128), BF16)
            b_kT = [Buf() for _ in range(4)]
            Vaug = ar.alloc((RING, 8, 65), BF16)
            b_V = [Buf() for _ in range(RING)]
            biasT = wbuf[bi][:, 8 * IN_EVEN + 8 * D:8 * IN_EVEN + 8 * D + A_HEADS * 5 * 128].rearrange(
                "p (h j q) -> p h j q", h=A_HEADS, j=5)
            b_bias = Buf()
            probsT = [ar.alloc((5, 128), BF16) for _ in range(2)]
            b_pr = [Buf(), Buf()]
            cbias = ar.alloc((A_HEADS,), F32)
            hc = ar.alloc((4, 30 + TBM), BF16)
            b_hc = [Buf() for _ in range(4)]
            dg = ar.alloc((CONV_W, 128), BF16)
            b_dg = [Buf() for _ in range(CONV_W)]
            cacc = ar.alloc((4, TBM), F32)
            b_cacc = [Buf() for _ in range(4)]
            th = ar.alloc((512,), F32)
            b_th = Buf()
            sig = th[:, 0:TBM]
            b_sig = b_th
            cw = ar.alloc((4, CONV_W), F32)
            cb = ar.alloc((4,), F32)
            cng = ar.alloc((512,), F32)
            cnb = ar.alloc((512,), F32)
            b_cc = Buf()
            rs = [ar.alloc((8,), F32) for _ in range(2)]
            b_rs = [Buf(), Buf()]
            ao = [ar.alloc((512,), BF16) for _ in range(2)]
            b_ao = [Buf(), Buf()]
            cn = [ar.alloc((512,), F32) for _ in range(2)]
            b_cn = [Buf(), Buf()]
            cbf = [ar.alloc((512,), BF16) for _ in range(2)]
            b_cbf = [Buf(), Buf()]
            catT = [ar.alloc((8, 128), BF16) for _ in range(2)]
            b_cat = [[Buf(), Buf()], [Buf(), Buf()]]
            cb2 = ar.alloc((4,), F32)
            stc = ar.alloc((2, 6), F32)
            mvc = ar.alloc((2, 2), F32)
            sdc = ar.alloc((2, 1), F32)
            rsc = ar.alloc((2, 1), F32)
            nmc = ar.alloc((2, 1), F32)
            b_stc = Buf()
            ste = ar.alloc((2, 12), F32)
            mve = ar.alloc((2, 2), F32)
            sde = ar.alloc((2, 1), F32)
            rse = ar.alloc((2, 1), F32)
            nme = ar.alloc((2, 1), F32)
            b_ste = Buf()
            if first:
                load_xbf(c, src, src_tb, 0)
                drip(24)
            load_ln(c, 0)
            k.dma("pool", biasT.rearrange("p h j q -> p (h j q)"), biasT_in, [], [b_bias])
            k.dma("sp", cw.rearrange("p a b -> p (a b)"), convw_in, [], [b_cc])
            k.dma("sp", cb, convb_in, [], [b_cc])
            k.ve("dve", "tensor_scalar", (cb2, cb, 2.0, None), [b_cc], [b_cc], op0=ALU.mult)
            k.dma("sp", cng, convn_in[0:1, :].partition_broadcast(128), [], [b_cc])
            k.dma("sp", cnb, convn_in[1:2, :].partition_broadcast(128), [], [b_cc])
            k.dma("sp", cbias, cbias_in.partition_broadcast(128), [], [b_bias])
            for pr_, bpr_ in zip(probsT, b_pr):
                k.ve("dve", "memset", (pr_[0:64, 0, 64:128], 0.0), [], [bpr_])
            k.ve("dve", "memset", (Vaug.rearrange("p r h e -> p (r h) e")[:, :, 64:65], 1.0), [], b_V)
            if not first:
                load_xbf(c, src, src_tb, 0)
            def stage_A(b):
                bi_ = b % BPS
                transpose_block(c, all_act=True)
                if b + 1 < NB:
                    load_xbf(c, src, src_tb, b + 1)
                drip(3)
                xT, bxT = c.xT, c.b_xT
                slot0 = (2 * bi_) % RING
                if bi_ == 0:
                    k.ve("dve", "memset", (hc[:, :, 0:30], 0.0), [], b_hc)
                else:
                    k.ve("dve", "tensor_copy", (hc[:, :, 0:30], hc[:, :, TBM:TBM + 30]), b_hc, b_hc)
                for ch in range(4):
                    pa, bpa = psum1()
                    pg, bpg = psum1()
                    for kk in range(8):
                        k.mm(pa[:, 0:TBM], w_in[:, kk, 1536 + ch * 128:1536 + (ch + 1) * 128], xT[:, kk, :],
                             kk == 0, kk == 7, [bw[0], bxT[kk]], [bpa])
                    for kk in range(8):
                        k.mm(pg[:, 0:TBM], w_in[:, kk, 2048 + ch * 128:2048 + (ch + 1) * 128], xT[:, kk, :],
                             kk == 0, kk == 7, [bw[0], bxT[kk]], [bpg])
                    k.act(sig, pg[:, 0:TBM], AF.Tanh, [bpg], [b_sig], scale=0.5)
                    k.ve("dve", "scalar_tensor_tensor", (hc[:, ch, 30:30 + TBM], sig, 1.0, pa[:, 0:TBM]),
                         [b_sig, bpa], [b_hc[ch]], op0=ALU.add, op1=ALU.mult)
                    yield

            def conv_gen(b):
                for ch in range(4):
                    for wi in range(CONV_W):
                        k.ve("dve", "tensor_scalar", (dg[:, wi, :], ident_bf[:], cw[:, ch, wi:wi + 1], None),
                             [b_ident, b_cc], [b_dg[wi]], op0=ALU.mult)
                        if wi % 4 == 3:
                            yield
                    yield
                    pcv, bpcv = bank(6 + ch % 2), b_ps[6 + ch % 2]
                    for wi in range(CONV_W):
                        k.mm(pcv[:, 0:TBM], dg[:, wi, :], hc[:, ch, wi:wi + TBM], wi == 0, wi == CONV_W - 1,
                             [b_dg[wi], b_hc[ch]], [bpcv])
                        if wi % 8 == 7:
                            yield
                    k.act(cacc[:, ch, :], pcv[:, 0:TBM], AF.Identity, [bpcv, b_cc], [b_cacc[ch]],
                          bias=cb2[:, ch:ch + 1])
                    yield

            def stage_B1(b):
                bi_ = b % BPS
                xT, bxT = c.xT, c.b_xT
                slot0 = (2 * bi_) % RING
                for ch in range(4):
                    pq, bpq = psum1()
                    for kk in range(8):
                        k.mm(pq[:, 0:TBM], w_in[:, kk, ch * 128:(ch + 1) * 128], xT[:, kk, :], kk == 0, kk == 7,
                             [bw[0], bxT[kk]], [bpq])
                    k.act(qT[:, ch, :], pq[:, 0:TBM], AF.Copy, [bpq], [b_qT[ch]], scale=0.125)
                for ch in range(4):
                    pk, bpk = psum1()
                    for kk in range(8):
                        k.mm(pk[:, 0:TBM], w_in[:, kk, 512 + ch * 128:512 + (ch + 1) * 128], xT[:, kk, :],
                             kk == 0, kk == 7, [bw[0], bxT[kk]], [bpk])
                    k.act(kT[:, ch, slot0 * 128:slot0 * 128 + TBM], pk[:, 0:TBM], AF.Copy, [bpk], [b_kT[ch]])
                for t in range(2):
                    pv, bpv = psum1()
                    for kk in range(8):
                        k.mm(pv, xT[:, kk, t * 128:(t + 1) * 128], w_in[:, kk, 1024:1536], kk == 0, kk == 7,
                             [bw[0], bxT[kk]], [bpv])
                    k.act(Vaug[:, slot0 + t, :, 0:64], pv.rearrange("p (h e) -> p h e", h=8), AF.Copy,
                          [bpv], [b_V[slot0 + t]])

            def att_gen(b):
                bi_ = b % BPS
                for t in range(2):
                    m = 2 * bi_ + t
                    O = bank(4, 2)
                    bO = (b_ps[4], b_ps[5])
                    js = [j for j in range(5) if m - 4 + j >= 0]
                    for h in range(A_HEADS):
                        hcn, hp = h // 2, h % 2
                        prow = slice(hp * 64, (hp + 1) * 64)
                        s1, bs1 = psum1()
                        s2, bsb2 = psum1()
                        for j in js:
                            slot = (m - 4 + j) % RING
                            tgt, btg = (s1[:, j * 128:(j + 1) * 128], bs1) if j < 4 else (s2[:, 0:128], bsb2)
                            far = j < 3
                            k.mm(tgt, kT[prow, hcn, slot * 128:(slot + 1) * 128], qT[prow, hcn, t * 128:(t + 1) * 128],
                                 True, far, [b_kT[hcn], b_qT[hcn]], [btg])
                            if not far:
                                k.mm(tgt, ident_bf[:], biasT[:, h, j, :], False, True, [b_ident, b_bias], [btg])
                        pr, bpr = probsT[h % 2], b_pr[h % 2]
                        cbh = cbias[:, h:h + 1]
                        if 0 in js:
                            k.act(pr[64:128, 0, :], s1[64:128, 0:128], AF.Exp, [bs1, b_bias], [bpr], bias=cbias[64:128, h:h + 1])
                            k.act(pr[0:64, 0, 0:64], s1[0:64, 0:64], AF.Exp, [bs1, b_bias], [bpr], bias=cbias[0:64, h:h + 1])
                        jf = [j for j in js if j in (1, 2)]
                        if jf:
                            k.act(pr[:, jf[0]:3, :], s1[:, jf[0] * 128:384].rearrange("p (j q) -> p j q", q=128), AF.Exp,
                                  [bs1, b_bias], [bpr], bias=cbh)
                        if 3 in js:
                            k.act(pr[:, 3, :], s1[:, 384:512], AF.Exp, [bs1], [bpr])
                        k.act(pr[:, 4, :], s2[:, 0:128], AF.Exp, [bsb2], [bpr])
                        ob = h // 4
                        oc = (h % 4) * 65
                        for j in js:
                            slot = (m - 4 + j) % RING
                            k.mm(O[:, ob * 512 + oc:ob * 512 + oc + 65], pr[:, j, :], Vaug[:, slot, h, :],
                                 j == js[0], j == 4, [bpr, b_V[slot]], [bO[ob]])
                        if h == A_HEADS - 1:
                            for ob2 in range(2):
                                Ov = O[:, ob2 * 512:ob2 * 512 + 260].rearrange("p (h e) -> p h e", h=4)
                                k.ve("dve", "reciprocal", (rs[t][:, ob2 * 4:(ob2 + 1) * 4].unsqueeze(2), Ov[:, :, 64:65]),
                                     [bO[ob2]], [b_rs[t]])
                                k.ve("dve", "tensor_tensor",
                                     (ao[t][:, ob2 * 256:(ob2 + 1) * 256].rearrange("p (h e) -> p h e", h=4),
                                      Ov[:, :, 0:64],
                                      rs[t][:, ob2 * 4:(ob2 + 1) * 4].unsqueeze(2).to_broadcast([128, 4, 64]), ALU.mult),
                                     [bO[ob2], b_rs[t]], [b_ao[t]])
                        yield

            def stage_C(b):
                toks = [b * 2, b * 2 + 1]
                for t in range(2):
                    pc, bpc = psum1()
                    for ch in range(4):
                        k.mm(pc[:, ch * 128:(ch + 1) * 128], cacc[:, ch, t * 128:(t + 1) * 128], ident_f[:],
                             True, True, [b_cacc[ch], b_ident], [bpc])
                    k.act(cn[t], pc, AF.Copy, [bpc], [b_cn[t]])
                    k.ve("dve", "bn_stats", (stc[:, t, :], cn[t]), [b_cn[t]], [b_stc])
                    k.ve("dve", "bn_aggr", (mvc[:, t, :], stc[:, t, :]), [b_stc], [b_stc])
                yield
                for t in range(2):
                    pt, bpt = psum1()
                    for ch in range(4):
                        k.mm(pt[:, ch * 128:(ch + 1) * 128], ao[t][:, ch * 128:(ch + 1) * 128], ident_bf[:],
                             True, True, [b_ao[t], b_ident], [bpt])
                    k.act(catT[t][:, 0:4, :], pt.rearrange("p (c q) -> p c q", c=4), AF.Copy, [bpt], [b_cat[t][0]])
                k.act(sdc, mvc[:, :, 1:2], AF.Sqrt, [b_stc], [b_stc], bias=float(4.0 * LN_EPS))
                k.ve("dve", "reciprocal", (rsc, sdc), [b_stc], [b_stc])
                k.ve("dve", "scalar_tensor_tensor", (nmc, mvc[:, :, 0:1], -1.0, rsc), [b_stc], [b_stc],
                     op0=ALU.mult, op1=ALU.mult)
                yield
                for t in range(2):
                    k.act(cn[t], cn[t], AF.Identity, [b_cn[t], b_stc], [b_cn[t]], bias=nmc[:, t, :], scale=rsc[:, t, :])
                yield
                for t in range(2):
                    k.ve("dve", "tensor_tensor", (cn[t], cn[t], cng, ALU.mult), [b_cn[t], b_cc], [b_cn[t]])
                    k.ve("dve", "tensor_tensor", (cn[t], cn[t], cnb, ALU.add), [b_cn[t], b_cc], [b_cn[t]])
                    yield
                for t in range(2):
                    k.act(th, cn[t], AF.Tanh, [b_cn[t]], [b_th], scale=0.5)
                    k.ve("dve", "scalar_tensor_tensor", (cbf[t], th, 1.0, cn[t]), [b_th, b_cn[t]], [b_cbf[t]],
                         op0=ALU.add, op1=ALU.mult)
                    yield
                Ys = []
                for t in range(2):
                    pt, bpt = psum1()
                    for ch in range(4):
                        k.mm(pt[:, ch * 128:(ch + 1) * 128], cbf[t][:, ch * 128:(ch + 1) * 128], ident_bf[:],
                             True, True, [b_cbf[t], b_ident], [bpt])
                    k.act(catT[t][:, 4:8, :], pt.rearrange("p (c q) -> p c q", c=4), AF.Copy, [bpt], [b_cat[t][1]],
                          scale=0.5)
                    yield
                es_ = []
                for t in range(2):
                    ys = [psum1(), psum1()]
                    for n in range(2):
                        for kc in range(8):
                            k.mm(ys[n][0], catT[t][:, kc, :], w_out[:, kc, n * 512:(n + 1) * 512],
                                 kc == 0, kc == 7, [b_cat[t][kc // 4], bw[1]], [ys[n][1]])
                    i = c.ep_slot.pop(toks[t])
                    e, be = c.ep[i], c.b_ep[i]
                    es_.append((e, be))
                    for n in range(2):
                        sl = slice(n * 512, (n + 1) * 512)
                        k.ve("dve", "scalar_tensor_tensor", (e[:, sl], e[:, sl], float(ALPHA), ys[n][0]),
                             [be, ys[n][1]], [be], op0=ALU.mult, op1=ALU.add)
                    yield
                    for n in range(2):
                        sl = slice(n * 512, (n + 1) * 512)
                        k.ve("dve", "bn_stats", (ste[:, t, n * 6:(n + 1) * 6], e[:, sl]), [be], [b_ste])
                    k.ve("dve", "bn_aggr", (mve[:, t, :], ste[:, t, :]), [b_ste], [b_ste])
                    yield
                k.act(sde, mve[:, :, 1:2], AF.Sqrt, [b_ste], [b_ste], bias=float(LN_EPS))
                k.ve("dve", "reciprocal", (rse, sde), [b_ste], [b_ste])
                k.ve("dve", "scalar_tensor_tensor", (nme, mve[:, :, 0:1], -1.0, rse), [b_ste], [b_ste],
                     op0=ALU.mult, op1=ALU.mult)
                yield
                for t in range(2):
                    e, be = es_[t]
                    k.act(e, e, AF.Identity, [be, b_ste], [be], bias=nme[:, t, :], scale=rse[:, t, :])
                yield
                for t in range(2):
                    e, be = es_[t]
                    k.ve("dve", "tensor_tensor", (e, e, c.lng, ALU.mult), [be, c.b_ln], [be])
                    k.ve("dve", "tensor_tensor", (e, e, c.lnb, ALU.add), [be, c.b_ln], [be])
                    rows = slice(toks[t] * 128, (toks[t] + 1) * 128)
                    k.dma("sp", dst[rows, :], e, [be], [dst_tb[toks[t]]])
                    yield
                for t in range(2):
                    ep_prefetch(c, toks[t] + 2, src, src_tb, None)

            def take(gen, n):
                if gen is None:
                    return None
                for _ in range(n):
                    if next(gen, "end") == "end":
                        return None
                return gen

            ep_prefetch(c, 0, src, src_tb, None)
            ep_prefetch(c, 1, src, src_tb, None)
            for _ in stage_A(0):
                pass
            stage_B1(0)
            cg, ag = conv_gen(0), att_gen(0)
            while cg is not None or ag is not None:
                ag = take(ag, 1)
                cg = take(cg, 8)
            if NB > 1:
                for _ in stage_A(1):
                    pass
                stage_B1(1)
            for b in range(NB):
                cg = ag = None
                if b + 1 < NB:
                    cg, ag = conv_gen(b + 1), att_gen(b + 1)
                for _ in stage_C(b):
                    cg = take(cg, 3)
                    ag = take(ag, 1)
                while ag is not None:
                    ag = take(ag, 1)
                    cg = take(cg, 4)
                while cg is not None:
                    cg = take(cg, 8)
                if b + 2 < NB:
                    for _ in stage_A(b + 2):
                        pass
                    stage_B1(b + 2)
            drip(1000)

        def mixer1_phase(bi, wts, src, src_tb, dst, dst_tb):
            w_in, w_out = wts
            bw = wbuf_b[bi]
            ar.reset()
            c = alloc_common(TBM, NEP)
            NB = T // TBM
            BPS = S // TBM
            NCH = TBM // 64
            zT = ar.alloc((TBM,), F32, parts=16)
            b_zT = Buf()
            gw = ar.alloc((512,), F32, parts=16)
            gb = ar.alloc((4,), F32)
            ngb = ar.alloc((4,), F32)
            hgT = ar.alloc((8,), F32)
            tri = ar.alloc((64,), F32)
            rmask = ar.alloc((TBM,), F32)
            b_const = Buf()
            tA = [ar.alloc((TBM,), F32) for _ in range(2)]
            nb_ = [ar.alloc((TBM,), F32) for _ in range(2)]
            eb = [ar.alloc((TBM,), F32) for _ in range(2)]
            enb = [ar.alloc((TBM,), F32) for _ in range(2)]
            b_tA = [Buf(), Buf()]
            b_nb = [Buf(), Buf()]
            b_eb = [Buf(), Buf()]
            b_enb = [Buf(), Buf()]
            kf = [ar.alloc((TBM,), F32) for _ in range(2)]
            b_kf = [Buf(), Buf()]
            qtT = ar.alloc((4, TBM), BF16)
            b_qt = [Buf() for _ in range(4)]
            ktT = ar.alloc((4, TBM), BF16)
            b_kt = [Buf() for _ in range(4)]
            kdT = ar.alloc((4, TBM), BF16)
            b_kdT = [Buf() for _ in range(4)]
            ebl = ar.alloc((4, NCH), F32)
            b_ebl = [Buf() for _ in range(4)]
            kd = ar.alloc((2, 4, 128), BF16)
            b_kd = [Buf(), Buf()]
            v = ar.alloc((2, D), BF16)
            b_v = [Buf(), Buf()]
            sgg = [ar.alloc((D,), F32) for _ in range(2)]
            b_sgg = [Buf(), Buf()]
            attnT = [ar.alloc((4, 64), BF16) for _ in range(2)]
            b_at = [Buf(), Buf()]
            Sst = ar.alloc((4, 256), F32)
            Sbf = ar.alloc((4, 256), BF16)
            b_S = [Buf() for _ in range(4)]
            b_Sbf = [Buf() for _ in range(4)]
            obf = [ar.alloc((D,), BF16) for _ in range(2)]
            b_obf = [Buf(), Buf()]
            ss = ar.alloc((2, 4), F32)
            lss = ar.alloc((2, 4), F32)
            rinv = ar.alloc((2, 4), F32)
            b_ss = Buf()
            catT = [ar.alloc((8, 128), BF16) for _ in range(2)]
            b_cat = [[Buf(), Buf()], [Buf(), Buf()]]
            ste = ar.alloc((2, 12), F32)
            mve = ar.alloc((2, 2), F32)
            lve = ar.alloc((2, 1), F32)
            rse = ar.alloc((2, 1), F32)
            nme = ar.alloc((2, 1), F32)
            b_ste = Buf()
            load_ln(c, 4)
            k.dma("sp", gw, gatew_in, [], [b_const])
            k.dma("sp", gb, gateb_in, [], [b_const])
            k.dma("sp", hgT, headgT_in, [], [b_const])
            k.dma("sp", tri, tri_in, [], [b_const])
            k.ve("dve", "tensor_scalar", (ngb, gb, -1.0, None), [b_const], [b_const], op0=ALU.mult)
            k.ve("dve", "memset", (rmask, 1.0), [], [b_const])
            k.ve("dve", "memset", (rmask.rearrange("p (c j) -> p c j", j=64)[:, :, 0:1], 0.0), [], [b_const])
            load_xbf(c, src, src_tb, 0)
            DKS = 128 ** -0.5
            O_t = [bank(4, 2), bank(6, 2)]
            bO_t = [(b_ps[4], b_ps[5]), (b_ps[6], b_ps[7])]

            def stage_FE(b):
                bi_ = b % BPS
                transpose_block(c, all_act=True)
                if b + 1 < NB:
                    load_xbf(c, src, src_tb, b + 1)
                drip(3)
                xT, bxT = c.xT, c.b_xT
                yield
                pz, bpz = psum1()
                for kk in range(8):
                    k.mm(pz[0:16, 0:TBM], w_in[:, kk, 3072:3088], xT[:, kk, :], kk == 0, kk == 7,
                         [bw[0], bxT[kk]], [bpz])
                k.act(zT, pz[0:16, 0:TBM], AF.Copy, [bpz], [b_zT])
                yield
                for hh in range(4):
                    r = hh % 2
                    pg, bpg = psum1()
                    k.mm(pg[:, 0:TBM], gw[:, hh * 128:(hh + 1) * 128], zT, True, True, [b_const, b_zT], [bpg])
                    k.act(tA[r], pg[:, 0:TBM], AF.Exp, [bpg, b_const], [b_tA[r]], scale=-1.0, bias=ngb[:, hh:hh + 1])
                    pq, bpq = psum1()
                    for kk in range(8):
                        k.mm(pq[:, 0:TBM], w_in[:, kk, hh * 128:(hh + 1) * 128], xT[:, kk, :], kk == 0, kk == 7,
                             [bw[0], bxT[kk]], [bpq])
                    pk, bpk = psum1()
                    for kk in range(8):
                        k.mm(pk[:, 0:TBM], w_in[:, kk, 512 + hh * 128:512 + (hh + 1) * 128], xT[:, kk, :],
                             kk == 0, kk == 7, [bw[0], bxT[kk]], [bpk])
                    k.act(tA[r], tA[r], AF.Ln, [b_tA[r]], [b_tA[r]], bias=1.0)
                    k.ve("dve", "tensor_tensor_scan", (nb_[r], rmask, tA[r], 0.0, ALU.mult, ALU.add),
                         [b_const, b_tA[r]], [b_nb[r]])
                    k.act(eb[r], nb_[r], AF.Exp, [b_nb[r]], [b_eb[r]], scale=-1.0 / 16.0)
                    k.act(enb[r], nb_[r], AF.Exp, [b_nb[r]], [b_enb[r]], scale=1.0 / 16.0)
                    k.ve("dve", "tensor_copy",
                         (ebl[:, hh, :].unsqueeze(2), eb[r].rearrange("p (c j) -> p c j", j=64)[:, :, 63:64]),
                         [b_eb[r]], [b_ebl[hh]])
                    k.ve("dve", "scalar_tensor_tensor", (qtT[:, hh, :], pq[:, 0:TBM], float(DKS), eb[r]),
                         [bpq, b_eb[r]], [b_qt[hh]], op0=ALU.mult, op1=ALU.mult)
                    k.ve("dve", "tensor_tensor", (kf[r], pk[:, 0:TBM], enb[r], ALU.mult), [bpk, b_enb[r]], [b_kf[r]])
                    k.act(ktT[:, hh, :], kf[r], AF.Copy, [b_kf[r]], [b_kt[hh]])
                    k.ve("dve", "tensor_tensor",
                         (kdT[:, hh, :].rearrange("p (c j) -> p c j", j=64),
                          kf[r].rearrange("p (c j) -> p c j", j=64),
                          ebl[:, hh, :].unsqueeze(2).to_broadcast([128, NCH, 64]), ALU.mult),
                         [b_kf[r], b_ebl[hh]], [b_kdT[hh]])
                    yield
                for t in range(2):
                    pkd, bpkd = psum1()
                    for hh in range(4):
                        k.mm(pkd[:, hh * 128:(hh + 1) * 128], kdT[:, hh, t * 128:(t + 1) * 128], ident_bf[:],
                             True, True, [b_kdT[hh], b_ident], [bpkd])
                    k.act(kd[:, t, :, :], pkd.rearrange("p (h f) -> p h f", h=4), AF.Copy, [bpkd], [b_kd[t]])
                    for n in range(2):
                        pv, bpv = psum1()
                        for kk in range(8):
                            k.mm(pv, xT[:, kk, t * 128:(t + 1) * 128], w_in[:, kk, 1024 + n * 512:1024 + (n + 1) * 512],
                                 kk == 0, kk == 7, [bw[0], bxT[kk]], [bpv])
                        if n == 0:
                            k.act(v[:, t, 0:512], pv, AF.Copy, [bpv], [b_v[t]])
                        else:
                            k.ve("dve", "tensor_copy", (v[:, t, 512:1024], pv), [bpv], [b_v[t]])
                    yield

            def stage_REC(b):
                bi_ = b % BPS
                if bi_ == 0:
                    k.ve("dve", "memset", (Sst, 0.0), [], b_S)
                    k.ve("dve", "memset", (Sbf, 0.0), [], b_Sbf)
                for t in range(2):
                    O, bO = O_t[t], bO_t[t]
                    for cc in range(2):
                        ci = t * 2 + cc
                        rows = slice(cc * 64, (cc + 1) * 64)
                        cols = slice(ci * 64, (ci + 1) * 64)
                        pA, bpA = psum1()
                        for hh in range(4):
                            k.mm(pA[rows, hh * 64:(hh + 1) * 64], ktT[:, hh, cols], qtT[:, hh, cols], True, True,
                                 [b_kt[hh], b_qt[hh]], [bpA])
                        dSs = [psum1(), psum1()]
                        for hh in range(4):
                            oc = slice(hh * 256, (hh + 1) * 256)
                            dsb, bdsb = dSs[hh // 2]
                            k.mm(dsb[:, (hh % 2) * 256:(hh % 2 + 1) * 256], kd[rows, t, hh, :], v[rows, t, oc], True, True,
                                 [b_kd[t], b_v[t]], [bdsb])
                        at, bat = attnT[cc], b_at[cc]
                        k.ve("dve", "tensor_tensor",
                             (at[rows, :, :], pA[rows, 0:256].rearrange("p (h i) -> p h i", h=4),
                              tri[rows, :].unsqueeze(1).to_broadcast([64, 4, 64]), ALU.mult),
                             [bpA, b_const], [bat])
                        for hh in range(4):
                            oc = slice(hh * 256, (hh + 1) * 256)
                            k.mm(O[rows, oc], at[rows, hh, :], v[rows, t, oc], True, False,
                                 [bat, b_v[t]], [bO[hh // 2]])
                            k.mm(O[rows, oc], qtT[:, hh, cols], Sbf[:, hh, :], False, True,
                                 [b_qt[hh], b_Sbf[hh]], [bO[hh // 2]])
                        for hh in range(4):
                            dsb, bdsb = dSs[hh // 2]
                            k.ve("dve", "scalar_tensor_tensor",
                                 (Sst[:, hh, :], Sst[:, hh, :], ebl[:, hh, ci:ci + 1],
                                  dsb[:, (hh % 2) * 256:(hh % 2 + 1) * 256]),
                                 [b_S[hh], b_ebl[hh], bdsb], [b_S[hh]], op0=ALU.mult, op1=ALU.add)
                            k.act(Sbf[:, hh, :], Sst[:, hh, :], AF.Copy, [b_S[hh]], [b_Sbf[hh]])
                        yield

            def stage_G(b):
                xT, bxT = c.xT, c.b_xT
                for t in range(2):
                    for n in range(2):
                        pgg, bpgg = psum1()
                        for kk in range(8):
                            k.mm(pgg, xT[:, kk, t * 128:(t + 1) * 128], w_in[:, kk, 2048 + n * 512:2048 + (n + 1) * 512],
                                 kk == 0, kk == 7, [bw[0], bxT[kk]], [bpgg])
                        k.act(sgg[t][:, n * 512:(n + 1) * 512], pgg, AF.Silu, [bpgg], [b_sgg[t]])
                        yield

            def stage_TAIL(b):
                toks = [b * 2, b * 2 + 1]
                for t in range(2):
                    O, bO = O_t[t], bO_t[t]
                    for hh in range(4):
                        k.act(obf[t][:, hh * 256:(hh + 1) * 256], O[:, hh * 256:(hh + 1) * 256], AF.Square,
                              [bO[hh // 2]], [b_obf[t], b_ss], accum_out=ss[:, t, hh:hh + 1])
                yield
                k.act(lss, ss, AF.Ln, [b_ss], [b_ss], scale=1.0 / 256.0, bias=float(RMS_EPS))
                k.act(rinv, lss, AF.Exp, [b_ss], [b_ss], scale=-0.5)
                yield
                for t in range(2):
                    O, bO = O_t[t], bO_t[t]
                    for hh in range(4):
                        oc = slice(hh * 256, (hh + 1) * 256)
                        k.ve("dve", "scalar_tensor_tensor", (obf[t][:, oc], O[:, oc], rinv[:, t, hh:hh + 1], sgg[t][:, oc]),
                             [bO[hh // 2], b_ss, b_sgg[t]], [b_obf[t]], op0=ALU.mult, op1=ALU.mult)
                    yield
                for t in range(2):
                    for half in range(2):
                        pt, bpt = psum1()
                        for ch in range(4):
                            kc = half * 4 + ch
                            k.mm(pt[:, ch * 128:(ch + 1) * 128], obf[t][:, kc * 128:(kc + 1) * 128], ident_bf[:],
                                 True, True, [b_obf[t], b_ident], [bpt])
                        k.ve("dve", "tensor_tensor",
                             (catT[t][:, half * 4:(half + 1) * 4, :], pt.rearrange("p (c q) -> p c q", c=4),
                              hgT[:, half * 4:(half + 1) * 4].unsqueeze(2).to_broadcast([128, 4, 128]), ALU.mult),
                             [bpt, b_const], [b_cat[t][half]])
                    yield
                es_ = []
                for t in range(2):
                    ys = [psum1(), psum1()]
                    for n in range(2):
                        for kc in range(8):
                            k.mm(ys[n][0], catT[t][:, kc, :], w_out[:, kc, n * 512:(n + 1) * 512],
                                 kc == 0, kc == 7, [b_cat[t][kc // 4], bw[1]], [ys[n][1]])
                    i = c.ep_slot.pop(toks[t])
                    e, be = c.ep[i], c.b_ep[i]
                    es_.append((e, be))
                    for n in range(2):
                        sl = slice(n * 512, (n + 1) * 512)
                        k.ve("dve", "scalar_tensor_tensor", (e[:, sl], e[:, sl], float(ALPHA), ys[n][0]),
                             [be, ys[n][1]], [be], op0=ALU.mult, op1=ALU.add)
                    yield
                    for n in range(2):
                        sl = slice(n * 512, (n + 1) * 512)
                        k.ve("dve", "bn_stats", (ste[:, t, n * 6:(n + 1) * 6], e[:, sl]), [be], [b_ste])
                    k.ve("dve", "bn_aggr", (mve[:, t, :], ste[:, t, :]), [b_ste], [b_ste])
                    yield
                k.act(lve, mve[:, :, 1:2], AF.Ln, [b_ste], [b_ste], bias=float(LN_EPS))
                k.act(rse, lve, AF.Exp, [b_ste], [b_ste], scale=-0.5)
                k.ve("dve", "scalar_tensor_tensor", (nme, mve[:, :, 0:1], -1.0, rse), [b_ste], [b_ste],
                     op0=ALU.mult, op1=ALU.mult)
                yield
                for t in range(2):
                    e, be = es_[t]
                    k.act(e, e, AF.Identity, [be, b_ste], [be], bias=nme[:, t, :], scale=rse[:, t, :])
                yield
                for t in range(2):
                    e, be = es_[t]
                    k.ve("dve", "tensor_tensor", (e, e, c.lng, ALU.mult), [be, c.b_ln], [be])
                    k.ve("dve", "tensor_tensor", (e, e, c.lnb, ALU.add), [be, c.b_ln], [be])
                    rows_ = slice(toks[t] * 128, (toks[t] + 1) * 128)
                    k.dma("sp", dst[rows_, :], e, [be], [dst_tb[toks[t]]])
                    yield
                for t in range(2):
                    ep_prefetch(c, toks[t] + 2, src, src_tb, None)

            def take(gen, n):
                if gen is None:
                    return None
                for _ in range(n):
                    if next(gen, "end") == "end":
                        return None
                return gen

            ep_prefetch(c, 0, src, src_tb, None)
            ep_prefetch(c, 1, src, src_tb, None)
            for _ in stage_FE(0):
                pass
            rec, gg = stage_REC(0), stage_G(0)
            while rec is not None or gg is not None:
                rec = take(rec, 1)
                gg = take(gg, 1)
            for b in range(NB):
                tail = stage_TAIL(b)
                fe = stage_FE(b + 1) if b + 1 < NB else None
                while fe is not None:
                    tail = take(tail, 1)
                    fe = take(fe, 1)
                rec = gg = None
                if b + 1 < NB:
                    rec, gg = stage_REC(b + 1), stage_G(b + 1)
                while tail is not None or rec is not None or gg is not None:
                    tail = take(tail, 1)
                    rec = take(rec, 1)
                    gg = take(gg, 1)
            drip(1000)

        if phases == ("m0", "f0", "m1", "f1"):
            w0, l0 = mixer_weight_loads(0, "even_w_in", IN_EVEN, "even_w_out", first_cols=(1536, 2560))
            w1, l1 = ffn_weight_loads(1, 0, 0)
            pending_loads.extend(l0)
            pending_loads.extend(l1)
            mixer0_phase(0, w0, x_in, xin_tb, xs[0], xs_tb[0], first=True)
            p.barrier()
            w2, l2 = ffn_weight_loads(0, 0, 1)
            pending_loads.extend(l2)
            ffn_phase(0, 0, 1, w1, xs[0], xs_tb[0], None, None, 2)
            p.barrier()
            w3, l3 = mixer_weight_loads(1, "odd_w_in", IN_ODD, "odd_w_out")
            pending_loads.extend(l3)
            ffn_phase(0, 1, 0, w2, xs[0], xs_tb[0], xs[1], xs_tb[1], 2)
            p.barrier()
            w4, l4 = ffn_weight_loads(0, 1, 0)
            pending_loads.extend(l4)
            mixer1_phase(1, w3, xs[1], xs_tb[1], xs[0], xs_tb[0])
            p.barrier()
            w5, l5 = ffn_weight_loads(1, 1, 1)
            pending_loads.extend(l5)
            ffn_phase(1, 0, 0, w4, xs[0], xs_tb[0], None, None, 6)
            p.barrier()
            ffn_phase(1, 1, 1, w5, xs[0], xs_tb[0], out_d, out_tb, 6)
        elif phases == ("f0",):
            wa, la = ffn_weight_loads(0, 0, 0)
            wb, lb = ffn_weight_loads(1, 0, 1)
            pending_loads.extend(la + lb)
            drip(1000)
            ffn_phase(0, 0, 0, wa, x_in, xin_tb, None, None, 2)
            p.barrier()
            ffn_phase(0, 1, 1, wb, x_in, xin_tb, out_d, out_tb, 2)
        elif phases == ("m1",):
            wm, lm = mixer_weight_loads(0, "odd_w_in", IN_ODD, "odd_w_out")
            pending_loads.extend(lm)
            drip(1000)
            mixer1_phase(0, wm, x_in, xin_tb, out_d, out_tb)
        elif phases == ("m0",):
            wm, lm = mixer_weight_loads(0, "even_w_in", IN_EVEN, "even_w_out")
            pending_loads.extend(lm)
            drip(1000)
            mixer0_phase(0, wm, x_in, xin_tb, out_d, out_tb)
        p.finalize_and_emit()
    return nc


def make_lnv(inp):
    return np.ascontiguousarray(np.stack([
        inp["mix_norm_g"][0], inp["mix_norm_b"][0], inp["ffn_norm_g"][0], inp["ffn_norm_b"][0],
        inp["mix_norm_g"][1], inp["mix_norm_b"][1], inp["ffn_norm_g"][1], inp["ffn_norm_b"][1]], 0).astype(np.float32))


def make_biasT(rel_bias):
    rb = np.asarray(rel_bias, dtype=np.float32)
    ki = np.arange(128)[:, None]
    qi = np.arange(128)[None, :]
    out = np.empty((128, A_HEADS, 5, 128), np.float32)
    for j in range(5):
        idx = np.clip(128 * (4 - j) + qi - ki, -128, 128) + 128
        t = rb[:, idx]
        out[:, :, j, :] = np.transpose(t, (1, 0, 2))
    out[0:64, :, 0, 64:128] = NEGM
    out[64:128, :, 4, 0:64] = NEGM
    return np.ascontiguousarray(out.reshape(128, A_HEADS * 5 * 128))


def make_in_maps(inp, nseq=NSEQ, ncores=NCORES):
    x = np.asarray(inp["x"], dtype=np.float32)
    cw = np.asarray(inp["even_conv_w"][0], np.float32)
    common = {
        "even_w_in": np.ascontiguousarray(inp["even_w_in"][0]),
        "even_w_out": np.ascontiguousarray(inp["even_w_out"][0]),
        "odd_w_in": np.ascontiguousarray(inp["odd_w_in"][0]),
        "odd_w_out": np.ascontiguousarray(inp["odd_w_out"][0]),
        "lnv": make_lnv(inp),
        "ident": np.eye(128, dtype=np.float32),
        "biasT": make_biasT(inp["even_rel_bias"][0]),
        "convw": np.ascontiguousarray(cw.T.reshape(4, 128, CONV_W).transpose(1, 0, 2).reshape(128, 4 * CONV_W)),
        "convb": np.ascontiguousarray(np.asarray(inp["even_conv_b"][0], np.float32).reshape(4, 128).T),
        "cbias": np.ascontiguousarray(np.asarray(inp["even_rel_bias"][0], np.float32)[:, 2 * 128].reshape(1, A_HEADS)),
        "convn": np.ascontiguousarray(np.stack([inp["even_conv_norm_g"][0], inp["even_conv_norm_b"][0]], 0)
                                      .astype(np.float32)),
        "gatew": np.ascontiguousarray(np.asarray(inp["odd_gate_w"][0], np.float32)),
        "gateb": np.ascontiguousarray(np.asarray(inp["odd_gate_b"][0], np.float32).reshape(4, 128).T),
        "headg": np.ascontiguousarray(np.asarray(inp["odd_head_norm_g"][0], np.float32).reshape(1, D)),
        "headgT": np.ascontiguousarray(np.asarray(inp["odd_head_norm_g"][0], np.float32).reshape(8, 128).T),
        "tri": np.ascontiguousarray((np.arange(128)[:, None] % 64 <= np.arange(64)[None, :]).astype(np.float32)),
    }
    for l in range(2):
        common["wg%d" % l] = np.ascontiguousarray(inp["ffn_w_gate"][l])
        common["wu%d" % l] = np.ascontiguousarray(inp["ffn_w_up"][l])
        common["wd%d" % l] = np.ascontiguousarray(inp["ffn_w_down"][l])
    maps = []
    for c_ in range(ncores):
        m = dict(common)
        m["x"] = np.ascontiguousarray(x[c_ * nseq:(c_ + 1) * nseq].reshape(nseq * S, D))
        maps.append(m)
    return maps


def kernel(**inputs):
    inp = {k_: np.asarray(v) for k_, v in inputs.items()}
    nc = build()
    in_maps = make_in_maps(inp)
    res = run_bass_kernel_spmd(nc, in_maps, core_ids=list(range(NCORES)))
    outs = [np.asarray(r["out"]).reshape(NSEQ, S, D) for r in res.results]
    return np.concatenate(outs, axis=0).astype(np.float32)
```

```python
import contextlib
import numpy as np
import concourse.bass as bass
import concourse.mybir as mybir
from concourse.bass_utils import run_bass_kernel_spmd

F32 = mybir.dt.float32
BF16 = mybir.dt.bfloat16
AF = mybir.ActivationFunctionType
ALU = mybir.AluOpType

D = 1024
S = 2048
NSEQ = 2
NCORES = 8
DFF = 2816
HF = DFF // 2
NJ = HF // 128
ALPHA = 4 ** 0.25
LN_EPS = 1e-5
RMS_EPS = 1e-6
TB = 512
TBM = 256
WCOLS = 3 * 8 * HF
ACOLS = 38000
A_HEADS = 8
CONV_W = 31
NEGM = -30000.0
STRICT = False
IN_EVEN = 2560
IN_ODD = 3088


class Op:
    __slots__ = ("eng", "fn", "deps", "marked", "sem", "val", "dma", "key")


class Buf:
    __slots__ = ("name", "writers", "readers")

    def __init__(self, name=""):
        self.name = name
        self.writers = {}
        self.readers = {}


class Prog:
    ENG = ("pe", "act", "dve", "pool", "sp")
    NDQ = 8

    def __init__(self, nc, es):
        self.nc = nc
        self.streams = {e: [] for e in self.ENG}
        self.sems = {e: es.enter_context(nc.semaphore("s_" + e)) for e in ("pe", "act", "dve", "pool")}
        self.dq = {q: [es.enter_context(nc.semaphore("d_%s%d" % (q, i))) for i in range(self.NDQ)]
                   for q in ("sp", "pool", "act")}
        self.dq_n = {q: 0 for q in self.dq}
        self.dq_cnt = {q: [0] * self.NDQ for q in self.dq}
        self.dq_last = {q: [None] * self.NDQ for q in self.dq}
        self.pending = {}

    def barrier(self):
        lasts = []
        for e in ("pe", "act", "dve", "pool"):
            for o in reversed(self.streams[e]):
                if not o.dma:
                    lasts.append(o)
                    break
        for q in self.dq:
            for last in self.dq_last[q]:
                if last is not None:
                    lasts.append(last)
        self.pending = {e: list(lasts) for e in self.ENG}

    def op(self, eng, fn, reads=(), writes=(), dma=False):
        o = Op()
        o.eng, o.fn, o.deps, o.marked, o.dma = eng, fn, [], False, dma
        o.sem = None
        o.val = 0
        if dma:
            i = self.dq_n[eng]
            self.dq_n[eng] += 1
            slot = i % self.NDQ
            o.key = (eng, slot)
            prev = self.dq_last[eng][slot]
            if prev is not None:
                o.deps.append(prev)
            o.sem = self.dq[eng][slot]
            self.dq_cnt[eng][slot] += 16
            o.val = self.dq_cnt[eng][slot]
            self.dq_last[eng][slot] = o
        else:
            o.key = eng
        seen = set()

        def add(d, raw):
            if d is o or id(d) in seen:
                return
            if not dma and not d.dma and d.eng == eng:
                if eng == "pe" or not (raw or STRICT):
                    return
            seen.add(id(d))
            o.deps.append(d)

        pend = self.pending.pop(eng, None)
        if pend:
            for d in pend:
                add(d, False)
        for b in reads:
            for d in b.writers.values():
                add(d, True)
        for b in writes:
            for d in b.writers.values():
                add(d, False)
            for d in b.readers.values():
                add(d, False)
        for b in reads:
            b.readers[o.key] = o
        for b in writes:
            if b.readers:
                b.readers = {}
                b.writers = {}
            b.writers[o.key] = o
        self.streams[eng].append(o)
        return o

    def finalize_and_emit(self):
        nc = self.nc
        for e in self.ENG:
            for o in self.streams[e]:
                for d in o.deps:
                    if not d.dma:
                        d.marked = True
        for e in ("pe", "act", "dve", "pool"):
            c = 0
            for o in self.streams[e]:
                if not o.dma and o.marked:
                    c += 1
                    o.sem = self.sems[e]
                    o.val = c
        streams = self.streams

        def emit(ename, eng):
            waited = {}
            for o in streams[ename]:
                need = {}
                for d in o.deps:
                    k = id(d.sem)
                    if waited.get(k, 0) >= d.val:
                        continue
                    if k not in need or need[k][1] < d.val:
                        need[k] = (d.sem, d.val)
                for k, (sem, val) in need.items():
                    eng.wait_ge(sem, val)
                    waited[k] = val
                ins = o.fn(eng)
                if o.dma:
                    ins.then_inc(o.sem, 16)
                elif o.marked:
                    ins.then_inc(o.sem, 1)
            if ename in self.dq:
                for slot in range(self.NDQ):
                    last = self.dq_last[ename][slot]
                    if last is not None and waited.get(id(last.sem), 0) < last.val:
                        eng.wait_ge(last.sem, last.val)

        with nc.Block() as block:
            @block.tensor
            def _(e):
                emit("pe", e)

            @block.scalar
            def _(e):
                emit("act", e)

            @block.vector
            def _(e):
                emit("dve", e)

            @block.gpsimd
            def _(e):
                emit("pool", e)

            @block.sync
            def _(e):
                emit("sp", e)


class K:
    def __init__(self, nc, es):
        self.nc = nc
        self.p = Prog(nc, es)
        self.nbuf = 0

    def sb(self, name, shape, dt):
        return self.nc.alloc_sbuf_tensor(name, list(shape), dt)

    def mm(self, out, lhsT, rhs, start, stop, reads, writes):
        return self.p.op("pe", lambda e: e.matmul(out, lhsT, rhs, start=start, stop=stop), reads, writes)

    def act(self, out, in_, func, reads, writes, bias=None, scale=None, accum_out=None):
        kw = {}
        if bias is not None:
            kw["bias"] = bias
        if scale is not None:
            kw["scale"] = scale
        if accum_out is not None:
            kw["accum_out"] = accum_out
        return self.p.op("act", lambda e: e.activation(out, in_, func, **kw), reads, writes)

    def ve(self, eng, name, args, reads, writes, **kw):
        return self.p.op(eng, lambda e: getattr(e, name)(*args, **kw), reads, writes)

    def dma(self, q, out, in_, reads, writes):
        return self.p.op(q, lambda e: e.dma_start(out=out, in_=in_), reads, writes, dma=True)


class Arena:
    def __init__(self, t, ncols):
        self.t = t
        self.n = ncols
        self.off = 0

    def reset(self):
        self.off = 0

    def alloc(self, shape, dt, parts=128):
        n = 1
        for s_ in shape:
            n *= s_
        cols = n * (2 if dt == F32 else 1)
        cols = (cols + 15) // 16 * 16
        assert self.off + cols <= self.n, ("arena overflow", self.off, cols, self.n)
        ap = self.t[0:parts, self.off:self.off + cols]
        self.off += cols
        if dt == F32:
            ap = ap.bitcast(F32)
        ap = ap[:, 0:n]
        if len(shape) == 2:
            ap = ap.rearrange("p (a b) -> p a b", a=shape[0])
        elif len(shape) == 3:
            ap = ap.rearrange("p (a b c) -> p a b c", a=shape[0], b=shape[1])
        return ap


def build(cfg=None):
    cfg = cfg or {}
    nseq = cfg.get("nseq", NSEQ)
    T = nseq * S
    NT = T // 128
    phases = cfg.get("phases", ("m0", "f0", "m1", "f1"))
    nc = bass.Bass("TRN2", target_bir_lowering=False)

    def din(name, shape):
        return nc.dram_tensor(name, list(shape), F32, kind="ExternalInput").ap()

    x_in = din("x", (T, D))
    w = {}
    w["even_w_in"] = din("even_w_in", (D, IN_EVEN))
    w["even_w_out"] = din("even_w_out", (D, D))
    w["odd_w_in"] = din("odd_w_in", (D, IN_ODD))
    w["odd_w_out"] = din("odd_w_out", (D, D))
    for l in range(2):
        w["wg%d" % l] = din("wg%d" % l, (D, DFF))
        w["wu%d" % l] = din("wu%d" % l, (D, DFF))
        w["wd%d" % l] = din("wd%d" % l, (DFF, D))
    lnv = din("lnv", (8, D))
    ident_in = din("ident", (128, 128))
    biasT_in = din("biasT", (128, A_HEADS * 5 * 128))
    convw_in = din("convw", (128, 4 * CONV_W))
    convb_in = din("convb", (128, 4))
    convn_in = din("convn", (2, 512))
    cbias_in = din("cbias", (1, A_HEADS))
    gatew_in = din("gatew", (16, 512))
    gateb_in = din("gateb", (128, 4))
    headg_in = din("headg", (1, D))
    headgT_in = din("headgT", (128, 8))
    tri_in = din("tri", (128, 64))
    out_d = nc.dram_tensor("out", [T, D], F32, kind="ExternalOutput").ap()
    xs = [nc.dram_tensor("xs%d" % i, [T, D], F32, kind="Internal").ap() for i in range(2)]
    ya = nc.dram_tensor("ya", [T, D], F32, kind="Internal").ap()
    xs_tb = [[Buf() for _ in range(NT)] for _ in range(2)]
    ya_tb = [Buf() for _ in range(NT)]
    xin_tb = [Buf() for _ in range(NT)]
    out_tb = [Buf() for _ in range(NT)]

    es = contextlib.ExitStack()
    with es:
        k = K(nc, es)
        p = k.p
        wbuf = [k.sb("wbuf%d" % i, (128, WCOLS), BF16) for i in range(2)]
        wbuf_b = [[Buf("wb%d_%d" % (i, j)) for j in range(4)] for i in range(2)]
        ident_bf = k.sb("ident_bf", (128, 128), BF16)
        ident_f = k.sb("ident_f", (128, 128), F32)
        b_ident = Buf("ident")
        NEP = 2
        stt = [k.sb("stt%d" % i, (128, 16), F32) for i in range(4)]
        b_stt = [Buf() for _ in range(4)]
        arena_t = k.sb("arena", (128, ACOLS), BF16)
        ar = Arena(arena_t, ACOLS)
        ps_all = nc.alloc_psum_tensor("ps_all", [128, 4096], F32)
        b_ps = [Buf("ps%d" % i) for i in range(8)]

        def bank(i, n=1):
            return ps_all[:, i * 512:(i + n) * 512]

        ps_rr = [0]

        def psum1():
            i = ps_rr[0] % 4
            ps_rr[0] += 1
            return bank(i), b_ps[i]

        k.dma("pool", ident_bf[:], ident_in, [], [b_ident])
        k.dma("sp", ident_f[:], ident_in, [], [b_ident])

        def wview(bi, off, kk, n):
            return wbuf[bi][:, off:off + kk * n].rearrange("p (k n) -> p k n", k=kk)

        def ffn_weight_loads(bi, l, hf):
            wg = wview(bi, 0, 8, HF)
            wu = wview(bi, 8 * HF, 8, HF)
            wd = wview(bi, 16 * HF, NJ, D)
            loads = []
            for kk in range(8):
                loads.append(lambda kk=kk: k.dma(
                    "pool", wg[:, kk, :], w["wg%d" % l][kk * 128:(kk + 1) * 128, hf * HF:(hf + 1) * HF],
                    [], [wbuf_b[bi][0]]))
                loads.append(lambda kk=kk: k.dma(
                    "pool", wu[:, kk, :], w["wu%d" % l][kk * 128:(kk + 1) * 128, hf * HF:(hf + 1) * HF],
                    [], [wbuf_b[bi][1]]))
            for j in range(NJ):
                r0 = hf * HF + j * 128
                loads.append(lambda j=j, r0=r0: k.dma("pool", wd[:, j, :], w["wd%d" % l][r0:r0 + 128, :],
                                                     [], [wbuf_b[bi][2]]))
            return (wg, wu, wd), loads

        def mixer_weight_loads(bi, name_in, nin, name_out, first_cols=None):
            w_in = wview(bi, 0, 8, nin)
            w_out = wview(bi, 8 * nin, 8, D)
            loads = []
            if first_cols is not None:
                c0, c1 = first_cols
                for kk in range(8):
                    loads.append(lambda kk=kk: k.dma("pool", w_in[:, kk, c0:c1], w[name_in][kk * 128:(kk + 1) * 128, c0:c1],
                                                     [], [wbuf_b[bi][0]]))
                for kk in range(8):
                    loads.append(lambda kk=kk: k.dma("pool", w_in[:, kk, 0:c0], w[name_in][kk * 128:(kk + 1) * 128, 0:c0],
                                                     [], [wbuf_b[bi][0]]))
                    if c1 < nin:
                        loads.append(lambda kk=kk: k.dma("pool", w_in[:, kk, c1:nin],
                                                         w[name_in][kk * 128:(kk + 1) * 128, c1:nin],
                                                         [], [wbuf_b[bi][0]]))
            else:
                for kk in range(8):
                    loads.append(lambda kk=kk: k.dma("pool", w_in[:, kk, :], w[name_in][kk * 128:(kk + 1) * 128, :],
                                                     [], [wbuf_b[bi][0]]))
            for kk in range(8):
                loads.append(lambda kk=kk: k.dma("pool", w_out[:, kk, :], w[name_out][kk * 128:(kk + 1) * 128, :],
                                                 [], [wbuf_b[bi][1]]))
            return (w_in, w_out), loads

        pending_loads = []

        def drip(n):
            for _ in range(n):
                if pending_loads:
                    pending_loads.pop(0)()

        class Ctx:
            pass

        def alloc_common(tb, nep):
            c = Ctx()
            c.tb = tb
            c.ntl = tb // 128
            c.lng = ar.alloc((D,), F32)
            c.lnb = ar.alloc((D,), F32)
            c.b_ln = Buf("ln")
            c.xbf = ar.alloc((c.ntl, D), BF16)
            c.b_xbf = Buf("xbf")
            c.xT = ar.alloc((8, tb), BF16)
            c.b_xT = [Buf() for _ in range(8)]
            c.nep = nep
            c.ep = [ar.alloc((D,), F32) for _ in range(nep)]
            c.b_ep = [Buf() for _ in range(nep)]
            c.ep_n = 0
            c.ep_slot = {}
            return c

        def load_ln(c, row):
            k.dma("sp", c.lng, lnv[row:row + 1, :].partition_broadcast(128), [], [c.b_ln])
            k.dma("sp", c.lnb, lnv[row + 1:row + 2, :].partition_broadcast(128), [], [c.b_ln])

        def load_xbf(c, src, src_tb, b):
            v = src[b * c.tb:(b + 1) * c.tb, :].rearrange("(t p) d -> p t d", p=128)
            k.dma("pool", c.xbf, v, [src_tb[b * c.ntl + t] for t in range(c.ntl)], [c.b_xbf])

        def transpose_block(c, all_act=False):
            ntl = c.ntl
            for kk in range(8):
                pt, bpt = psum1()
                for t in range(ntl):
                    k.mm(pt[:, t * 128:(t + 1) * 128], c.xbf[:, t, kk * 128:(kk + 1) * 128], ident_bf[:],
                         True, True, [c.b_xbf, b_ident], [bpt])
                if all_act or kk % 2 == 0:
                    k.act(c.xT[:, kk, :], pt[:, 0:c.tb], AF.Copy, [bpt], [c.b_xT[kk]])
                else:
                    k.ve("dve", "tensor_copy", (c.xT[:, kk, :], pt[:, 0:c.tb]), [bpt], [c.b_xT[kk]])

        def ep_prefetch(c, tok, src, src_tb, ex):
            if tok >= NT or tok in c.ep_slot:
                return
            i = c.ep_n % c.nep
            c.ep_n += 1
            c.ep_slot[tok] = i
            rows = slice(tok * 128, (tok + 1) * 128)
            k.dma("sp", c.ep[i], src[rows, :], [src_tb[tok]], [c.b_ep[i]])
            if ex is not None:
                yat, b_yat = ex
                k.dma("sp", yat[i], ya[rows, :], [ya_tb[tok]], [b_yat[i]])

        def ln_epilogue(c, tok, Y, bY, src, src_tb, ex, dst, dst_tb):
            ep_prefetch(c, tok, src, src_tb, ex)
            i = c.ep_slot.pop(tok)
            ep_prefetch(c, tok + 1, src, src_tb, ex)
            e, be = c.ep[i], c.b_ep[i]
            st, bs = stt[i], b_stt[i]
            rows = slice(tok * 128, (tok + 1) * 128)
            for n in range(2):
                sl = slice(n * 512, (n + 1) * 512)
                k.ve("dve", "scalar_tensor_tensor", (e[:, sl], e[:, sl], float(ALPHA), Y[:, sl]),
                     [be, bY[n]], [be], op0=ALU.mult, op1=ALU.add)
            if ex is not None:
                yat, b_yat = ex
                k.ve("dve", "tensor_tensor", (e, e, yat[i], ALU.add), [be, b_yat[i]], [be])
            for n in range(2):
                sl = slice(n * 512, (n + 1) * 512)
                k.ve("dve", "bn_stats", (st[:, n * 6:(n + 1) * 6], e[:, sl]), [be], [bs])
            k.ve("dve", "bn_aggr", (st[:, 12:14], st[:, 0:12]), [bs], [bs])
            k.act(st[:, 11:12], st[:, 13:14], AF.Sqrt, [bs], [bs], bias=float(LN_EPS))
            k.ve("dve", "reciprocal", (st[:, 14:15], st[:, 11:12]), [bs], [bs])
            k.ve("dve", "scalar_tensor_tensor", (st[:, 15:16], st[:, 12:13], -1.0, st[:, 14:15]),
                 [bs], [bs], op0=ALU.mult, op1=ALU.mult)
            k.act(e, e, AF.Identity, [be, bs], [be], bias=st[:, 15:16], scale=st[:, 14:15])
            k.ve("dve", "tensor_tensor", (e, e, c.lng, ALU.mult), [be, c.b_ln], [be])
            k.ve("dve", "tensor_tensor", (e, e, c.lnb, ALU.add), [be, c.b_ln], [be])
            k.dma("sp", dst[rows, :], e, [be], [dst_tb[tok]])

        def ffn_phase(l, hf, bi, wts, src, src_tb, dst, dst_tb, lnrow):
            wg, wu, wd = wts
            bw = wbuf_b[bi]
            ar.reset()
            nep = 2 if hf == 0 else 4
            c = alloc_common(TB, nep)
            NB = T // TB
            hT = ar.alloc((NJ, TB), BF16)
            b_hT = [Buf() for _ in range(NJ)]
            sg = [ar.alloc((TB,), F32) for _ in range(2)]
            b_sg = [Buf(), Buf()]
            ste = ar.alloc((4, 12), F32)
            mve = ar.alloc((4, 2), F32)
            sde = ar.alloc((4, 1), F32)
            rse = ar.alloc((4, 1), F32)
            nme = ar.alloc((4, 1), F32)
            b_ste = Buf()
            if hf == 1:
                load_ln(c, lnrow)
            esrc, esrc_tb = (src, src_tb) if hf == 0 else (ya, ya_tb)
            load_xbf(c, src, src_tb, 0)
            if hf == 0:
                ep_prefetch(c, 0, esrc, esrc_tb, None)
            else:
                for t in range(4):
                    ep_prefetch(c, t, esrc, esrc_tb, None)
            yn = 0
            deferred = []
            transpose_block(c, all_act=True)
            if NB > 1:
                load_xbf(c, src, src_tb, 1)
            for b in range(NB):
                drip(4)
                for j in range(NJ):
                    pg, bpg = psum1()
                    pu, bpu = psum1()
                    for kk in range(8):
                        k.mm(pg, wg[:, kk, j * 128:(j + 1) * 128], c.xT[:, kk, :], kk == 0, kk == 7,
                             [bw[0], c.b_xT[kk]], [bpg])
                    for kk in range(8):
                        k.mm(pu, wu[:, kk, j * 128:(j + 1) * 128], c.xT[:, kk, :], kk == 0, kk == 7,
                             [bw[1], c.b_xT[kk]], [bpu])
                    si = j % 2
                    k.act(sg[si], pg, AF.Silu, [bpg], [b_sg[si]])
                    k.ve("dve", "tensor_tensor", (hT[:, j, :], sg[si], pu, ALU.mult),
                         [b_sg[si], bpu], [b_hT[j]])
                    if j == 3 and deferred:
                        deferred.pop(0)()
                es_ = []
                for t in range(4):
                    tok = b * 4 + t
                    yb = 4 + 2 * (yn % 2)
                    yn += 1
                    Y = bank(yb, 2)
                    bY = (b_ps[yb], b_ps[yb + 1])
                    for n in range(2):
                        for j in range(NJ):
                            k.mm(Y[:, n * 512:(n + 1) * 512], hT[:, j, t * 128:(t + 1) * 128],
                                 wd[:, j, n * 512:(n + 1) * 512], j == 0, j == NJ - 1, [b_hT[j], bw[2]], [bY[n]])
                    rows = slice(tok * 128, (tok + 1) * 128)
                    i = c.ep_slot.pop(tok)
                    e, be = c.ep[i], c.b_ep[i]
                    if hf == 0:
                        ep_prefetch(c, tok + 1, esrc, esrc_tb, None)
                        for n in range(2):
                            sl = slice(n * 512, (n + 1) * 512)
                            k.ve("dve", "scalar_tensor_tensor", (e[:, sl], e[:, sl], float(ALPHA), Y[:, sl]),
                                 [be, bY[n]], [be], op0=ALU.mult, op1=ALU.add)
                        k.dma("sp", ya[rows, :], e, [be], [ya_tb[tok]])
                    else:
                        for n in range(2):
                            sl = slice(n * 512, (n + 1) * 512)
                            k.ve("dve", "tensor_tensor", (e[:, sl], e[:, sl], Y[:, sl], ALU.add),
                                 [be, bY[n]], [be])
                        for n in range(2):
                            sl = slice(n * 512, (n + 1) * 512)
                            k.ve("dve", "bn_stats", (ste[:, t, n * 6:(n + 1) * 6], e[:, sl]), [be], [b_ste])
                        k.ve("dve", "bn_aggr", (mve[:, t, :], ste[:, t, :]), [b_ste], [b_ste])
                        es_.append((e, be, tok))
                if b + 1 < NB:
                    transpose_block(c, all_act=True)
                    if b + 2 < NB:
                        load_xbf(c, src, src_tb, b + 2)
                if hf == 1:
                    k.act(sde, mve[:, :, 1:2], AF.Sqrt, [b_ste], [b_ste], bias=float(LN_EPS))
                    k.ve("dve", "reciprocal", (rse, sde), [b_ste], [b_ste])
                    k.ve("dve", "scalar_tensor_tensor", (nme, mve[:, :, 0:1], -1.0, rse), [b_ste], [b_ste],
                         op0=ALU.mult, op1=ALU.mult)
                    for t in range(4):
                        e, be, tok = es_[t]
                        k.act(e, e, AF.Identity, [be, b_ste], [be], bias=nme[:, t, :], scale=rse[:, t, :])

                    def fin(es_=es_, b=b):
                        for (e, be, tok) in es_:
                            k.ve("dve", "tensor_tensor", (e, e, c.lng, ALU.mult), [be, c.b_ln], [be])
                            k.ve("dve", "tensor_tensor", (e, e, c.lnb, ALU.add), [be, c.b_ln], [be])
                            rows = slice(tok * 128, (tok + 1) * 128)
                            k.dma("sp", dst[rows, :], e, [be], [dst_tb[tok]])
                        for (e, be, tok) in es_:
                            ep_prefetch(c, tok + 4, esrc, esrc_tb, None)
                    deferred.append(fin)
            while deferred:
                deferred.pop(0)()
            drip(1000)

        def mixer0_phase(bi, wts, src, src_tb, dst, dst_tb, first=False):
            w_in, w_out = wts
            bw = wbuf_b[bi]
            ar.reset()
            c = alloc_common(TBM, NEP)
            NB = T // TBM
            BPS = S // TBM
            RING = 6
            qT = ar.alloc((4, TBM), BF16)
            b_qT = [Buf() for _ in range(4)]
            kT = ar.alloc((4, RING * 128), BF16)
            b_kT = [Buf() for _ in range(4)]
            Vaug = ar.alloc((RING, 8, 65), BF16)
            b_V = [Buf() for _ in range(RING)]
            biasT = wbuf[bi][:, 8 * IN_EVEN + 8 * D:8 * IN_EVEN + 8 * D + A_HEADS * 5 * 128].rearrange(
                "p (h j q) -> p h j q", h=A_HEADS, j=5)
            b_bias = Buf()
            probsT = [ar.alloc((5, 128), BF16) for _ in range(2)]
            b_pr = [Buf(), Buf()]
            cbias = ar.alloc((A_HEADS,), F32)
            hc = ar.alloc((4, 30 + TBM), BF16)
            b_hc = [Buf() for _ in range(4)]
            dg = ar.alloc((CONV_W, 128), BF16)
            b_dg = [Buf() for _ in range(CONV_W)]
            cacc = ar.alloc((4, TBM), F32)
            b_cacc = [Buf() for _ in range(4)]
            th = ar.alloc((512,), F32)
            b_th = Buf()
            sig = th[:, 0:TBM]
            b_sig = b_th
            cw = ar.alloc((4, CONV_W), F32)
            cb = ar.alloc((4,), F32)
            cng = ar.alloc((512,), F32)
            cnb = ar.alloc((512,), F32)
            b_cc = Buf()
            rs = [ar.alloc((8,), F32) for _ in range(2)]
            b_rs = [Buf(), Buf()]
            ao = [ar.alloc((512,), BF16) for _ in range(2)]
            b_ao = [Buf(), Buf()]
            cn = [ar.alloc((512,), F32) for _ in range(2)]
            b_cn = [Buf(), Buf()]
            cbf = [ar.alloc((512,), BF16) for _ in range(2)]
            b_cbf = [Buf(), Buf()]
            catT = [ar.alloc((8, 128), BF16) for _ in range(2)]
            b_cat = [[Buf(), Buf()], [Buf(), Buf()]]
            cb2 = ar.alloc((4,), F32)
            stc = ar.alloc((2, 6), F32)
            mvc = ar.alloc((2, 2), F32)
            sdc = ar.alloc((2, 1), F32)
            rsc = ar.alloc((2, 1), F32)
            nmc = ar.alloc((2, 1), F32)
            b_stc = Buf()
            ste = ar.alloc((2, 12), F32)
            mve = ar.alloc((2, 2), F32)
            sde = ar.alloc((2, 1), F32)
            rse = ar.alloc((2, 1), F32)
            nme = ar.alloc((2, 1), F32)
            b_ste = Buf()
            if first:
                load_xbf(c, src, src_tb, 0)
                drip(24)
            load_ln(c, 0)
            k.dma("pool", biasT.rearrange("p h j q -> p (h j q)"), biasT_in, [], [b_bias])
            k.dma("sp", cw.rearrange("p a b -> p (a b)"), convw_in, [], [b_cc])
            k.dma("sp", cb, convb_in, [], [b_cc])
            k.ve("dve", "tensor_scalar", (cb2, cb, 2.0, None), [b_cc], [b_cc], op0=ALU.mult)
            k.dma("sp", cng, convn_in[0:1, :].partition_broadcast(128), [], [b_cc])
            k.dma("sp", cnb, convn_in[1:2, :].partition_broadcast(128), [], [b_cc])
            k.dma("sp", cbias, cbias_in.partition_broadcast(128), [], [b_bias])
            for pr_, bpr_ in zip(probsT, b_pr):
                k.ve("dve", "memset", (pr_[0:64, 0, 64:128], 0.0), [], [bpr_])
            k.ve("dve", "memset", (Vaug.rearrange("p r h e -> p (r h) e")[:, :, 64:65], 1.0), [], b_V)
            if not first:
                load_xbf(c, src, src_tb, 0)
            def stage_A(b):
                bi_ = b % BPS
                transpose_block(c, all_act=True)
                if b + 1 < NB:
                    load_xbf(c, src, src_tb, b + 1)
                drip(3)
                xT, bxT = c.xT, c.b_xT
                slot0 = (2 * bi_) % RING
                if bi_ == 0:
                    k.ve("dve", "memset", (hc[:, :, 0:30], 0.0), [], b_hc)
                else:
                    k.ve("dve", "tensor_copy", (hc[:, :, 0:30], hc[:, :, TBM:TBM + 30]), b_hc, b_hc)
                for ch in range(4):
                    pa, bpa = psum1()
                    pg, bpg = psum1()
                    for kk in range(8):
                        k.mm(pa[:, 0:TBM], w_in[:, kk, 1536 + ch * 128:1536 + (ch + 1) * 128], xT[:, kk, :],
                             kk == 0, kk == 7, [bw[0], bxT[kk]], [bpa])
                    for kk in range(8):
                        k.mm(pg[:, 0:TBM], w_in[:, kk, 2048 + ch * 128:2048 + (ch + 1) * 128], xT[:, kk, :],
                             kk == 0, kk == 7, [bw[0], bxT[kk]], [bpg])
                    k.act(sig, pg[:, 0:TBM], AF.Tanh, [bpg], [b_sig], scale=0.5)
                    k.ve("dve", "scalar_tensor_tensor", (hc[:, ch, 30:30 + TBM], sig, 1.0, pa[:, 0:TBM]),
                         [b_sig, bpa], [b_hc[ch]], op0=ALU.add, op1=ALU.mult)
                    yield

            def conv_gen(b):
                for ch in range(4):
                    for wi in range(CONV_W):
                        k.ve("dve", "tensor_scalar", (dg[:, wi, :], ident_bf[:], cw[:, ch, wi:wi + 1], None),
                             [b_ident, b_cc], [b_dg[wi]], op0=ALU.mult)
                        if wi % 4 == 3:
                            yield
                    yield
                    pcv, bpcv = bank(6 + ch % 2), b_ps[6 + ch % 2]
                    for wi in range(CONV_W):
                        k.mm(pcv[:, 0:TBM], dg[:, wi, :], hc[:, ch, wi:wi + TBM], wi == 0, wi == CONV_W - 1,
                             [b_dg[wi], b_hc[ch]], [bpcv])
                        if wi % 8 == 7:
                            yield
                    k.act(cacc[:, ch, :], pcv[:, 0:TBM], AF.Identity, [bpcv, b_cc], [b_cacc[ch]],
                          bias=cb2[:, ch:ch + 1])
                    yield

            def stage_B1(b):
                bi_ = b % BPS
                xT, bxT = c.xT, c.b_xT
                slot0 = (2 * bi_) % RING
                for ch in range(4):
                    pq, bpq = psum1()
                    for kk in range(8):
                        k.mm(pq[:, 0:TBM], w_in[:, kk, ch * 128:(ch + 1) * 128], xT[:, kk, :], kk == 0, kk == 7,
                             [bw[0], bxT[kk]], [bpq])
                    k.act(qT[:, ch, :], pq[:, 0:TBM], AF.Copy, [bpq], [b_qT[ch]], scale=0.125)
                for ch in range(4):
                    pk, bpk = psum1()
                    for kk in range(8):
                        k.mm(pk[:, 0:TBM], w_in[:, kk, 512 + ch * 128:512 + (ch + 1) * 128], xT[:, kk, :],
                             kk == 0, kk == 7, [bw[0], bxT[kk]], [bpk])
                    k.act(kT[:, ch, slot0 * 128:slot0 * 128 + TBM], pk[:, 0:TBM], AF.Copy, [bpk], [b_kT[ch]])
                for t in range(2):
                    pv, bpv = psum1()
                    for kk in range(8):
                        k.mm(pv, xT[:, kk, t * 128:(t + 1) * 128], w_in[:, kk, 1024:1536], kk == 0, kk == 7,
                             [bw[0], bxT[kk]], [bpv])
                    k.act(Vaug[:, slot0 + t, :, 0:64], pv.rearrange("p (h e) -> p h e", h=8), AF.Copy,
                          [bpv], [b_V[slot0 + t]])

            def att_gen(b):
                bi_ = b % BPS
                for t in range(2):
                    m = 2 * bi_ + t
                    O = bank(4, 2)
                    bO = (b_ps[4], b_ps[5])
                    js = [j for j in range(5) if m - 4 + j >= 0]

                    def emit_pv(h, pr, bpr):
                        ob = h // 4
                        oc = (h % 4) * 65
                        for j in js:
                            slot = (m - 4 + j) % RING
                            k.mm(O[:, ob * 512 + oc:ob * 512 + oc + 65], pr[:, j, :], Vaug[:, slot, h, :],
                                 j == js[0], j == 4, [bpr, b_V[slot]], [bO[ob]])
                        if h == A_HEADS - 1:
                            for ob2 in range(2):
                                Ov = O[:, ob2 * 512:ob2 * 512 + 260].rearrange("p (h e) -> p h e", h=4)
                                k.ve("dve", "reciprocal", (rs[t][:, ob2 * 4:(ob2 + 1) * 4].unsqueeze(2), Ov[:, :, 64:65]),
                                     [bO[ob2]], [b_rs[t]])
                                k.ve("dve", "tensor_tensor",
                                     (ao[t][:, ob2 * 256:(ob2 + 1) * 256].rearrange("p (h e) -> p h e", h=4),
                                      Ov[:, :, 0:64],
                                      rs[t][:, ob2 * 4:(ob2 + 1) * 4].unsqueeze(2).to_broadcast([128, 4, 64]), ALU.mult),
                                     [bO[ob2], b_rs[t]], [b_ao[t]])

                    pend = None
                    for h in range(A_HEADS):
                        hcn, hp = h // 2, h % 2
                        prow = slice(hp * 64, (hp + 1) * 64)
                        s1, bs1 = psum1()
                        s2, bsb2 = psum1()
                        for j in js:
                            slot = (m - 4 + j) % RING
                            tgt, btg = (s1[:, j * 128:(j + 1) * 128], bs1) if j < 4 else (s2[:, 0:128], bsb2)
                            far = j < 3
                            k.mm(tgt, kT[prow, hcn, slot * 128:(slot + 1) * 128], qT[prow, hcn, t * 128:(t + 1) * 128],
                                 True, far, [b_kT[hcn], b_qT[hcn]], [btg])
                            if not far:
                                k.mm(tgt, ident_bf[:], biasT[:, h, j, :], False, True, [b_ident, b_bias], [btg])
                        pr, bpr = probsT[h % 2], b_pr[h % 2]
                        cbh = cbias[:, h:h + 1]
                        if 0 in js:
                            k.act(pr[64:128, 0, :], s1[64:128, 0:128], AF.Exp, [bs1, b_bias], [bpr], bias=cbias[64:128, h:h + 1])
                            k.act(pr[0:64, 0, 0:64], s1[0:64, 0:64], AF.Exp, [bs1, b_bias], [bpr], bias=cbias[0:64, h:h + 1])
                        jf = [j for j in js if j in (1, 2)]
                        if jf:
                            k.act(pr[:, jf[0]:3, :], s1[:, jf[0] * 128:384].rearrange("p (j q) -> p j q", q=128), AF.Exp,
                                  [bs1, b_bias], [bpr], bias=cbh)
                        if 3 in js:
                            k.act(pr[:, 3, :], s1[:, 384:512], AF.Exp, [bs1], [bpr])
                        k.act(pr[:, 4, :], s2[:, 0:128], AF.Exp, [bsb2], [bpr])
                        if pend is not None:
                            emit_pv(*pend)
                        pend = (h, pr, bpr)
                        yield
                    emit_pv(*pend)
                    yield

            def stage_C(b):
                toks = [b * 2, b * 2 + 1]
                for t in range(2):
                    pc, bpc = psum1()
                    for ch in range(4):
                        k.mm(pc[:, ch * 128:(ch + 1) * 128], cacc[:, ch, t * 128:(t + 1) * 128], ident_f[:],
                             True, True, [b_cacc[ch], b_ident], [bpc])
                    k.act(cn[t], pc, AF.Copy, [bpc], [b_cn[t]])
                    k.ve("dve", "bn_stats", (stc[:, t, :], cn[t]), [b_cn[t]], [b_stc])
                    k.ve("dve", "bn_aggr", (mvc[:, t, :], stc[:, t, :]), [b_stc], [b_stc])
                yield
                for t in range(2):
                    pt, bpt = psum1()
                    for ch in range(4):
                        k.mm(pt[:, ch * 128:(ch + 1) * 128], ao[t][:, ch * 128:(ch + 1) * 128], ident_bf[:],
                             True, True, [b_ao[t], b_ident], [bpt])
                    k.act(catT[t][:, 0:4, :], pt.rearrange("p (c q) -> p c q", c=4), AF.Copy, [bpt], [b_cat[t][0]])
                k.act(sdc, mvc[:, :, 1:2], AF.Sqrt, [b_stc], [b_stc], bias=float(4.0 * LN_EPS))
                k.ve("dve", "reciprocal", (rsc, sdc), [b_stc], [b_stc])
                k.ve("dve", "scalar_tensor_tensor", (nmc, mvc[:, :, 0:1], -1.0, rsc), [b_stc], [b_stc],
                     op0=ALU.mult, op1=ALU.mult)
                yield
                for t in range(2):
                    k.act(cn[t], cn[t], AF.Identity, [b_cn[t], b_stc], [b_cn[t]], bias=nmc[:, t, :], scale=rsc[:, t, :])
                yield
                for t in range(2):
                    k.ve("dve", "tensor_tensor", (cn[t], cn[t], cng, ALU.mult), [b_cn[t], b_cc], [b_cn[t]])
                    k.ve("dve", "tensor_tensor", (cn[t], cn[t], cnb, ALU.add), [b_cn[t], b_cc], [b_cn[t]])
                    yield
                for t in range(2):
                    k.act(th, cn[t], AF.Tanh, [b_cn[t]], [b_th], scale=0.5)
                    k.ve("dve", "scalar_tensor_tensor", (cbf[t], th, 1.0, cn[t]), [b_th, b_cn[t]], [b_cbf[t]],
                         op0=ALU.add, op1=ALU.mult)
                    yield
                Ys = []
                for t in range(2):
                    pt, bpt = psum1()
                    for ch in range(4):
                        k.mm(pt[:, ch * 128:(ch + 1) * 128], cbf[t][:, ch * 128:(ch + 1) * 128], ident_bf[:],
                             True, True, [b_cbf[t], b_ident], [bpt])
                    k.act(catT[t][:, 4:8, :], pt.rearrange("p (c q) -> p c q", c=4), AF.Copy, [bpt], [b_cat[t][1]],
                          scale=0.5)
                    yield
                es_ = []
                for t in range(2):
                    ys = [psum1(), psum1()]
                    for n in range(2):
                        for kc in range(8):
                            k.mm(ys[n][0], catT[t][:, kc, :], w_out[:, kc, n * 512:(n + 1) * 512],
                                 kc == 0, kc == 7, [b_cat[t][kc // 4], bw[1]], [ys[n][1]])
                    i = c.ep_slot.pop(toks[t])
                    e, be = c.ep[i], c.b_ep[i]
                    es_.append((e, be))
                    for n in range(2):
                        sl = slice(n * 512, (n + 1) * 512)
                        k.ve("dve", "scalar_tensor_tensor", (e[:, sl], e[:, sl], float(ALPHA), ys[n][0]),
                             [be, ys[n][1]], [be], op0=ALU.mult, op1=ALU.add)
                    yield
                    for n in range(2):
                        sl = slice(n * 512, (n + 1) * 512)
                        k.ve("dve", "bn_stats", (ste[:, t, n * 6:(n + 1) * 6], e[:, sl]), [be], [b_ste])
                    k.ve("dve", "bn_aggr", (mve[:, t, :], ste[:, t, :]), [b_ste], [b_ste])
                    yield
                k.act(sde, mve[:, :, 1:2], AF.Sqrt, [b_ste], [b_ste], bias=float(LN_EPS))
                k.ve("dve", "reciprocal", (rse, sde), [b_ste], [b_ste])
                k.ve("dve", "scalar_tensor_tensor", (nme, mve[:, :, 0:1], -1.0, rse), [b_ste], [b_ste],
                     op0=ALU.mult, op1=ALU.mult)
                yield
                for t in range(2):
                    e, be = es_[t]
                    k.act(e, e, AF.Identity, [be, b_ste], [be], bias=nme[:, t, :], scale=rse[:, t, :])
                yield
                for t in range(2):
                    e, be = es_[t]
                    k.ve("dve", "tensor_tensor", (e, e, c.lng, ALU.mult), [be, c.b_ln], [be])
                    k.ve("dve", "tensor_tensor", (e, e, c.lnb, ALU.add), [be, c.b_ln], [be])
                    rows = slice(toks[t] * 128, (toks[t] + 1) * 128)
                    k.dma("sp", dst[rows, :], e, [be], [dst_tb[toks[t]]])
                    yield
                for t in range(2):
                    ep_prefetch(c, toks[t] + 2, src, src_tb, None)

            def take(gen, n):
                if gen is None:
                    return None
                for _ in range(n):
                    if next(gen, "end") == "end":
                        return None
                return gen

            ep_prefetch(c, 0, src, src_tb, None)
            ep_prefetch(c, 1, src, src_tb, None)
            for _ in stage_A(0):
                pass
            stage_B1(0)
            cg, ag = conv_gen(0), att_gen(0)
            while cg is not None or ag is not None:
                ag = take(ag, 1)
                cg = take(cg, 8)
            if NB > 1:
                for _ in stage_A(1):
                    pass
                stage_B1(1)
            for b in range(NB):
                cg = ag = None
                if b + 1 < NB:
                    cg, ag = conv_gen(b + 1), att_gen(b + 1)
                for _ in stage_C(b):
                    cg = take(cg, 3)
                    ag = take(ag, 1)
                while ag is not None:
                    ag = take(ag, 1)
                    cg = take(cg, 4)
                while cg is not None:
                    cg = take(cg, 8)
                if b + 2 < NB:
                    for _ in stage_A(b + 2):
                        pass
                    stage_B1(b + 2)
            drip(1000)

        def mixer1_phase(bi, wts, src, src_tb, dst, dst_tb):
            w_in, w_out = wts
            bw = wbuf_b[bi]
            ar.reset()
            c = alloc_common(TBM, NEP)
            NB = T // TBM
            BPS = S // TBM
            NCH = TBM // 64
            zT = ar.alloc((TBM,), F32, parts=16)
            b_zT = Buf()
            gw = ar.alloc((512,), F32, parts=16)
            gb = ar.alloc((4,), F32)
            ngb = ar.alloc((4,), F32)
            hgT = ar.alloc((8,), F32)
            tri = ar.alloc((64,), F32)
            rmask = ar.alloc((TBM,), F32)
            b_const = Buf()
            tA = [ar.alloc((TBM,), F32) for _ in range(2)]
            nb_ = [ar.alloc((TBM,), F32) for _ in range(2)]
            eb = [ar.alloc((TBM,), F32) for _ in range(2)]
            enb = [ar.alloc((TBM,), F32) for _ in range(2)]
            b_tA = [Buf(), Buf()]
            b_nb = [Buf(), Buf()]
            b_eb = [Buf(), Buf()]
            b_enb = [Buf(), Buf()]
            kf = [ar.alloc((TBM,), F32) for _ in range(2)]
            b_kf = [Buf(), Buf()]
            qtT = ar.alloc((4, TBM), BF16)
            b_qt = [Buf() for _ in range(4)]
            ktT = ar.alloc((4, TBM), BF16)
            b_kt = [Buf() for _ in range(4)]
            kdT = ar.alloc((4, TBM), BF16)
            b_kdT = [Buf() for _ in range(4)]
            ebl = ar.alloc((4, NCH), F32)
            b_ebl = [Buf() for _ in range(4)]
            kd = ar.alloc((2, 4, 128), BF16)
            b_kd = [Buf(), Buf()]
            v = ar.alloc((2, D), BF16)
            b_v = [Buf(), Buf()]
            sgg = [ar.alloc((D,), F32) for _ in range(2)]
            b_sgg = [Buf(), Buf()]
            attnT = [ar.alloc((4, 64), BF16) for _ in range(2)]
            b_at = [Buf(), Buf()]
            Sst = ar.alloc((4, 256), F32)
            Sbf = ar.alloc((4, 256), BF16)
            b_S = [Buf() for _ in range(4)]
            b_Sbf = [Buf() for _ in range(4)]
            obf = [ar.alloc((D,), BF16) for _ in range(2)]
            b_obf = [Buf(), Buf()]
            ss = ar.alloc((2, 4), F32)
            lss = ar.alloc((2, 4), F32)
            rinv = ar.alloc((2, 4), F32)
            b_ss = Buf()
            catT = [ar.alloc((8, 128), BF16) for _ in range(2)]
            b_cat = [[Buf(), Buf()], [Buf(), Buf()]]
            ste = ar.alloc((2, 12), F32)
            mve = ar.alloc((2, 2), F32)
            lve = ar.alloc((2, 1), F32)
            rse = ar.alloc((2, 1), F32)
            nme = ar.alloc((2, 1), F32)
            b_ste = Buf()
            load_ln(c, 4)
            k.dma("sp", gw, gatew_in, [], [b_const])
            k.dma("sp", gb, gateb_in, [], [b_const])
            k.dma("sp", hgT, headgT_in, [], [b_const])
            k.dma("sp", tri, tri_in, [], [b_const])
            k.ve("dve", "tensor_scalar", (ngb, gb, -1.0, None), [b_const], [b_const], op0=ALU.mult)
            k.ve("dve", "memset", (rmask, 1.0), [], [b_const])
            k.ve("dve", "memset", (rmask.rearrange("p (c j) -> p c j", j=64)[:, :, 0:1], 0.0), [], [b_const])
            load_xbf(c, src, src_tb, 0)
            DKS = 128 ** -0.5
            O_t = [bank(4, 2), bank(6, 2)]
            bO_t = [(b_ps[4], b_ps[5]), (b_ps[6], b_ps[7])]

            def stage_FE(b):
                bi_ = b % BPS
                transpose_block(c, all_act=True)
                if b + 1 < NB:
                    load_xbf(c, src, src_tb, b + 1)
                drip(3)
                xT, bxT = c.xT, c.b_xT
                yield
                pz, bpz = psum1()
                for kk in range(8):
                    k.mm(pz[0:16, 0:TBM], w_in[:, kk, 3072:3088], xT[:, kk, :], kk == 0, kk == 7,
                         [bw[0], bxT[kk]], [bpz])
                k.act(zT, pz[0:16, 0:TBM], AF.Copy, [bpz], [b_zT])
                yield
                for hh in range(4):
                    r = hh % 2
                    pg, bpg = psum1()
                    k.mm(pg[:, 0:TBM], gw[:, hh * 128:(hh + 1) * 128], zT, True, True, [b_const, b_zT], [bpg])
                    k.act(tA[r], pg[:, 0:TBM], AF.Exp, [bpg, b_const], [b_tA[r]], scale=-1.0, bias=ngb[:, hh:hh + 1])
                    pq, bpq = psum1()
                    for kk in range(8):
                        k.mm(pq[:, 0:TBM], w_in[:, kk, hh * 128:(hh + 1) * 128], xT[:, kk, :], kk == 0, kk == 7,
                             [bw[0], bxT[kk]], [bpq])
                    pk, bpk = psum1()
                    for kk in range(8):
                        k.mm(pk[:, 0:TBM], w_in[:, kk, 512 + hh * 128:512 + (hh + 1) * 128], xT[:, kk, :],
                             kk == 0, kk == 7, [bw[0], bxT[kk]], [bpk])
                    k.act(tA[r], tA[r], AF.Ln, [b_tA[r]], [b_tA[r]], bias=1.0)
                    k.ve("dve", "tensor_tensor_scan", (nb_[r], rmask, tA[r], 0.0, ALU.mult, ALU.add),
                         [b_const, b_tA[r]], [b_nb[r]])
                    k.act(eb[r], nb_[r], AF.Exp, [b_nb[r]], [b_eb[r]], scale=-1.0 / 16.0)
                    k.act(enb[r], nb_[r], AF.Exp, [b_nb[r]], [b_enb[r]], scale=1.0 / 16.0)
                    k.ve("dve", "tensor_copy",
                         (ebl[:, hh, :].unsqueeze(2), eb[r].rearrange("p (c j) -> p c j", j=64)[:, :, 63:64]),
                         [b_eb[r]], [b_ebl[hh]])
                    k.ve("dve", "scalar_tensor_tensor", (qtT[:, hh, :], pq[:, 0:TBM], float(DKS), eb[r]),
                         [bpq, b_eb[r]], [b_qt[hh]], op0=ALU.mult, op1=ALU.mult)
                    k.ve("dve", "tensor_tensor", (kf[r], pk[:, 0:TBM], enb[r], ALU.mult), [bpk, b_enb[r]], [b_kf[r]])
                    k.act(ktT[:, hh, :], kf[r], AF.Copy, [b_kf[r]], [b_kt[hh]])
                    k.ve("dve", "tensor_tensor",
                         (kdT[:, hh, :].rearrange("p (c j) -> p c j", j=64),
                          kf[r].rearrange("p (c j) -> p c j", j=64),
                          ebl[:, hh, :].unsqueeze(2).to_broadcast([128, NCH, 64]), ALU.mult),
                         [b_kf[r], b_ebl[hh]], [b_kdT[hh]])
                    yield
                for t in range(2):
                    pkd, bpkd = psum1()
                    for hh in range(4):
                        k.mm(pkd[:, hh * 128:(hh + 1) * 128], kdT[:, hh, t * 128:(t + 1) * 128], ident_bf[:],
                             True, True, [b_kdT[hh], b_ident], [bpkd])
                    k.act(kd[:, t, :, :], pkd.rearrange("p (h f) -> p h f", h=4), AF.Copy, [bpkd], [b_kd[t]])
                    for n in range(2):
                        pv, bpv = psum1()
                        for kk in range(8):
                            k.mm(pv, xT[:, kk, t * 128:(t + 1) * 128], w_in[:, kk, 1024 + n * 512:1024 + (n + 1) * 512],
                                 kk == 0, kk == 7, [bw[0], bxT[kk]], [bpv])
                        if n == 0:
                            k.act(v[:, t, 0:512], pv, AF.Copy, [bpv], [b_v[t]])
                        else:
                            k.ve("dve", "tensor_copy", (v[:, t, 512:1024], pv), [bpv], [b_v[t]])
                    yield

            def stage_REC(b):
                bi_ = b % BPS
                if bi_ == 0:
                    k.ve("dve", "memset", (Sst, 0.0), [], b_S)
                    k.ve("dve", "memset", (Sbf, 0.0), [], b_Sbf)
                for t in range(2):
                    O, bO = O_t[t], bO_t[t]
                    for cc in range(2):
                        ci = t * 2 + cc
                        rows = slice(cc * 64, (cc + 1) * 64)
                        cols = slice(ci * 64, (ci + 1) * 64)
                        pA, bpA = psum1()
                        for hh in range(4):
                            k.mm(pA[rows, hh * 64:(hh + 1) * 64], ktT[:, hh, cols], qtT[:, hh, cols], True, True,
                                 [b_kt[hh], b_qt[hh]], [bpA])
                        dSs = [psum1(), psum1()]
                        for hh in range(4):
                            oc = slice(hh * 256, (hh + 1) * 256)
                            dsb, bdsb = dSs[hh // 2]
                            k.mm(dsb[:, (hh % 2) * 256:(hh % 2 + 1) * 256], kd[rows, t, hh, :], v[rows, t, oc], True, True,
                                 [b_kd[t], b_v[t]], [bdsb])
                        at, bat = attnT[cc], b_at[cc]
                        k.ve("dve", "tensor_tensor",
                             (at[rows, :, :], pA[rows, 0:256].rearrange("p (h i) -> p h i", h=4),
                              tri[rows, :].unsqueeze(1).to_broadcast([64, 4, 64]), ALU.mult),
                             [bpA, b_const], [bat])
                        for hh in range(4):
                            oc = slice(hh * 256, (hh + 1) * 256)
                            k.mm(O[rows, oc], at[rows, hh, :], v[rows, t, oc], True, False,
                                 [bat, b_v[t]], [bO[hh // 2]])
                            k.mm(O[rows, oc], qtT[:, hh, cols], Sbf[:, hh, :], False, True,
                                 [b_qt[hh], b_Sbf[hh]], [bO[hh // 2]])
                        for hh in range(4):
                            dsb, bdsb = dSs[hh // 2]
                            k.ve("dve", "scalar_tensor_tensor",
                                 (Sst[:, hh, :], Sst[:, hh, :], ebl[:, hh, ci:ci + 1],
                                  dsb[:, (hh % 2) * 256:(hh % 2 + 1) * 256]),
                                 [b_S[hh], b_ebl[hh], bdsb], [b_S[hh]], op0=ALU.mult, op1=ALU.add)
                            k.act(Sbf[:, hh, :], Sst[:, hh, :], AF.Copy, [b_S[hh]], [b_Sbf[hh]])
                        yield

            def stage_G(b):
                xT, bxT = c.xT, c.b_xT
                for t in range(2):
                    for n in range(2):
                        pgg, bpgg = psum1()
                        for kk in range(8):
                            k.mm(pgg, xT[:, kk, t * 128:(t + 1) * 128], w_in[:, kk, 2048 + n * 512:2048 + (n + 1) * 512],
                                 kk == 0, kk == 7, [bw[0], bxT[kk]], [bpgg])
                        k.act(sgg[t][:, n * 512:(n + 1) * 512], pgg, AF.Silu, [bpgg], [b_sgg[t]])
                        yield

            def stage_TAIL(b):
                toks = [b * 2, b * 2 + 1]
                for t in range(2):
                    O, bO = O_t[t], bO_t[t]
                    for hh in range(4):
                        k.act(obf[t][:, hh * 256:(hh + 1) * 256], O[:, hh * 256:(hh + 1) * 256], AF.Square,
                              [bO[hh // 2]], [b_obf[t], b_ss], accum_out=ss[:, t, hh:hh + 1])
                yield
                k.act(lss, ss, AF.Ln, [b_ss], [b_ss], scale=1.0 / 256.0, bias=float(RMS_EPS))
                k.act(rinv, lss, AF.Exp, [b_ss], [b_ss], scale=-0.5)
                yield
                for t in range(2):
                    O, bO = O_t[t], bO_t[t]
                    for hh in range(4):
                        oc = slice(hh * 256, (hh + 1) * 256)
                        k.ve("dve", "scalar_tensor_tensor", (obf[t][:, oc], O[:, oc], rinv[:, t, hh:hh + 1], sgg[t][:, oc]),
                             [bO[hh // 2], b_ss, b_sgg[t]], [b_obf[t]], op0=ALU.mult, op1=ALU.mult)
                    yield
                for t in range(2):
                    for half in range(2):
                        pt, bpt = psum1()
                        for ch in range(4):
                            kc = half * 4 + ch
                            k.mm(pt[:, ch * 128:(ch + 1) * 128], obf[t][:, kc * 128:(kc + 1) * 128], ident_bf[:],
                                 True, True, [b_obf[t], b_ident], [bpt])
                        k.ve("dve", "tensor_tensor",
                             (catT[t][:, half * 4:(half + 1) * 4, :], pt.rearrange("p (c q) -> p c q", c=4),
                              hgT[:, half * 4:(half + 1) * 4].unsqueeze(2).to_broadcast([128, 4, 128]), ALU.mult),
                             [bpt, b_const], [b_cat[t][half]])
                    yield
                es_ = []
                for t in range(2):
                    ys = [psum1(), psum1()]
                    for n in range(2):
                        for kc in range(8):
                            k.mm(ys[n][0], catT[t][:, kc, :], w_out[:, kc, n * 512:(n + 1) * 512],
                                 kc == 0, kc == 7, [b_cat[t][kc // 4], bw[1]], [ys[n][1]])
                    i = c.ep_slot.pop(toks[t])
                    e, be = c.ep[i], c.b_ep[i]
                    es_.append((e, be))
                    for n in range(2):
                        sl = slice(n * 512, (n + 1) * 512)
                        k.ve("dve", "scalar_tensor_tensor", (e[:, sl], e[:, sl], float(ALPHA), ys[n][0]),
                             [be, ys[n][1]], [be], op0=ALU.mult, op1=ALU.add)
                    yield
                    for n in range(2):
                        sl = slice(n * 512, (n + 1) * 512)
                        k.ve("dve", "bn_stats", (ste[:, t, n * 6:(n + 1) * 6], e[:, sl]), [be], [b_ste])
                    k.ve("dve", "bn_aggr", (mve[:, t, :], ste[:, t, :]), [b_ste], [b_ste])
                    yield
                k.act(lve, mve[:, :, 1:2], AF.Ln, [b_ste], [b_ste], bias=float(LN_EPS))
                k.act(rse, lve, AF.Exp, [b_ste], [b_ste], scale=-0.5)
                k.ve("dve", "scalar_tensor_tensor", (nme, mve[:, :, 0:1], -1.0, rse), [b_ste], [b_ste],
                     op0=ALU.mult, op1=ALU.mult)
                yield
                for t in range(2):
                    e, be = es_[t]
                    k.act(e, e, AF.Identity, [be, b_ste], [be], bias=nme[:, t, :], scale=rse[:, t, :])
                yield
                for t in range(2):
                    e, be = es_[t]
                    k.ve("dve", "tensor_tensor", (e, e, c.lng, ALU.mult), [be, c.b_ln], [be])
                    k.ve("dve", "tensor_tensor", (e, e, c.lnb, ALU.add), [be, c.b_ln], [be])
                    rows_ = slice(toks[t] * 128, (toks[t] + 1) * 128)
                    k.dma("sp", dst[rows_, :], e, [be], [dst_tb[toks[t]]])
                    yield
                for t in range(2):
                    ep_prefetch(c, toks[t] + 2, src, src_tb, None)

            def take(gen, n):
                if gen is None:
                    return None
                for _ in range(n):
                    if next(gen, "end") == "end":
                        return None
                return gen

            ep_prefetch(c, 0, src, src_tb, None)
            ep_prefetch(c, 1, src, src_tb, None)
            for _ in stage_FE(0):
                pass
            rec, gg = stage_REC(0), stage_G(0)
            while rec is not None or gg is not None:
                rec = take(rec, 1)
                gg = take(gg, 1)
            for b in range(NB):
                tail = stage_TAIL(b)
                fe = stage_FE(b + 1) if b + 1 < NB else None
                while fe is not None:
                    tail = take(tail, 1)
                    fe = take(fe, 1)
                rec = gg = None
                if b + 1 < NB:
                    rec, gg = stage_REC(b + 1), stage_G(b + 1)
                while tail is not None or rec is not None or gg is not None:
                    tail = take(tail, 1)
                    rec = take(rec, 1)
                    gg = take(gg, 1)
            drip(1000)

        if phases == ("m0", "f0", "m1", "f1"):
            w0, l0 = mixer_weight_loads(0, "even_w_in", IN_EVEN, "even_w_out", first_cols=(1536, 2560))
            w1, l1 = ffn_weight_loads(1, 0, 0)
            pending_loads.extend(l0)
            pending_loads.extend(l1)
            mixer0_phase(0, w0, x_in, xin_tb, xs[0], xs_tb[0], first=True)
            p.barrier()
            w2, l2 = ffn_weight_loads(0, 0, 1)
            pending_loads.extend(l2)
            ffn_phase(0, 0, 1, w1, xs[0], xs_tb[0], None, None, 2)
            p.barrier()
            w3, l3 = mixer_weight_loads(1, "odd_w_in", IN_ODD, "odd_w_out")
            pending_loads.extend(l3)
            ffn_phase(0, 1, 0, w2, xs[0], xs_tb[0], xs[1], xs_tb[1], 2)
            p.barrier()
            w4, l4 = ffn_weight_loads(0, 1, 0)
            pending_loads.extend(l4)
            mixer1_phase(1, w3, xs[1], xs_tb[1], xs[0], xs_tb[0])
            p.barrier()
            w5, l5 = ffn_weight_loads(1, 1, 1)
            pending_loads.extend(l5)
            ffn_phase(1, 0, 0, w4, xs[0], xs_tb[0], None, None, 6)
            p.barrier()
            ffn_phase(1, 1, 1, w5, xs[0], xs_tb[0], out_d, out_tb, 6)
        elif phases == ("f0",):
            wa, la = ffn_weight_loads(0, 0, 0)
            wb, lb = ffn_weight_loads(1, 0, 1)
            pending_loads.extend(la + lb)
            drip(1000)
            ffn_phase(0, 0, 0, wa, x_in, xin_tb, None, None, 2)
            p.barrier()
            ffn_phase(0, 1, 1, wb, x_in, xin_tb, out_d, out_tb, 2)
        elif phases == ("m1",):
            wm, lm = mixer_weight_loads(0, "odd_w_in", IN_ODD, "odd_w_out")
            pending_loads.extend(lm)
            drip(1000)
            mixer1_phase(0, wm, x_in, xin_tb, out_d, out_tb)
        elif phases == ("m0",):
            wm, lm = mixer_weight_loads(0, "even_w_in", IN_EVEN, "even_w_out")
            pending_loads.extend(lm)
            drip(1000)
            mixer0_phase(0, wm, x_in, xin_tb, out_d, out_tb)
        p.finalize_and_emit()
    return nc


def make_lnv(inp):
    return np.ascontiguousarray(np.stack([
        inp["mix_norm_g"][0], inp["mix_norm_b"][0], inp["ffn_norm_g"][0], inp["ffn_norm_b"][0],
        inp["mix_norm_g"][1], inp["mix_norm_b"][1], inp["ffn_norm_g"][1], inp["ffn_norm_b"][1]], 0).astype(np.float32))


def make_biasT(rel_bias):
    rb = np.asarray(rel_bias, dtype=np.float32)
    ki = np.arange(128)[:, None]
    qi = np.arange(128)[None, :]
    out = np.empty((128, A_HEADS, 5, 128), np.float32)
    for j in range(5):
        idx = np.clip(128 * (4 - j) + qi - ki, -128, 128) + 128
        t = rb[:, idx]
        out[:, :, j, :] = np.transpose(t, (1, 0, 2))
    out[0:64, :, 0, 64:128] = NEGM
    out[64:128, :, 4, 0:64] = NEGM
    return np.ascontiguousarray(out.reshape(128, A_HEADS * 5 * 128))


def make_in_maps(inp, nseq=NSEQ, ncores=NCORES):
    x = np.asarray(inp["x"], dtype=np.float32)
    cw = np.asarray(inp["even_conv_w"][0], np.float32)
    common = {
        "even_w_in": np.ascontiguousarray(inp["even_w_in"][0]),
        "even_w_out": np.ascontiguousarray(inp["even_w_out"][0]),
        "odd_w_in": np.ascontiguousarray(inp["odd_w_in"][0]),
        "odd_w_out": np.ascontiguousarray(inp["odd_w_out"][0]),
        "lnv": make_lnv(inp),
        "ident": np.eye(128, dtype=np.float32),
        "biasT": make_biasT(inp["even_rel_bias"][0]),
        "convw": np.ascontiguousarray(cw.T.reshape(4, 128, CONV_W).transpose(1, 0, 2).reshape(128, 4 * CONV_W)),
        "convb": np.ascontiguousarray(np.asarray(inp["even_conv_b"][0], np.float32).reshape(4, 128).T),
        "cbias": np.ascontiguousarray(np.asarray(inp["even_rel_bias"][0], np.float32)[:, 2 * 128].reshape(1, A_HEADS)),
        "convn": np.ascontiguousarray(np.stack([inp["even_conv_norm_g"][0], inp["even_conv_norm_b"][0]], 0)
                                      .astype(np.float32)),
        "gatew": np.ascontiguousarray(np.asarray(inp["odd_gate_w"][0], np.float32)),
        "gateb": np.ascontiguousarray(np.asarray(inp["odd_gate_b"][0], np.float32).reshape(4, 128).T),
        "headg": np.ascontiguousarray(np.asarray(inp["odd_head_norm_g"][0], np.float32).reshape(1, D)),
        "headgT": np.ascontiguousarray(np.asarray(inp["odd_head_norm_g"][0], np.float32).reshape(8, 128).T),
        "tri": np.ascontiguousarray((np.arange(128)[:, None] % 64 <= np.arange(64)[None, :]).astype(np.float32)),
    }
    for l in range(2):
        common["wg%d" % l] = np.ascontiguousarray(inp["ffn_w_gate"][l])
        common["wu%d" % l] = np.ascontiguousarray(inp["ffn_w_up"][l])
        common["wd%d" % l] = np.ascontiguousarray(inp["ffn_w_down"][l])
    maps = []
    for c_ in range(ncores):
        m = dict(common)
        m["x"] = np.ascontiguousarray(x[c_ * nseq:(c_ + 1) * nseq].reshape(nseq * S, D))
        maps.append(m)
    return maps


def kernel(**inputs):
    inp = {k_: np.asarray(v) for k_, v in inputs.items()}
    nc = build()
    in_maps = make_in_maps(inp)
    res = run_bass_kernel_spmd(nc, in_maps, core_ids=list(range(NCORES)))
    outs = [np.asarray(r["out"]).reshape(NSEQ, S, D) for r in res.results]
    return np.concatenate(outs, axis=0).astype(np.float32)
```

```python
import contextlib
import numpy as np
import concourse.bass as bass
import concourse.mybir as mybir
from concourse.bass_utils import run_bass_kernel_spmd

F32 = mybir.dt.float32
BF16 = mybir.dt.bfloat16
AF = mybir.ActivationFunctionType
ALU = mybir.AluOpType

D = 1024
S = 2048
NSEQ = 2
NCORES = 8
DFF = 2816
HF = DFF // 2
NJ = HF // 128
ALPHA = 4 ** 0.25
LN_EPS = 1e-5
RMS_EPS = 1e-6
TB = 512
TBM = 256
WCOLS = 3 * 8 * HF
ACOLS = 38000
A_HEADS = 8
CONV_W = 31
NEGM = -30000.0
STRICT = False
IN_EVEN = 2560
IN_ODD = 3088


class Op:
    __slots__ = ("eng", "fn", "deps", "marked", "sem", "val", "dma", "key")


class Buf:
    __slots__ = ("name", "writers", "readers")

    def __init__(self, name=""):
        self.name = name
        self.writers = {}
        self.readers = {}


class Prog:
    ENG = ("pe", "act", "dve", "pool", "sp")
    NDQ = 8

    def __init__(self, nc, es):
        self.nc = nc
        self.streams = {e: [] for e in self.ENG}
        self.sems = {e: es.enter_context(nc.semaphore("s_" + e)) for e in ("pe", "act", "dve", "pool")}
        self.dq = {q: [es.enter_context(nc.semaphore("d_%s%d" % (q, i))) for i in range(self.NDQ)]
                   for q in ("sp", "pool", "act")}
        self.dq_n = {q: 0 for q in self.dq}
        self.dq_cnt = {q: [0] * self.NDQ for q in self.dq}
        self.dq_last = {q: [None] * self.NDQ for q in self.dq}
        self.pending = {}

    def barrier(self):
        lasts = []
        for e in ("pe", "act", "dve", "pool"):
            for o in reversed(self.streams[e]):
                if not o.dma:
                    lasts.append(o)
                    break
        for q in self.dq:
            for last in self.dq_last[q]:
                if last is not None:
                    lasts.append(last)
        self.pending = {e: list(lasts) for e in self.ENG}

    def op(self, eng, fn, reads=(), writes=(), dma=False):
        o = Op()
        o.eng, o.fn, o.deps, o.marked, o.dma = eng, fn, [], False, dma
        o.sem = None
        o.val = 0
        if dma:
            i = self.dq_n[eng]
            self.dq_n[eng] += 1
            slot = i % self.NDQ
            o.key = (eng, slot)
            prev = self.dq_last[eng][slot]
            if prev is not None:
                o.deps.append(prev)
            o.sem = self.dq[eng][slot]
            self.dq_cnt[eng][slot] += 16
            o.val = self.dq_cnt[eng][slot]
            self.dq_last[eng][slot] = o
        else:
            o.key = eng
        seen = set()

        def add(d, raw):
            if d is o or id(d) in seen:
                return
            if not dma and not d.dma and d.eng == eng:
                if eng == "pe" or not (raw or STRICT):
                    return
            seen.add(id(d))
            o.deps.append(d)

        pend = self.pending.pop(eng, None)
        if pend:
            for d in pend:
                add(d, False)
        for b in reads:
            for d in b.writers.values():
                add(d, True)
        for b in writes:
            for d in b.writers.values():
                add(d, False)
            for d in b.readers.values():
                add(d, False)
        for b in reads:
            b.readers[o.key] = o
        for b in writes:
            if b.readers:
                b.readers = {}
                b.writers = {}
            b.writers[o.key] = o
        self.streams[eng].append(o)
        return o

    def finalize_and_emit(self):
        nc = self.nc
        for e in self.ENG:
            for o in self.streams[e]:
                for d in o.deps:
                    if not d.dma:
                        d.marked = True
        for e in ("pe", "act", "dve", "pool"):
            c = 0
            for o in self.streams[e]:
                if not o.dma and o.marked:
                    c += 1
                    o.sem = self.sems[e]
                    o.val = c
        streams = self.streams

        def emit(ename, eng):
            waited = {}
            for o in streams[ename]:
                need = {}
                for d in o.deps:
                    k = id(d.sem)
                    if waited.get(k, 0) >= d.val:
                        continue
                    if k not in need or need[k][1] < d.val:
                        need[k] = (d.sem, d.val)
                for k, (sem, val) in need.items():
                    eng.wait_ge(sem, val)
                    waited[k] = val
                ins = o.fn(eng)
                if o.dma:
                    ins.then_inc(o.sem, 16)
                elif o.marked:
                    ins.then_inc(o.sem, 1)
            if ename in self.dq:
                for slot in range(self.NDQ):
                    last = self.dq_last[ename][slot]
                    if last is not None and waited.get(id(last.sem), 0) < last.val:
                        eng.wait_ge(last.sem, last.val)

        with nc.Block() as block:
            @block.tensor
            def _(e):
                emit("pe", e)

            @block.scalar
            def _(e):
                emit("act", e)

            @block.vector
            def _(e):
                emit("dve", e)

            @block.gpsimd
            def _(e):
                emit("pool", e)

            @block.sync
            def _(e):
                emit("sp", e)


class K:
    def __init__(self, nc, es):
        self.nc = nc
        self.p = Prog(nc, es)
        self.nbuf = 0

    def sb(self, name, shape, dt):
        return self.nc.alloc_sbuf_tensor(name, list(shape), dt)

    def mm(self, out, lhsT, rhs, start, stop, reads, writes):
        return self.p.op("pe", lambda e: e.matmul(out, lhsT, rhs, start=start, stop=stop), reads, writes)

    def act(self, out, in_, func, reads, writes, bias=None, scale=None, accum_out=None):
        kw = {}
        if bias is not None:
            kw["bias"] = bias
        if scale is not None:
            kw["scale"] = scale
        if accum_out is not None:
            kw["accum_out"] = accum_out
        return self.p.op("act", lambda e: e.activation(out, in_, func, **kw), reads, writes)

    def ve(self, eng, name, args, reads, writes, **kw):
        return self.p.op(eng, lambda e: getattr(e, name)(*args, **kw), reads, writes)

    def dma(self, q, out, in_, reads, writes):
        return self.p.op(q, lambda e: e.dma_start(out=out, in_=in_), reads, writes, dma=True)


class Arena:
    def __init__(self, t, ncols):
        self.t = t
        self.n = ncols
        self.off = 0

    def reset(self):
        self.off = 0

    def alloc(self, shape, dt, parts=128):
        n = 1
        for s_ in shape:
            n *= s_
        cols = n * (2 if dt == F32 else 1)
        cols = (cols + 15) // 16 * 16
        assert self.off + cols <= self.n, ("arena overflow", self.off, cols, self.n)
        ap = self.t[0:parts, self.off:self.off + cols]
        self.off += cols
        if dt == F32:
            ap = ap.bitcast(F32)
        ap = ap[:, 0:n]
        if len(shape) == 2:
            ap = ap.rearrange("p (a b) -> p a b", a=shape[0])
        elif len(shape) == 3:
            ap = ap.rearrange("p (a b c) -> p a b c", a=shape[0], b=shape[1])
        return ap


def build(cfg=None):
    cfg = cfg or {}
    nseq = cfg.get("nseq", NSEQ)
    T = nseq * S
    NT = T // 128
    phases = cfg.get("phases", ("m0", "f0", "m1", "f1"))
    nc = bass.Bass("TRN2", target_bir_lowering=False)

    def din(name, shape):
        return nc.dram_tensor(name, list(shape), F32, kind="ExternalInput").ap()

    x_in = din("x", (T, D))
    w = {}
    w["even_w_in"] = din("even_w_in", (D, IN_EVEN))
    w["even_w_out"] = din("even_w_out", (D, D))
    w["odd_w_in"] = din("odd_w_in", (D, IN_ODD))
    w["odd_w_out"] = din("odd_w_out", (D, D))
    for l in range(2):
        w["wg%d" % l] = din("wg%d" % l, (D, DFF))
        w["wu%d" % l] = din("wu%d" % l, (D, DFF))
        w["wd%d" % l] = din("wd%d" % l, (DFF, D))
    lnv = din("lnv", (8, D))
    ident_in = din("ident", (128, 128))
    biasT_in = din("biasT", (128, A_HEADS * 5 * 128))
    convw_in = din("convw", (128, 4 * CONV_W))
    convb_in = din("convb", (128, 4))
    convn_in = din("convn", (2, 512))
    cbias_in = din("cbias", (1, A_HEADS))
    gatew_in = din("gatew", (16, 512))
    gateb_in = din("gateb", (128, 4))
    headg_in = din("headg", (1, D))
    headgT_in = din("headgT", (128, 8))
    tri_in = din("tri", (128, 64))
    out_d = nc.dram_tensor("out", [T, D], F32, kind="ExternalOutput").ap()
    xs = [nc.dram_tensor("xs%d" % i, [T, D], F32, kind="Internal").ap() for i in range(2)]
    ya = nc.dram_tensor("ya", [T, D], F32, kind="Internal").ap()
    xs_tb = [[Buf() for _ in range(NT)] for _ in range(2)]
    ya_tb = [Buf() for _ in range(NT)]
    xin_tb = [Buf() for _ in range(NT)]
    out_tb = [Buf() for _ in range(NT)]

    es = contextlib.ExitStack()
    with es:
        k = K(nc, es)
        p = k.p
        wbuf = [k.sb("wbuf%d" % i, (128, WCOLS), BF16) for i in range(2)]
        wbuf_b = [[Buf("wb%d_%d" % (i, j)) for j in range(4)] for i in range(2)]
        ident_bf = k.sb("ident_bf", (128, 128), BF16)
        ident_f = k.sb("ident_f", (128, 128), F32)
        b_ident = Buf("ident")
        NEP = 2
        stt = [k.sb("stt%d" % i, (128, 16), F32) for i in range(4)]
        b_stt = [Buf() for _ in range(4)]
        arena_t = k.sb("arena", (128, ACOLS), BF16)
        ar = Arena(arena_t, ACOLS)
        ps_all = nc.alloc_psum_tensor("ps_all", [128, 4096], F32)
        b_ps = [Buf("ps%d" % i) for i in range(8)]

        def bank(i, n=1):
            return ps_all[:, i * 512:(i + n) * 512]

        ps_rr = [0]

        def psum1():
            i = ps_rr[0] % 4
            ps_rr[0] += 1
            return bank(i), b_ps[i]

        k.dma("pool", ident_bf[:], ident_in, [], [b_ident])
        k.dma("sp", ident_f[:], ident_in, [], [b_ident])

        def wview(bi, off, kk, n):
            return wbuf[bi][:, off:off + kk * n].rearrange("p (k n) -> p k n", k=kk)

        def ffn_weight_loads(bi, l, hf):
            wg = wview(bi, 0, 8, HF)
            wu = wview(bi, 8 * HF, 8, HF)
            wd = wview(bi, 16 * HF, NJ, D)
            loads = []
            for kk in range(8):
                loads.append(lambda kk=kk: k.dma(
                    "pool", wg[:, kk, :], w["wg%d" % l][kk * 128:(kk + 1) * 128, hf * HF:(hf + 1) * HF],
                    [], [wbuf_b[bi][0]]))
                loads.append(lambda kk=kk: k.dma(
                    "pool", wu[:, kk, :], w["wu%d" % l][kk * 128:(kk + 1) * 128, hf * HF:(hf + 1) * HF],
                    [], [wbuf_b[bi][1]]))
            for j in range(NJ):
                r0 = hf * HF + j * 128
                loads.append(lambda j=j, r0=r0: k.dma("pool", wd[:, j, :], w["wd%d" % l][r0:r0 + 128, :],
                                                     [], [wbuf_b[bi][2]]))
            return (wg, wu, wd), loads

        def mixer_weight_loads(bi, name_in, nin, name_out, first_cols=None):
            w_in = wview(bi, 0, 8, nin)
            w_out = wview(bi, 8 * nin, 8, D)
            loads = []
            if first_cols is not None:
                c0, c1 = first_cols
                for kk in range(8):
                    loads.append(lambda kk=kk: k.dma("pool", w_in[:, kk, c0:c1], w[name_in][kk * 128:(kk + 1) * 128, c0:c1],
                                                     [], [wbuf_b[bi][0]]))
                for kk in range(8):
                    loads.append(lambda kk=kk: k.dma("pool", w_in[:, kk, 0:c0], w[name_in][kk * 128:(kk + 1) * 128, 0:c0],
                                                     [], [wbuf_b[bi][0]]))
                    if c1 < nin:
                        loads.append(lambda kk=kk: k.dma("pool", w_in[:, kk, c1:nin],
                                                         w[name_in][kk * 128:(kk + 1) * 128, c1:nin],
                                                         [], [wbuf_b[bi][0]]))
            else:
                for kk in range(8):
                    loads.append(lambda kk=kk: k.dma("pool", w_in[:, kk, :], w[name_in][kk * 128:(kk + 1) * 128, :],
                                                     [], [wbuf_b[bi][0]]))
            for kk in range(8):
                loads.append(lambda kk=kk: k.dma("pool", w_out[:, kk, :], w[name_out][kk * 128:(kk + 1) * 128, :],
                                                 [], [wbuf_b[bi][1]]))
            return (w_in, w_out), loads

        pending_loads = []

        def drip(n):
            for _ in range(n):
                if pending_loads:
                    pending_loads.pop(0)()

        class Ctx:
            pass

        def alloc_common(tb, nep):
            c = Ctx()
            c.tb = tb
            c.ntl = tb // 128
            c.lng = ar.alloc((D,), F32)
            c.lnb = ar.alloc((D,), F32)
            c.b_ln = Buf("ln")
            c.xbf = ar.alloc((c.ntl, D), BF16)
            c.b_xbf = Buf("xbf")
            c.xT = ar.alloc((8, tb), BF16)
            c.b_xT = [Buf() for _ in range(8)]
            c.nep = nep
            c.ep = [ar.alloc((D,), F32) for _ in range(nep)]
            c.b_ep = [Buf() for _ in range(nep)]
            c.ep_n = 0
            c.ep_slot = {}
            return c

        def load_ln(c, row):
            k.dma("sp", c.lng, lnv[row:row + 1, :].partition_broadcast(128), [], [c.b_ln])
            k.dma("sp", c.lnb, lnv[row + 1:row + 2, :].partition_broadcast(128), [], [c.b_ln])

        def load_xbf(c, src, src_tb, b):
            v = src[b * c.tb:(b + 1) * c.tb, :].rearrange("(t p) d -> p t d", p=128)
            k.dma("pool", c.xbf, v, [src_tb[b * c.ntl + t] for t in range(c.ntl)], [c.b_xbf])

        def transpose_block(c, all_act=False):
            ntl = c.ntl
            for kk in range(8):
                pt, bpt = psum1()
                for t in range(ntl):
                    k.mm(pt[:, t * 128:(t + 1) * 128], c.xbf[:, t, kk * 128:(kk + 1) * 128], ident_bf[:],
                         True, True, [c.b_xbf, b_ident], [bpt])
                if all_act or kk % 2 == 0:
                    k.act(c.xT[:, kk, :], pt[:, 0:c.tb], AF.Copy, [bpt], [c.b_xT[kk]])
                else:
                    k.ve("dve", "tensor_copy", (c.xT[:, kk, :], pt[:, 0:c.tb]), [bpt], [c.b_xT[kk]])

        def ep_prefetch(c, tok, src, src_tb, ex):
            if tok >= NT or tok in c.ep_slot:
                return
            i = c.ep_n % c.nep
            c.ep_n += 1
            c.ep_slot[tok] = i
            rows = slice(tok * 128, (tok + 1) * 128)
            k.dma("sp", c.ep[i], src[rows, :], [src_tb[tok]], [c.b_ep[i]])
            if ex is not None:
                yat, b_yat = ex
                k.dma("sp", yat[i], ya[rows, :], [ya_tb[tok]], [b_yat[i]])

        def ln_epilogue(c, tok, Y, bY, src, src_tb, ex, dst, dst_tb):
            ep_prefetch(c, tok, src, src_tb, ex)
            i = c.ep_slot.pop(tok)
            ep_prefetch(c, tok + 1, src, src_tb, ex)
            e, be = c.ep[i], c.b_ep[i]
            st, bs = stt[i], b_stt[i]
            rows = slice(tok * 128, (tok + 1) * 128)
            for n in range(2):
                sl = slice(n * 512, (n + 1) * 512)
                k.ve("dve", "scalar_tensor_tensor", (e[:, sl], e[:, sl], float(ALPHA), Y[:, sl]),
                     [be, bY[n]], [be], op0=ALU.mult, op1=ALU.add)
            if ex is not None:
                yat, b_yat = ex
                k.ve("dve", "tensor_tensor", (e, e, yat[i], ALU.add), [be, b_yat[i]], [be])
            for n in range(2):
                sl = slice(n * 512, (n + 1) * 512)
                k.ve("dve", "bn_stats", (st[:, n * 6:(n + 1) * 6], e[:, sl]), [be], [bs])
            k.ve("dve", "bn_aggr", (st[:, 12:14], st[:, 0:12]), [bs], [bs])
            k.act(st[:, 11:12], st[:, 13:14], AF.Sqrt, [bs], [bs], bias=float(LN_EPS))
            k.ve("dve", "reciprocal", (st[:, 14:15], st[:, 11:12]), [bs], [bs])
            k.ve("dve", "scalar_tensor_tensor", (st[:, 15:16], st[:, 12:13], -1.0, st[:, 14:15]),
                 [bs], [bs], op0=ALU.mult, op1=ALU.mult)
            k.act(e, e, AF.Identity, [be, bs], [be], bias=st[:, 15:16], scale=st[:, 14:15])
            k.ve("dve", "tensor_tensor", (e, e, c.lng, ALU.mult), [be, c.b_ln], [be])
            k.ve("dve", "tensor_tensor", (e, e, c.lnb, ALU.add), [be, c.b_ln], [be])
            k.dma("sp", dst[rows, :], e, [be], [dst_tb[tok]])

        def ffn_phase(l, hf, bi, wts, src, src_tb, dst, dst_tb, lnrow):
            wg, wu, wd = wts
            bw = wbuf_b[bi]
            ar.reset()
            nep = 2 if hf == 0 else 4
            c = alloc_common(TB, nep)
            NB = T // TB
            hT = ar.alloc((NJ, TB), BF16)
            b_hT = [Buf() for _ in range(NJ)]
            sg = [ar.alloc((TB,), F32) for _ in range(2)]
            b_sg = [Buf(), Buf()]
            ste = ar.alloc((4, 12), F32)
            mve = ar.alloc((4, 2), F32)
            sde = ar.alloc((4, 1), F32)
            rse = ar.alloc((4, 1), F32)
            nme = ar.alloc((4, 1), F32)
            b_ste = Buf()
            if hf == 1:
                load_ln(c, lnrow)
            esrc, esrc_tb = (src, src_tb) if hf == 0 else (ya, ya_tb)
            load_xbf(c, src, src_tb, 0)
            if hf == 0:
                ep_prefetch(c, 0, esrc, esrc_tb, None)
            else:
                for t in range(4):
                    ep_prefetch(c, t, esrc, esrc_tb, None)
            yn = 0
            deferred = []
            transpose_block(c, all_act=True)
            if NB > 1:
                load_xbf(c, src, src_tb, 1)
            for b in range(NB):
                drip(4)
                for j in range(NJ):
                    pg, bpg = psum1()
                    pu, bpu = psum1()
                    for kk in range(8):
                        k.mm(pg, wg[:, kk, j * 128:(j + 1) * 128], c.xT[:, kk, :], kk == 0, kk == 7,
                             [bw[0], c.b_xT[kk]], [bpg])
                    for kk in range(8):
                        k.mm(pu, wu[:, kk, j * 128:(j + 1) * 128], c.xT[:, kk, :], kk == 0, kk == 7,
                             [bw[1], c.b_xT[kk]], [bpu])
                    si = j % 2
                    k.act(sg[si], pg, AF.Silu, [bpg], [b_sg[si]])
                    k.ve("dve", "tensor_tensor", (hT[:, j, :], sg[si], pu, ALU.mult),
                         [b_sg[si], bpu], [b_hT[j]])
                    if j == 3 and deferred:
                        deferred.pop(0)()
                es_ = []
                for t in range(4):
                    tok = b * 4 + t
                    yb = 4 + 2 * (yn % 2)
                    yn += 1
                    Y = bank(yb, 2)
                    bY = (b_ps[yb], b_ps[yb + 1])
                    for n in range(2):
                        for j in range(NJ):
                            k.mm(Y[:, n * 512:(n + 1) * 512], hT[:, j, t * 128:(t + 1) * 128],
                                 wd[:, j, n * 512:(n + 1) * 512], j == 0, j == NJ - 1, [b_hT[j], bw[2]], [bY[n]])
                    rows = slice(tok * 128, (tok + 1) * 128)
                    i = c.ep_slot.pop(tok)
                    e, be = c.ep[i], c.b_ep[i]
                    if hf == 0:
                        ep_prefetch(c, tok + 1, esrc, esrc_tb, None)
                        for n in range(2):
                            sl = slice(n * 512, (n + 1) * 512)
                            k.ve("dve", "scalar_tensor_tensor", (e[:, sl], e[:, sl], float(ALPHA), Y[:, sl]),
                                 [be, bY[n]], [be], op0=ALU.mult, op1=ALU.add)
                        k.dma("sp", ya[rows, :], e, [be], [ya_tb[tok]])
                    else:
                        for n in range(2):
                            sl = slice(n * 512, (n + 1) * 512)
                            k.ve("dve", "tensor_tensor", (e[:, sl], e[:, sl], Y[:, sl], ALU.add),
                                 [be, bY[n]], [be])
                        for n in range(2):
                            sl = slice(n * 512, (n + 1) * 512)
                            k.ve("dve", "bn_stats", (ste[:, t, n * 6:(n + 1) * 6], e[:, sl]), [be], [b_ste])
                        k.ve("dve", "bn_aggr", (mve[:, t, :], ste[:, t, :]), [b_ste], [b_ste])
                        es_.append((e, be, tok))
                if b + 1 < NB:
                    transpose_block(c, all_act=True)
                    if b + 2 < NB:
                        load_xbf(c, src, src_tb, b + 2)
                if hf == 1:
                    k.act(sde, mve[:, :, 1:2], AF.Sqrt, [b_ste], [b_ste], bias=float(LN_EPS))
                    k.ve("dve", "reciprocal", (rse, sde), [b_ste], [b_ste])
                    k.ve("dve", "scalar_tensor_tensor", (nme, mve[:, :, 0:1], -1.0, rse), [b_ste], [b_ste],
                         op0=ALU.mult, op1=ALU.mult)
                    for t in range(4):
                        e, be, tok = es_[t]
                        k.act(e, e, AF.Identity, [be, b_ste], [be], bias=nme[:, t, :], scale=rse[:, t, :])

                    def fin(es_=es_, b=b):
                        for (e, be, tok) in es_:
                            k.ve("dve", "tensor_tensor", (e, e, c.lng, ALU.mult), [be, c.b_ln], [be])
                            k.ve("dve", "tensor_tensor", (e, e, c.lnb, ALU.add), [be, c.b_ln], [be])
                            rows = slice(tok * 128, (tok + 1) * 128)
                            k.dma("sp", dst[rows, :], e, [be], [dst_tb[tok]])
                        for (e, be, tok) in es_:
                            ep_prefetch(c, tok + 4, esrc, esrc_tb, None)
                    deferred.append(fin)
            while deferred:
                deferred.pop(0)()
            drip(1000)

        def mixer0_phase(bi, wts, src, src_tb, dst, dst_tb, first=False):
            w_in, w_out = wts
            bw = wbuf_b[bi]
            ar.reset()
            c = alloc_common(TBM, NEP)
            NB = T // TBM
            BPS = S // TBM
            RING = 6
            qT = ar.alloc((4, TBM), BF16)
            b_qT = [Buf() for _ in range(4)]
            kT = ar.alloc((4, RING * 128), BF16)
            b_kT = [Buf() for _ in range(4)]
            Vaug = ar.alloc((RING, 8, 65), BF16)
            b_V = [Buf() for _ in range(RING)]
            biasT = wbuf[bi][:, 8 * IN_EVEN + 8 * D:8 * IN_EVEN + 8 * D + A_HEADS * 5 * 128].rearrange(
                "p (h j q) -> p h j q", h=A_HEADS, j=5)
            b_bias = Buf()
            probsT = [ar.alloc((5, 128), BF16) for _ in range(2)]
            b_pr = [Buf(), Buf()]
            cbias = ar.alloc((A_HEADS,), F32)
            hc = ar.alloc((4, 30 + TBM), BF16)
            b_hc = [Buf() for _ in range(4)]
            dg = ar.alloc((CONV_W, 128), BF16)
            b_dg = [Buf() for _ in range(CONV_W)]
            cacc = ar.alloc((4, TBM), F32)
            b_cacc = [Buf() for _ in range(4)]
            th = ar.alloc((512,), F32)
            b_th = Buf()
            sig = th[:, 0:TBM]
            b_sig = b_th
            cw = ar.alloc((4, CONV_W), F32)
            cb = ar.alloc((4,), F32)
            cng = ar.alloc((512,), F32)
            cnb = ar.alloc((512,), F32)
            b_cc = Buf()
            rs = [ar.alloc((8,), F32) for _ in range(2)]
            b_rs = [Buf(), Buf()]
            ao = [ar.alloc((512,), BF16) for _ in range(2)]
            b_ao = [Buf(), Buf()]
            cn = [ar.alloc((512,), F32) for _ in range(2)]
            b_cn = [Buf(), Buf()]
            cbf = [ar.alloc((512,), BF16) for _ in range(2)]
            b_cbf = [Buf(), Buf()]
            catT = [ar.alloc((8, 128), BF16) for _ in range(2)]
            b_cat = [[Buf(), Buf()], [Buf(), Buf()]]
            cb2 = ar.alloc((4,), F32)
            stc = ar.alloc((2, 6), F32)
            mvc = ar.alloc((2, 2), F32)
            sdc = ar.alloc((2, 1), F32)
            rsc = ar.alloc((2, 1), F32)
            nmc = ar.alloc((2, 1), F32)
            b_stc = Buf()
            ste = ar.alloc((2, 12), F32)
            mve = ar.alloc((2, 2), F32)
            sde = ar.alloc((2, 1), F32)
            rse = ar.alloc((2, 1), F32)
            nme = ar.alloc((2, 1), F32)
            b_ste = Buf()
            if first:
                load_xbf(c, src, src_tb, 0)
                drip(24)
            load_ln(c, 0)
            k.dma("pool", biasT.rearrange("p h j q -> p (h j q)"), biasT_in, [], [b_bias])
            k.dma("sp", cw.rearrange("p a b -> p (a b)"), convw_in, [], [b_cc])
            k.dma("sp", cb, convb_in, [], [b_cc])
            k.ve("dve", "tensor_scalar", (cb2, cb, 2.0, None), [b_cc], [b_cc], op0=ALU.mult)
            k.dma("sp", cng, convn_in[0:1, :].partition_broadcast(128), [], [b_cc])
            k.dma("sp", cnb, convn_in[1:2, :].partition_broadcast(128), [], [b_cc])
            k.dma("sp", cbias, cbias_in.partition_broadcast(128), [], [b_bias])
            for pr_, bpr_ in zip(probsT, b_pr):
                k.ve("dve", "memset", (pr_[0:64, 0, 64:128], 0.0), [], [bpr_])
            k.ve("dve", "memset", (Vaug.rearrange("p r h e -> p (r h) e")[:, :, 64:65], 1.0), [], b_V)
            if not first:
                load_xbf(c, src, src_tb, 0)
            def stage_A(b):
                bi_ = b % BPS
                transpose_block(c, all_act=True)
                if b + 1 < NB:
                    load_xbf(c, src, src_tb, b + 1)
                drip(3)
                xT, bxT = c.xT, c.b_xT
                slot0 = (2 * bi_) % RING
                if bi_ == 0:
                    k.ve("dve", "memset", (hc[:, :, 0:30], 0.0), [], b_hc)
                else:
                    k.ve("dve", "tensor_copy", (hc[:, :, 0:30], hc[:, :, TBM:TBM + 30]), b_hc, b_hc)
                for ch in range(4):
                    pa, bpa = psum1()
                    pg, bpg = psum1()
                    for kk in range(8):
                        k.mm(pa[:, 0:TBM], w_in[:, kk, 1536 + ch * 128:1536 + (ch + 1) * 128], xT[:, kk, :],
                             kk == 0, kk == 7, [bw[0], bxT[kk]], [bpa])
                    for kk in range(8):
                        k.mm(pg[:, 0:TBM], w_in[:, kk, 2048 + ch * 128:2048 + (ch + 1) * 128], xT[:, kk, :],
                             kk == 0, kk == 7, [bw[0], bxT[kk]], [bpg])
                    k.act(sig, pg[:, 0:TBM], AF.Tanh, [bpg], [b_sig], scale=0.5)
                    k.ve("dve", "scalar_tensor_tensor", (hc[:, ch, 30:30 + TBM], sig, 1.0, pa[:, 0:TBM]),
                         [b_sig, bpa], [b_hc[ch]], op0=ALU.add, op1=ALU.mult)
                    yield

            def conv_gen(b):
                for ch in range(4):
                    for wi in range(CONV_W):
                        k.ve("dve", "tensor_scalar", (dg[:, wi, :], ident_bf[:], cw[:, ch, wi:wi + 1], None),
                             [b_ident, b_cc], [b_dg[wi]], op0=ALU.mult)
                        if wi % 4 == 3:
                            yield
                    yield
                    pcv, bpcv = bank(6 + ch % 2), b_ps[6 + ch % 2]
                    for wi in range(CONV_W):
                        k.mm(pcv[:, 0:TBM], dg[:, wi, :], hc[:, ch, wi:wi + TBM], wi == 0, wi == CONV_W - 1,
                             [b_dg[wi], b_hc[ch]], [bpcv])
                        if wi % 8 == 7:
                            yield
                    k.act(cacc[:, ch, :], pcv[:, 0:TBM], AF.Identity, [bpcv, b_cc], [b_cacc[ch]],
                          bias=cb2[:, ch:ch + 1])
                    yield

            def stage_B1(b):
                bi_ = b % BPS
                xT, bxT = c.xT, c.b_xT
                slot0 = (2 * bi_) % RING
                for ch in range(4):
                    pq, bpq = psum1()
                    for kk in range(8):
                        k.mm(pq[:, 0:TBM], w_in[:, kk, ch * 128:(ch + 1) * 128], xT[:, kk, :], kk == 0, kk == 7,
                             [bw[0], bxT[kk]], [bpq])
                    k.act(qT[:, ch, :], pq[:, 0:TBM], AF.Copy, [bpq], [b_qT[ch]], scale=0.125)
                for ch in range(4):
                    pk, bpk = psum1()
                    for kk in range(8):
                        k.mm(pk[:, 0:TBM], w_in[:, kk, 512 + ch * 128:512 + (ch + 1) * 128], xT[:, kk, :],
                             kk == 0, kk == 7, [bw[0], bxT[kk]], [bpk])
                    k.act(kT[:, ch, slot0 * 128:slot0 * 128 + TBM], pk[:, 0:TBM], AF.Copy, [bpk], [b_kT[ch]])
                for t in range(2):
                    pv, bpv = psum1()
                    for kk in range(8):
                        k.mm(pv, xT[:, kk, t * 128:(t + 1) * 128], w_in[:, kk, 1024:1536], kk == 0, kk == 7,
                             [bw[0], bxT[kk]], [bpv])
                    k.act(Vaug[:, slot0 + t, :, 0:64], pv.rearrange("p (h e) -> p h e", h=8), AF.Copy,
                          [bpv], [b_V[slot0 + t]])

            def att_gen(b):
                bi_ = b % BPS
                for t in range(2):
                    m = 2 * bi_ + t
                    O = bank(4, 2)
                    bO = (b_ps[4], b_ps[5])
                    js = [j for j in range(5) if m - 4 + j >= 0]

                    def emit_pv(h, pr, bpr):
                        ob = h // 4
                        oc = (h % 4) * 65
                        for j in js:
                            slot = (m - 4 + j) % RING
                            k.mm(O[:, ob * 512 + oc:ob * 512 + oc + 65], pr[:, j, :], Vaug[:, slot, h, :],
                                 j == js[0], j == 4, [bpr, b_V[slot]], [bO[ob]])
                        if h == A_HEADS - 1:
                            for ob2 in range(2):
                                Ov = O[:, ob2 * 512:ob2 * 512 + 260].rearrange("p (h e) -> p h e", h=4)
                                k.ve("dve", "reciprocal", (rs[t][:, ob2 * 4:(ob2 + 1) * 4].unsqueeze(2), Ov[:, :, 64:65]),
                                     [bO[ob2]], [b_rs[t]])
                                k.ve("dve", "tensor_tensor",
                                     (ao[t][:, ob2 * 256:(ob2 + 1) * 256].rearrange("p (h e) -> p h e", h=4),
                                      Ov[:, :, 0:64],
                                      rs[t][:, ob2 * 4:(ob2 + 1) * 4].unsqueeze(2).to_broadcast([128, 4, 64]), ALU.mult),
                                     [bO[ob2], b_rs[t]], [b_ao[t]])

                    pend = None
                    for h in range(A_HEADS):
                        hcn, hp = h // 2, h % 2
                        prow = slice(hp * 64, (hp + 1) * 64)
                        s1, bs1 = psum1()
                        s2, bsb2 = psum1()
                        for j in js:
                            slot = (m - 4 + j) % RING
                            tgt, btg = (s1[:, j * 128:(j + 1) * 128], bs1) if j < 4 else (s2[:, 0:128], bsb2)
                            far = j < 3
                            k.mm(tgt, kT[prow, hcn, slot * 128:(slot + 1) * 128], qT[prow, hcn, t * 128:(t + 1) * 128],
                                 True, far, [b_kT[hcn], b_qT[hcn]], [btg])
                            if not far:
                                k.mm(tgt, ident_bf[:], biasT[:, h, j, :], False, True, [b_ident, b_bias], [btg])
                        pr, bpr = probsT[h % 2], b_pr[h % 2]
                        cbh = cbias[:, h:h + 1]
                        if 0 in js:
                            k.act(pr[64:128, 0, :], s1[64:128, 0:128], AF.Exp, [bs1, b_bias], [bpr], bias=cbias[64:128, h:h + 1])
                            k.act(pr[0:64, 0, 0:64], s1[0:64, 0:64], AF.Exp, [bs1, b_bias], [bpr], bias=cbias[0:64, h:h + 1])
                        jf = [j for j in js if j in (1, 2)]
                        if jf:
                            k.act(pr[:, jf[0]:3, :], s1[:, jf[0] * 128:384].rearrange("p (j q) -> p j q", q=128), AF.Exp,
                                  [bs1, b_bias], [bpr], bias=cbh)
                        if 3 in js:
                            k.act(pr[:, 3, :], s1[:, 384:512], AF.Exp, [bs1], [bpr])
                        k.act(pr[:, 4, :], s2[:, 0:128], AF.Exp, [bsb2], [bpr])
                        if pend is not None:
                            emit_pv(*pend)
                        pend = (h, pr, bpr)
                        yield
                    emit_pv(*pend)
                    yield

            def stage_C(b):
                toks = [b * 2, b * 2 + 1]
                for t in range(2):
                    pc, bpc = psum1()
                    for ch in range(4):
                        k.mm(pc[:, ch * 128:(ch + 1) * 128], cacc[:, ch, t * 128:(t + 1) * 128], ident_f[:],
                             True, True, [b_cacc[ch], b_ident], [bpc])
                    k.act(cn[t], pc, AF.Copy, [bpc], [b_cn[t]])
                    k.ve("dve", "bn_stats", (stc[:, t, :], cn[t]), [b_cn[t]], [b_stc])
                    k.ve("dve", "bn_aggr", (mvc[:, t, :], stc[:, t, :]), [b_stc], [b_stc])
                yield
                for t in range(2):
                    pt, bpt = psum1()
                    for ch in range(4):
                        k.mm(pt[:, ch * 128:(ch + 1) * 128], ao[t][:, ch * 128:(ch + 1) * 128], ident_bf[:],
                             True, True, [b_ao[t], b_ident], [bpt])
                    k.act(catT[t][:, 0:4, :], pt.rearrange("p (c q) -> p c q", c=4), AF.Copy, [bpt], [b_cat[t][0]])
                k.act(sdc, mvc[:, :, 1:2], AF.Sqrt, [b_stc], [b_stc], bias=float(4.0 * LN_EPS))
                k.ve("dve", "reciprocal", (rsc, sdc), [b_stc], [b_stc])
                k.ve("dve", "scalar_tensor_tensor", (nmc, mvc[:, :, 0:1], -1.0, rsc), [b_stc], [b_stc],
                     op0=ALU.mult, op1=ALU.mult)
                yield
                for t in range(2):
                    k.act(cn[t], cn[t], AF.Identity, [b_cn[t], b_stc], [b_cn[t]], bias=nmc[:, t, :], scale=rsc[:, t, :])
                yield
                for t in range(2):
                    k.ve("dve", "tensor_tensor", (cn[t], cn[t], cng, ALU.mult), [b_cn[t], b_cc], [b_cn[t]])
                    k.ve("dve", "tensor_tensor", (cn[t], cn[t], cnb, ALU.add), [b_cn[t], b_cc], [b_cn[t]])
                    yield
                for t in range(2):
                    k.act(th, cn[t], AF.Tanh, [b_cn[t]], [b_th], scale=0.5)
                    k.ve("dve", "scalar_tensor_tensor", (cbf[t], th, 1.0, cn[t]), [b_th, b_cn[t]], [b_cbf[t]],
                         op0=ALU.add, op1=ALU.mult)
                    yield
                Ys = []
                for t in range(2):
                    pt, bpt = psum1()
                    for ch in range(4):
                        k.mm(pt[:, ch * 128:(ch + 1) * 128], cbf[t][:, ch * 128:(ch + 1) * 128], ident_bf[:],
                             True, True, [b_cbf[t], b_ident], [bpt])
                    k.act(catT[t][:, 4:8, :], pt.rearrange("p (c q) -> p c q", c=4), AF.Copy, [bpt], [b_cat[t][1]],
                          scale=0.5)
                    yield
                es_ = []
                for t in range(2):
                    ys = [psum1(), psum1()]
                    for n in range(2):
                        for kc in range(8):
                            k.mm(ys[n][0], catT[t][:, kc, :], w_out[:, kc, n * 512:(n + 1) * 512],
                                 kc == 0, kc == 7, [b_cat[t][kc // 4], bw[1]], [ys[n][1]])
                    i = c.ep_slot.pop(toks[t])
                    e, be = c.ep[i], c.b_ep[i]
                    es_.append((e, be))
                    for n in range(2):
                        sl = slice(n * 512, (n + 1) * 512)
                        k.ve("dve", "scalar_tensor_tensor", (e[:, sl], e[:, sl], float(ALPHA), ys[n][0]),
                             [be, ys[n][1]], [be], op0=ALU.mult, op1=ALU.add)
                    yield
                    for n in range(2):
                        sl = slice(n * 512, (n + 1) * 512)
                        k.ve("dve", "bn_stats", (ste[:, t, n * 6:(n + 1) * 6], e[:, sl]), [be], [b_ste])
                    k.ve("dve", "bn_aggr", (mve[:, t, :], ste[:, t, :]), [b_ste], [b_ste])
                    yield
                k.act(sde, mve[:, :, 1:2], AF.Sqrt, [b_ste], [b_ste], bias=float(LN_EPS))
                k.ve("dve", "reciprocal", (rse, sde), [b_ste], [b_ste])
                k.ve("dve", "scalar_tensor_tensor", (nme, mve[:, :, 0:1], -1.0, rse), [b_ste], [b_ste],
                     op0=ALU.mult, op1=ALU.mult)
                yield
                for t in range(2):
                    e, be = es_[t]
                    k.act(e, e, AF.Identity, [be, b_ste], [be], bias=nme[:, t, :], scale=rse[:, t, :])
                yield
                for t in range(2):
                    e, be = es_[t]
                    k.ve("dve", "tensor_tensor", (e, e, c.lng, ALU.mult), [be, c.b_ln], [be])
                    k.ve("dve", "tensor_tensor", (e, e, c.lnb, ALU.add), [be, c.b_ln], [be])
                    rows = slice(toks[t] * 128, (toks[t] + 1) * 128)
                    k.dma("sp", dst[rows, :], e, [be], [dst_tb[toks[t]]])
                    yield
                for t in range(2):
                    ep_prefetch(c, toks[t] + 2, src, src_tb, None)

            def take(gen, n):
                if gen is None:
                    return None
                for _ in range(n):
                    if next(gen, "end") == "end":
                        return None
                return gen

            ep_prefetch(c, 0, src, src_tb, None)
            ep_prefetch(c, 1, src, src_tb, None)
            for _ in stage_A(0):
                pass
            stage_B1(0)
            cg, ag = conv_gen(0), att_gen(0)
            while cg is not None or ag is not None:
                ag = take(ag, 1)
                cg = take(cg, 8)
            if NB > 1:
                for _ in stage_A(1):
                    pass
                stage_B1(1)
            for b in range(NB):
                cg = ag = None
                if b + 1 < NB:
                    cg, ag = conv_gen(b + 1), att_gen(b + 1)
                for _ in stage_C(b):
                    cg = take(cg, 3)
                    ag = take(ag, 1)
                while ag is not None:
                    ag = take(ag, 1)
                    cg = take(cg, 4)
                while cg is not None:
                    cg = take(cg, 8)
                if b + 2 < NB:
                    for _ in stage_A(b + 2):
                        pass
                    stage_B1(b + 2)
            drip(1000)

        def mixer1_phase(bi, wts, src, src_tb, dst, dst_tb):
            w_in, w_out = wts
            bw = wbuf_b[bi]
            ar.reset()
            c = alloc_common(TBM, NEP)
            NB = T // TBM
            BPS = S // TBM
            NCH = TBM // 64
            zT = ar.alloc((TBM,), F32, parts=16)
            b_zT = Buf()
            gw = ar.alloc((512,), F32, parts=16)
            gb = ar.alloc((4,), F32)
            ngb = ar.alloc((4,), F32)
            hgT = ar.alloc((8,), F32)
            tri = ar.alloc((64,), F32)
            rmask = ar.alloc((TBM,), F32)
            b_const = Buf()
            tA = [ar.alloc((TBM,), F32) for _ in range(2)]
            nb_ = [ar.alloc((TBM,), F32) for _ in range(2)]
            eb = [ar.alloc((TBM,), F32) for _ in range(2)]
            enb = [ar.alloc((TBM,), F32) for _ in range(2)]
            b_tA = [Buf(), Buf()]
            b_nb = [Buf(), Buf()]
            b_eb = [Buf(), Buf()]
            b_enb = [Buf(), Buf()]
            kf = [ar.alloc((TBM,), F32) for _ in range(2)]
            b_kf = [Buf(), Buf()]
            qtT = ar.alloc((4, TBM), BF16)
            b_qt = [Buf() for _ in range(4)]
            ktT = ar.alloc((4, TBM), BF16)
            b_kt = [Buf() for _ in range(4)]
            kdT = ar.alloc((4, TBM), BF16)
            b_kdT = [Buf() for _ in range(4)]
            ebl = ar.alloc((4, NCH), F32)
            b_ebl = [Buf() for _ in range(4)]
            kd = ar.alloc((2, 4, 128), BF16)
            b_kd = [Buf(), Buf()]
            v = ar.alloc((2, D), BF16)
            b_v = [Buf(), Buf()]
            sgg = [ar.alloc((D,), F32) for _ in range(2)]
            b_sgg = [Buf(), Buf()]
            attnT = [ar.alloc((4, 64), BF16) for _ in range(2)]
            b_at = [Buf(), Buf()]
            Sst = ar.alloc((4, 256), F32)
            Sbf = ar.alloc((4, 256), BF16)
            b_S = [Buf() for _ in range(4)]
            b_Sbf = [Buf() for _ in range(4)]
            obf = [ar.alloc((D,), BF16) for _ in range(2)]
            b_obf = [Buf(), Buf()]
            ss = ar.alloc((2, 4), F32)
            lss = ar.alloc((2, 4), F32)
            rinv = ar.alloc((2, 4), F32)
            b_ss = Buf()
            catT = [ar.alloc((8, 128), BF16) for _ in range(2)]
            b_cat = [[Buf(), Buf()], [Buf(), Buf()]]
            ste = ar.alloc((2, 12), F32)
            mve = ar.alloc((2, 2), F32)
            lve = ar.alloc((2, 1), F32)
            rse = ar.alloc((2, 1), F32)
            nme = ar.alloc((2, 1), F32)
            b_ste = Buf()
            load_ln(c, 4)
            k.dma("sp", gw, gatew_in, [], [b_const])
            k.dma("sp", gb, gateb_in, [], [b_const])
            k.dma("sp", hgT, headgT_in, [], [b_const])
            k.dma("sp", tri, tri_in, [], [b_const])
            k.ve("dve", "tensor_scalar", (ngb, gb, -1.0, None), [b_const], [b_const], op0=ALU.mult)
            k.ve("dve", "memset", (rmask, 1.0), [], [b_const])
            k.ve("dve", "memset", (rmask.rearrange("p (c j) -> p c j", j=64)[:, :, 0:1], 0.0), [], [b_const])
            load_xbf(c, src, src_tb, 0)
            DKS = 128 ** -0.5
            O_t = [bank(4, 2), bank(6, 2)]
            bO_t = [(b_ps[4], b_ps[5]), (b_ps[6], b_ps[7])]

            def stage_FE(b):
                bi_ = b % BPS
                transpose_block(c, all_act=True)
                if b + 1 < NB:
                    load_xbf(c, src, src_tb, b + 1)
                drip(3)
                xT, bxT = c.xT, c.b_xT
                yield
                pz, bpz = psum1()
                for kk in range(8):
                    k.mm(pz[0:16, 0:TBM], w_in[:, kk, 3072:3088], xT[:, kk, :], kk == 0, kk == 7,
                         [bw[0], bxT[kk]], [bpz])
                k.act(zT, pz[0:16, 0:TBM], AF.Copy, [bpz], [b_zT])
                yield
                for hh in range(4):
                    r = hh % 2
                    pg, bpg = psum1()
                    k.mm(pg[:, 0:TBM], gw[:, hh * 128:(hh + 1) * 128], zT, True, True, [b_const, b_zT], [bpg])
                    k.act(tA[r], pg[:, 0:TBM], AF.Exp, [bpg, b_const], [b_tA[r]], scale=-1.0, bias=ngb[:, hh:hh + 1])
                    pq, bpq = psum1()
                    for kk in range(8):
                        k.mm(pq[:, 0:TBM], w_in[:, kk, hh * 128:(hh + 1) * 128], xT[:, kk, :], kk == 0, kk == 7,
                             [bw[0], bxT[kk]], [bpq])
                    pk, bpk = psum1()
                    for kk in range(8):
                        k.mm(pk[:, 0:TBM], w_in[:, kk, 512 + hh * 128:512 + (hh + 1) * 128], xT[:, kk, :],
                             kk == 0, kk == 7, [bw[0], bxT[kk]], [bpk])
                    k.act(tA[r], tA[r], AF.Ln, [b_tA[r]], [b_tA[r]], bias=1.0)
                    k.ve("dve", "tensor_tensor_scan", (nb_[r], rmask, tA[r], 0.0, ALU.mult, ALU.add),
                         [b_const, b_tA[r]], [b_nb[r]])
                    k.act(eb[r], nb_[r], AF.Exp, [b_nb[r]], [b_eb[r]], scale=-1.0 / 16.0)
                    k.act(enb[r], nb_[r], AF.Exp, [b_nb[r]], [b_enb[r]], scale=1.0 / 16.0)
                    k.ve("dve", "tensor_copy",
                         (ebl[:, hh, :].unsqueeze(2), eb[r].rearrange("p (c j) -> p c j", j=64)[:, :, 63:64]),
                         [b_eb[r]], [b_ebl[hh]])
                    k.ve("dve", "scalar_tensor_tensor", (qtT[:, hh, :], pq[:, 0:TBM], float(DKS), eb[r]),
                         [bpq, b_eb[r]], [b_qt[hh]], op0=ALU.mult, op1=ALU.mult)
                    k.ve("dve", "tensor_tensor", (kf[r], pk[:, 0:TBM], enb[r], ALU.mult), [bpk, b_enb[r]], [b_kf[r]])
                    k.act(ktT[:, hh, :], kf[r], AF.Copy, [b_kf[r]], [b_kt[hh]])
                    k.ve("dve", "tensor_tensor",
                         (kdT[:, hh, :].rearrange("p (c j) -> p c j", j=64),
                          kf[r].rearrange("p (c j) -> p c j", j=64),
                          ebl[:, hh, :].unsqueeze(2).to_broadcast([128, NCH, 64]), ALU.mult),
                         [b_kf[r], b_ebl[hh]], [b_kdT[hh]])
                    yield
                for t in range(2):
                    pkd, bpkd = psum1()
                    for hh in range(4):
                        k.mm(pkd[:, hh * 128:(hh + 1) * 128], kdT[:, hh, t * 128:(t + 1) * 128], ident_bf[:],
                             True, True, [b_kdT[hh], b_ident], [bpkd])
                    k.act(kd[:, t, :, :], pkd.rearrange("p (h f) -> p h f", h=4), AF.Copy, [bpkd], [b_kd[t]])
                    for n in range(2):
                        pv, bpv = psum1()
                        for kk in range(8):
                            k.mm(pv, xT[:, kk, t * 128:(t + 1) * 128], w_in[:, kk, 1024 + n * 512:1024 + (n + 1) * 512],
                                 kk == 0, kk == 7, [bw[0], bxT[kk]], [bpv])
                        if n == 0:
                            k.act(v[:, t, 0:512], pv, AF.Copy, [bpv], [b_v[t]])
                        else:
                            k.ve("dve", "tensor_copy", (v[:, t, 512:1024], pv), [bpv], [b_v[t]])
                    yield

            def stage_REC(b):
                bi_ = b % BPS
                if bi_ == 0:
                    k.ve("dve", "memset", (Sst, 0.0), [], b_S)
                    k.ve("dve", "memset", (Sbf, 0.0), [], b_Sbf)

                def scores(ci):
                    cc = ci % 2
                    rows = slice(cc * 64, (cc + 1) * 64)
                    cols = slice(ci * 64, (ci + 1) * 64)
                    pA, bpA = psum1()
                    for hh in range(4):
                        k.mm(pA[rows, hh * 64:(hh + 1) * 64], ktT[:, hh, cols], qtT[:, hh, cols], True, True,
                             [b_kt[hh], b_qt[hh]], [bpA])
                    at, bat = attnT[cc], b_at[cc]
                    k.ve("dve", "tensor_tensor",
                         (at[rows, :, :], pA[rows, 0:256].rearrange("p (h i) -> p h i", h=4),
                          tri[rows, :].unsqueeze(1).to_broadcast([64, 4, 64]), ALU.mult),
                         [bpA, b_const], [bat])

                scores(0)
                for ci in range(NCH):
                    t, cc = ci // 2, ci % 2
                    O, bO = O_t[t], bO_t[t]
                    rows = slice(cc * 64, (cc + 1) * 64)
                    cols = slice(ci * 64, (ci + 1) * 64)
                    dSs = [psum1(), psum1()]
                    for hh in range(4):
                        oc = slice(hh * 256, (hh + 1) * 256)
                        dsb, bdsb = dSs[hh // 2]
                        k.mm(dsb[:, (hh % 2) * 256:(hh % 2 + 1) * 256], kd[rows, t, hh, :], v[rows, t, oc], True, True,
                             [b_kd[t], b_v[t]], [bdsb])
                    if ci + 1 < NCH:
                        scores(ci + 1)
                    at, bat = attnT[cc], b_at[cc]
                    for hh in range(4):
                        oc = slice(hh * 256, (hh + 1) * 256)
                        k.mm(O[rows, oc], at[rows, hh, :], v[rows, t, oc], True, False,
                             [bat, b_v[t]], [bO[hh // 2]])
                        k.mm(O[rows, oc], qtT[:, hh, cols], Sbf[:, hh, :], False, True,
                             [b_qt[hh], b_Sbf[hh]], [bO[hh // 2]])
                    for hh in range(4):
                        dsb, bdsb = dSs[hh // 2]
                        k.ve("dve", "scalar_tensor_tensor",
                             (Sst[:, hh, :], Sst[:, hh, :], ebl[:, hh, ci:ci + 1],
                              dsb[:, (hh % 2) * 256:(hh % 2 + 1) * 256]),
                             [b_S[hh], b_ebl[hh], bdsb], [b_S[hh]], op0=ALU.mult, op1=ALU.add)
                        k.act(Sbf[:, hh, :], Sst[:, hh, :], AF.Copy, [b_S[hh]], [b_Sbf[hh]])
                    yield

            def stage_G(b):
                xT, bxT = c.xT, c.b_xT
                for t in range(2):
                    for n in range(2):
                        pgg, bpgg = psum1()
                        for kk in range(8):
                            k.mm(pgg, xT[:, kk, t * 128:(t + 1) * 128], w_in[:, kk, 2048 + n * 512:2048 + (n + 1) * 512],
                                 kk == 0, kk == 7, [bw[0], bxT[kk]], [bpgg])
                        k.act(sgg[t][:, n * 512:(n + 1) * 512], pgg, AF.Silu, [bpgg], [b_sgg[t]])
                        yield

            def stage_TAIL(b):
                toks = [b * 2, b * 2 + 1]
                for t in range(2):
                    O, bO = O_t[t], bO_t[t]
                    for hh in range(4):
                        k.act(obf[t][:, hh * 256:(hh + 1) * 256], O[:, hh * 256:(hh + 1) * 256], AF.Square,
                              [bO[hh // 2]], [b_obf[t], b_ss], accum_out=ss[:, t, hh:hh + 1])
                yield
                k.act(lss, ss, AF.Ln, [b_ss], [b_ss], scale=1.0 / 256.0, bias=float(RMS_EPS))
                k.act(rinv, lss, AF.Exp, [b_ss], [b_ss], scale=-0.5)
                yield
                for t in range(2):
                    O, bO = O_t[t], bO_t[t]
                    for hh in range(4):
                        oc = slice(hh * 256, (hh + 1) * 256)
                        k.ve("dve", "scalar_tensor_tensor", (obf[t][:, oc], O[:, oc], rinv[:, t, hh:hh + 1], sgg[t][:, oc]),
                             [bO[hh // 2], b_ss, b_sgg[t]], [b_obf[t]], op0=ALU.mult, op1=ALU.mult)
                    yield
                for t in range(2):
                    for half in range(2):
                        pt, bpt = psum1()
                        for ch in range(4):
                            kc = half * 4 + ch
                            k.mm(pt[:, ch * 128:(ch + 1) * 128], obf[t][:, kc * 128:(kc + 1) * 128], ident_bf[:],
                                 True, True, [b_obf[t], b_ident], [bpt])
                        k.ve("dve", "tensor_tensor",
                             (catT[t][:, half * 4:(half + 1) * 4, :], pt.rearrange("p (c q) -> p c q", c=4),
                              hgT[:, half * 4:(half + 1) * 4].unsqueeze(2).to_broadcast([128, 4, 128]), ALU.mult),
                             [bpt, b_const], [b_cat[t][half]])
                    yield
                es_ = []
                for t in range(2):
                    ys = [psum1(), psum1()]
                    for n in range(2):
                        for kc in range(8):
                            k.mm(ys[n][0], catT[t][:, kc, :], w_out[:, kc, n * 512:(n + 1) * 512],
                                 kc == 0, kc == 7, [b_cat[t][kc // 4], bw[1]], [ys[n][1]])
                    i = c.ep_slot.pop(toks[t])
                    e, be = c.ep[i], c.b_ep[i]
                    es_.append((e, be))
                    for n in range(2):
                        sl = slice(n * 512, (n + 1) * 512)
                        k.ve("dve", "scalar_tensor_tensor", (e[:, sl], e[:, sl], float(ALPHA), ys[n][0]),
                             [be, ys[n][1]], [be], op0=ALU.mult, op1=ALU.add)
                    yield
                    for n in range(2):
                        sl = slice(n * 512, (n + 1) * 512)
                        k.ve("dve", "bn_stats", (ste[:, t, n * 6:(n + 1) * 6], e[:, sl]), [be], [b_ste])
                    k.ve("dve", "bn_aggr", (mve[:, t, :], ste[:, t, :]), [b_ste], [b_ste])
                    yield
                k.act(lve, mve[:, :, 1:2], AF.Ln, [b_ste], [b_ste], bias=float(LN_EPS))
                k.act(rse, lve, AF.Exp, [b_ste], [b_ste], scale=-0.5)
                k.ve("dve", "scalar_tensor_tensor", (nme, mve[:, :, 0:1], -1.0, rse), [b_ste], [b_ste],
                     op0=ALU.mult, op1=ALU.mult)
                yield
                for t in range(2):
                    e, be = es_[t]
                    k.act(e, e, AF.Identity, [be, b_ste], [be], bias=nme[:, t, :], scale=rse[:, t, :])
                yield
                for t in range(2):
                    e, be = es_[t]
                    k.ve("dve", "tensor_tensor", (e, e, c.lng, ALU.mult), [be, c.b_ln], [be])
                    k.ve("dve", "tensor_tensor", (e, e, c.lnb, ALU.add), [be, c.b_ln], [be])
                    rows_ = slice(toks[t] * 128, (toks[t] + 1) * 128)
                    k.dma("sp", dst[rows_, :], e, [be], [dst_tb[toks[t]]])
                    yield
                for t in range(2):
                    ep_prefetch(c, toks[t] + 2, src, src_tb, None)

            def take(gen, n):
                if gen is None:
                    return None
                for _ in range(n):
                    if next(gen, "end") == "end":
                        return None
                return gen

            ep_prefetch(c, 0, src, src_tb, None)
            ep_prefetch(c, 1, src, src_tb, None)
            for _ in stage_FE(0):
                pass
            rec, gg = stage_REC(0), stage_G(0)
            while rec is not None or gg is not None:
                rec = take(rec, 1)
                gg = take(gg, 1)
            for b in range(NB):
                tail = stage_TAIL(b)
                fe = stage_FE(b + 1) if b + 1 < NB else None
                while fe is not None:
                    tail = take(tail, 1)
                    fe = take(fe, 1)
                rec = gg = None
                if b + 1 < NB:
                    rec, gg = stage_REC(b + 1), stage_G(b + 1)
                while tail is not None or rec is not None or gg is not None:
                    tail = take(tail, 1)
                    rec = take(rec, 1)
                    gg = take(gg, 1)
            drip(1000)

        if phases == ("m0", "f0", "m1", "f1"):
            w0, l0 = mixer_weight_loads(0, "even_w_in", IN_EVEN, "even_w_out", first_cols=(1536, 2560))
            w1, l1 = ffn_weight_loads(1, 0, 0)
            pending_loads.extend(l0)
            pending_loads.extend(l1)
            mixer0_phase(0, w0, x_in, xin_tb, xs[0], xs_tb[0], first=True)
            p.barrier()
            w2, l2 = ffn_weight_loads(0, 0, 1)
            pending_loads.extend(l2)
            ffn_phase(0, 0, 1, w1, xs[0], xs_tb[0], None, None, 2)
            p.barrier()
            w3, l3 = mixer_weight_loads(1, "odd_w_in", IN_ODD, "odd_w_out")
            pending_loads.extend(l3)
            ffn_phase(0, 1, 0, w2, xs[0], xs_tb[0], xs[1], xs_tb[1], 2)
            p.barrier()
            w4, l4 = ffn_weight_loads(0, 1, 0)
            pending_loads.extend(l4)
            mixer1_phase(1, w3, xs[1], xs_tb[1], xs[0], xs_tb[0])
            p.barrier()
            w5, l5 = ffn_weight_loads(1, 1, 1)
            pending_loads.extend(l5)
            ffn_phase(1, 0, 0, w4, xs[0], xs_tb[0], None, None, 6)
            p.barrier()
            ffn_phase(1, 1, 1, w5, xs[0], xs_tb[0], out_d, out_tb, 6)
        elif phases == ("f0",):
            wa, la = ffn_weight_loads(0, 0, 0)
            wb, lb = ffn_weight_loads(1, 0, 1)
            pending_loads.extend(la + lb)
            drip(1000)
            ffn_phase(0, 0, 0, wa, x_in, xin_tb, None, None, 2)
            p.barrier()
            ffn_phase(0, 1, 1, wb, x_in, xin_tb, out_d, out_tb, 2)
        elif phases == ("m1",):
            wm, lm = mixer_weight_loads(0, "odd_w_in", IN_ODD, "odd_w_out")
            pending_loads.extend(lm)
            drip(1000)
            mixer1_phase(0, wm, x_in, xin_tb, out_d, out_tb)
        elif phases == ("m0",):
            wm, lm = mixer_weight_loads(0, "even_w_in", IN_EVEN, "even_w_out")
            pending_loads.extend(lm)
            drip(1000)
            mixer0_phase(0, wm, x_in, xin_tb, out_d, out_tb)
        p.finalize_and_emit()
    return nc


def make_lnv(inp):
    return np.ascontiguousarray(np.stack([
        inp["mix_norm_g"][0], inp["mix_norm_b"][0], inp["ffn_norm_g"][0], inp["ffn_norm_b"][0],
        inp["mix_norm_g"][1], inp["mix_norm_b"][1], inp["ffn_norm_g"][1], inp["ffn_norm_b"][1]], 0).astype(np.float32))


def make_biasT(rel_bias):
    rb = np.asarray(rel_bias, dtype=np.float32)
    ki = np.arange(128)[:, None]
    qi = np.arange(128)[None, :]
    out = np.empty((128, A_HEADS, 5, 128), np.float32)
    for j in range(5):
        idx = np.clip(128 * (4 - j) + qi - ki, -128, 128) + 128
        t = rb[:, idx]
        out[:, :, j, :] = np.transpose(t, (1, 0, 2))
    out[0:64, :, 0, 64:128] = NEGM
    out[64:128, :, 4, 0:64] = NEGM
    return np.ascontiguousarray(out.reshape(128, A_HEADS * 5 * 128))


def make_in_maps(inp, nseq=NSEQ, ncores=NCORES):
    x = np.asarray(inp["x"], dtype=np.float32)
    cw = np.asarray(inp["even_conv_w"][0], np.float32)
    common = {
        "even_w_in": np.ascontiguousarray(inp["even_w_in"][0]),
        "even_w_out": np.ascontiguousarray(inp["even_w_out"][0]),
        "odd_w_in": np.ascontiguousarray(inp["odd_w_in"][0]),
        "odd_w_out": np.ascontiguousarray(inp["odd_w_out"][0]),
        "lnv": make_lnv(inp),
        "ident": np.eye(128, dtype=np.float32),
        "biasT": make_biasT(inp["even_rel_bias"][0]),
        "convw": np.ascontiguousarray(cw.T.reshape(4, 128, CONV_W).transpose(1, 0, 2).reshape(128, 4 * CONV_W)),
        "convb": np.ascontiguousarray(np.asarray(inp["even_conv_b"][0], np.float32).reshape(4, 128).T),
        "cbias": np.ascontiguousarray(np.asarray(inp["even_rel_bias"][0], np.float32)[:, 2 * 128].reshape(1, A_HEADS)),
        "convn": np.ascontiguousarray(np.stack([inp["even_conv_norm_g"][0], inp["even_conv_norm_b"][0]], 0)
                                      .astype(np.float32)),
        "gatew": np.ascontiguousarray(np.asarray(inp["odd_gate_w"][0], np.float32)),
        "gateb": np.ascontiguousarray(np.asarray(inp["odd_gate_b"][0], np.float32).reshape(4, 128).T),
        "headg": np.ascontiguousarray(np.asarray(inp["odd_head_norm_g"][0], np.float32).reshape(1, D)),
        "headgT": np.ascontiguousarray(np.asarray(inp["odd_head_norm_g"][0], np.float32).reshape(8, 128).T),
        "tri": np.ascontiguousarray((np.arange(128)[:, None] % 64 <= np.arange(64)[None, :]).astype(np.float32)),
    }
    for l in range(2):
        common["wg%d" % l] = np.ascontiguousarray(inp["ffn_w_gate"][l])
        common["wu%d" % l] = np.ascontiguousarray(inp["ffn_w_up"][l])
        common["wd%d" % l] = np.ascontiguousarray(inp["ffn_w_down"][l])
    maps = []
    for c_ in range(ncores):
        m = dict(common)
        m["x"] = np.ascontiguousarray(x[c_ * nseq:(c_ + 1) * nseq].reshape(nseq * S, D))
        maps.append(m)
    return maps


def kernel(**inputs):
    inp = {k_: np.asarray(v) for k_, v in inputs.items()}
    nc = build()
    in_maps = make_in_maps(inp)
    res = run_bass_kernel_spmd(nc, in_maps, core_ids=list(range(NCORES)))
    outs = [np.asarray(r["out"]).reshape(NSEQ, S, D) for r in res.results]
    return np.concatenate(outs, axis=0).astype(np.float32)
```
